# Optimizing a Trainium2 kernel written in Bass

```python
import jax, jax.numpy as jnp
from jax import lax
import numpy as np

D_MODEL = 1024
BATCH = 8
SEQ = 4096
DEPTH = 2

N_A_LAYERS = (DEPTH + 1) // 2
N_C_LAYERS = DEPTH // 2
D_FF = 4 * D_MODEL
EPS = 1e-6
NEG = -1e30

CONV_WIDTH = D_MODEL // 2
CONV_GROUPS = 8
CONV_K = 3
MLA_HEADS = 8
MLA_NOPE = 64
MLA_ROPE = 32
MLA_V = 64
MLA_KV_RANK = 256
MLA_Q_RANK = 3 * MLA_KV_RANK
ROPE_THETA = 10000.0
MLA_Q_BLOCK = 128
A_IN = 3 * CONV_WIDTH + MLA_Q_RANK + MLA_KV_RANK + MLA_ROPE
A_MIX = CONV_WIDTH + MLA_HEADS * MLA_V
NSA_HEADS = 16
NSA_GROUPS = 4
NSA_REP = NSA_HEADS // NSA_GROUPS
NSA_DH = 64
CMP_LEN = 32
CMP_STRIDE = 16
SLC_LEN = 64
N_SEL = 16
WINDOW = 512
NSA_Q_BLOCK = 32
FORCE_BONUS = 1e4
KV_W = NSA_GROUPS * NSA_DH
C_MIX = NSA_HEADS * NSA_DH
C_IN = C_MIX + 6 * KV_W + 3 * NSA_HEADS

kernel_name = "hybrid_shortconv_mla_nsa_trunk"


def rms_norm(x, g):
    xf = x.astype(jnp.float32)
    y = xf * lax.rsqrt(jnp.mean(xf * xf, axis=-1, keepdims=True) + EPS)
    return (y * g.astype(jnp.float32)).astype(x.dtype)


def alibi_slopes(n):
    return jnp.asarray(2.0 ** (-8.0 * np.arange(1, n + 1) / n), jnp.float32)


def rope(x, pos):
    half = x.shape[-1] // 2
    inv = ROPE_THETA ** (-jnp.arange(half, dtype=jnp.float32) / half)
    ang = pos.astype(jnp.float32)[:, None] * inv[None, :]
    cos, sin = jnp.cos(ang)[:, None, :], jnp.sin(ang)[:, None, :]
    x1 = x[..., :half].astype(jnp.float32)
    x2 = x[..., half:].astype(jnp.float32)
    return jnp.concatenate([x1 * cos - x2 * sin, x2 * cos + x1 * sin], axis=-1).astype(x.dtype)


def short_conv(u, w):
    c = u.shape[-1]
    return lax.conv_general_dilated(u, w[:, None, :].astype(u.dtype), window_strides=(1,),
                                    padding=[(CONV_K - 1, 0)],
                                    dimension_numbers=('NWC', 'WIO', 'NWC'),
                                    feature_group_count=c)


def dense_causal_attention(q, k, v):
    b, s, h, d = q.shape
    nb = s // MLA_Q_BLOCK
    scale = d ** -0.5
    qb = q.reshape(b, nb, MLA_Q_BLOCK, h, d).transpose(1, 0, 2, 3, 4)
    kpos = jnp.arange(s)

    def one(args):
        q_blk, i = args
        t = i * MLA_Q_BLOCK + jnp.arange(MLA_Q_BLOCK)
        sc = jnp.einsum('bqhd,bshd->bhqs', q_blk, k).astype(jnp.float32) * scale
        sc = jnp.where(kpos[None, :] <= t[:, None], sc, NEG)
        p = jax.nn.softmax(sc, axis=-1).astype(v.dtype)
        return jnp.einsum('bhqs,bshd->bqhd', p, v)

    out = lax.map(one, (qb, jnp.arange(nb)))
    return out.transpose(1, 0, 2, 3, 4).reshape(b, s, h, v.shape[-1])


def mixer_a(xn, w_in, conv_w, q_norm, w_q_up, kv_norm, w_kv_up, w_out):
    b, s, _ = xn.shape
    proj = xn @ w_in
    o1 = CONV_WIDTH; o2 = 2 * CONV_WIDTH; o3 = 3 * CONV_WIDTH
    o4 = o3 + MLA_Q_RANK; o5 = o4 + MLA_KV_RANK
    g_b, g_c, hv, c_q, c_kv, k_r = jnp.split(proj, [o1, o2, o3, o4, o5], axis=-1)
    y_conv = g_b * short_conv(g_c * hv, conv_w)
    pos = jnp.arange(s)
    q = (rms_norm(c_q, q_norm) @ w_q_up).reshape(b, s, MLA_HEADS, MLA_NOPE + MLA_ROPE)
    q = jnp.concatenate([q[..., :MLA_NOPE], rope(q[..., MLA_NOPE:], pos)], axis=-1)
    kv = (rms_norm(c_kv, kv_norm) @ w_kv_up).reshape(b, s, MLA_HEADS, MLA_NOPE + MLA_V)
    k_rope = jnp.broadcast_to(rope(k_r[:, :, None, :], pos), (b, s, MLA_HEADS, MLA_ROPE))
    k = jnp.concatenate([kv[..., :MLA_NOPE], k_rope], axis=-1)
    v = kv[..., MLA_NOPE:]
    y_mla = dense_causal_attention(q, k, v).reshape(b, s, MLA_HEADS * MLA_V)
    return jnp.concatenate([y_conv, y_mla], axis=-1) @ w_out


def mixer_c(xn, w_in, pe_k, w1_k, w2_k, pe_v, w1_v, w2_v, w_out):
    b, s, _ = xn.shape
    G, R, Dh, QB = NSA_GROUPS, NSA_REP, NSA_DH, NSA_Q_BLOCK
    proj = xn @ w_in
    q = proj[..., :C_MIX].reshape(b, s, G, R, Dh)
    kvs = jnp.moveaxis(proj[..., C_MIX:C_MIX + 6 * KV_W].reshape(b, s, 6, G, Dh), 2, 0)
    k_c, v_c, k_s, v_s, k_w, v_w = kvs
    gates = jax.nn.sigmoid(proj[..., C_MIX + 6 * KV_W:].astype(jnp.float32))
    gates = gates.reshape(b, s, 3, G, R).astype(xn.dtype)

    n_cmp = (s - CMP_LEN) // CMP_STRIDE + 1
    starts = np.arange(n_cmp) * CMP_STRIDE
    idx = starts[:, None] + np.arange(CMP_LEN)[None, :]

    def compress(z, pe, w1, w2):
        blk = z[:, idx] + pe[:, None, :]
        hdn = jax.nn.gelu(jnp.einsum('bnlgd,lde->bnge', blk, w1))
        return hdn @ w2

    kc = compress(k_c, pe_k, w1_k, w2_k)
    vc = compress(v_c, pe_v, w1_v, w2_v)
    cmp_centre = jnp.asarray(starts + (CMP_LEN - 1) / 2.0, jnp.float32)
    cmp_end = jnp.asarray(starts + CMP_LEN - 1, jnp.int32)

    n_slc = s // SLC_LEN
    n_sel = min(N_SEL, n_slc)
    ss = np.arange(n_slc) * SLC_LEN
    ov = np.clip(np.minimum(starts[:, None] + CMP_LEN, ss[None, :] + SLC_LEN)
                 - np.maximum(starts[:, None], ss[None, :]), 0, None) / CMP_LEN
    ov = jnp.asarray(ov, jnp.float32)
    kb = k_s.reshape(b, n_slc, SLC_LEN, G, Dh).transpose(0, 3, 1, 2, 4)
    vb = v_s.reshape(b, n_slc, SLC_LEN, G, Dh).transpose(0, 3, 1, 2, 4)

    kwp = jnp.pad(k_w, ((0, 0), (WINDOW, 0), (0, 0), (0, 0)))
    vwp = jnp.pad(v_w, ((0, 0), (WINDOW, 0), (0, 0), (0, 0)))

    slopes = alibi_slopes(NSA_HEADS).reshape(G, R)
    scale = Dh ** -0.5
    nb = s // QB
    qb = q.reshape(b, nb, QB, G, R, Dh).transpose(1, 0, 2, 3, 4, 5)
    gb = gates.reshape(b, nb, QB, 3, G, R).transpose(1, 0, 2, 3, 4, 5)
    bi = jnp.arange(b)[:, None, None, None]
    gi = jnp.arange(G)[None, :, None, None]
    jblk = jnp.arange(n_slc)

    def one(args):
        q_blk, g_blk, i = args
        q0 = i * QB
        t = q0 + jnp.arange(QB)
        tf = t.astype(jnp.float32)
        sc = jnp.einsum('bqgrd,bngd->bgrqn', q_blk, kc).astype(jnp.float32) * scale
        sc = sc - slopes[:, :, None, None] * (tf[:, None] - cmp_centre[None, :])
        valid = cmp_end[None, :] <= t[:, None]
        sc = jnp.where(valid, sc, NEG)
        has = jnp.any(valid, axis=-1).astype(jnp.float32)
        p_cmp = jax.nn.softmax(sc, axis=-1) * has[:, None]
        o_cmp = jnp.einsum('bgrqn,bngd->bqgrd', p_cmp.astype(vc.dtype), vc)
        imp = jnp.einsum('bgrqn,nj->bgqj', p_cmp, ov)
        cur = t // SLC_LEN
        forced = ((jblk[None, :] == 0) | (jblk[None, :] == cur[:, None])
                  | (jblk[None, :] == cur[:, None] - 1)).astype(jnp.float32)
        imp = jnp.where(jblk[None, :] > cur[:, None], NEG, imp + FORCE_BONUS * forced)
        _, sel = lax.top_k(imp, n_sel)
        ks = kb[bi, gi, sel]
        vs = vb[bi, gi, sel]
        kpos = sel[..., None] * SLC_LEN + jnp.arange(SLC_LEN)
        dist = (t[None, None, :, None, None] - kpos).astype(jnp.float32)
        sc = jnp.einsum('bqgrd,bgqnld->bgrqnl', q_blk, ks).astype(jnp.float32) * scale
        sc = sc - slopes[None, :, :, None, None, None] * dist[:, :, None]
        sc = jnp.where((dist >= 0)[:, :, None], sc, NEG)
        p = jax.nn.softmax(sc.reshape(b, G, R, QB, -1), axis=-1).reshape(sc.shape)
        o_slc = jnp.einsum('bgrqnl,bgqnld->bqgrd', p.astype(vs.dtype), vs)
        kw = lax.dynamic_slice_in_dim(kwp, q0, WINDOW + QB, axis=1)
        vw = lax.dynamic_slice_in_dim(vwp, q0, WINDOW + QB, axis=1)
        wpos = q0 - WINDOW + jnp.arange(WINDOW + QB)
        d = t[:, None] - wpos[None, :]
        sc = jnp.einsum('bqgrd,bsgd->bgrqs', q_blk, kw).astype(jnp.float32) * scale
        sc = sc - slopes[:, :, None, None] * d.astype(jnp.float32)
        sc = jnp.where((d >= 0) & (d < WINDOW) & (wpos[None, :] >= 0), sc, NEG)
        p = jax.nn.softmax(sc, axis=-1)
        o_win = jnp.einsum('bgrqs,bsgd->bqgrd', p.astype(vw.dtype), vw)
        return (g_blk[:, :, 0, :, :, None] * o_cmp + g_blk[:, :, 1, :, :, None] * o_slc
                + g_blk[:, :, 2, :, :, None] * o_win)

    out = lax.map(one, (qb, gb, jnp.arange(nb)))
    out = out.transpose(1, 0, 2, 3, 4, 5).reshape(b, s, C_MIX)
    return out @ w_out


def sq_relu_mlp(x, w1, w2):
    h = jax.nn.relu(x @ w1)
    return (h * h) @ w2


def _normal(k, shape, scale):
    return scale * jax.random.normal(k, shape, jnp.float32)


def setup_inputs(seed: int = 0) -> dict:
    key = jax.random.key(seed)
    ks = jax.random.split(key, 24)
    nA, nC, D = N_A_LAYERS, N_C_LAYERS, D_MODEL
    gain = lambda k, shape: 1.0 + _normal(k, shape, 0.05)
    return {
        "x": _normal(ks[0], (BATCH, SEQ, D), 1.0),
        "norm_mix_pre": gain(ks[1], (DEPTH, D)),
        "norm_mix_post": gain(ks[2], (DEPTH, D)),
        "norm_mlp_pre": gain(ks[3], (DEPTH, D)),
        "norm_mlp_post": gain(ks[4], (DEPTH, D)),
        "mlp_w1": _normal(ks[5], (DEPTH, D, D_FF), D ** -0.5),
        "mlp_w2": _normal(ks[6], (DEPTH, D_FF, D), D_FF ** -0.5),
        "a_w_in": _normal(ks[7], (nA, D, A_IN), D ** -0.5),
        "a_conv_w": _normal(ks[8], (nA, CONV_K, CONV_WIDTH), CONV_K ** -0.5),
        "a_q_norm": gain(ks[9], (nA, MLA_Q_RANK)),
        "a_w_q_up": _normal(ks[10], (nA, MLA_Q_RANK, MLA_HEADS * (MLA_NOPE + MLA_ROPE)), MLA_Q_RANK ** -0.5),
        "a_kv_norm": gain(ks[11], (nA, MLA_KV_RANK)),
        "a_w_kv_up": _normal(ks[12], (nA, MLA_KV_RANK, MLA_HEADS * (MLA_NOPE + MLA_V)), MLA_KV_RANK ** -0.5),
        "a_w_out": _normal(ks[13], (nA, A_MIX, D), A_MIX ** -0.5),
        "c_w_in": _normal(ks[14], (nC, D, C_IN), D ** -0.5),
        "c_cmp_pe_k": _normal(ks[15], (nC, CMP_LEN, NSA_DH), 0.1),
        "c_cmp_w1_k": _normal(ks[16], (nC, CMP_LEN, NSA_DH, NSA_DH), (CMP_LEN * NSA_DH) ** -0.5),
        "c_cmp_w2_k": _normal(ks[17], (nC, NSA_DH, NSA_DH), NSA_DH ** -0.5),
        "c_cmp_pe_v": _normal(ks[18], (nC, CMP_LEN, NSA_DH), 0.1),
        "c_cmp_w1_v": _normal(ks[19], (nC, CMP_LEN, NSA_DH, NSA_DH), (CMP_LEN * NSA_DH) ** -0.5),
        "c_cmp_w2_v": _normal(ks[20], (nC, NSA_DH, NSA_DH), NSA_DH ** -0.5),
        "c_w_out": _normal(ks[21], (nC, C_MIX, D), C_MIX ** -0.5),
    }


def reference(x, norm_mix_pre, norm_mix_post, norm_mlp_pre, norm_mlp_post, mlp_w1, mlp_w2,
              a_w_in, a_conv_w, a_q_norm, a_w_q_up, a_kv_norm, a_w_kv_up, a_w_out,
              c_w_in, c_cmp_pe_k, c_cmp_w1_k, c_cmp_w2_k, c_cmp_pe_v, c_cmp_w1_v, c_cmp_w2_v,
              c_w_out):
    for layer in range(DEPTH):
        i = layer // 2
        h = rms_norm(x, norm_mix_pre[layer])
        if layer % 2 == 0:
            m = mixer_a(h, a_w_in[i], a_conv_w[i], a_q_norm[i], a_w_q_up[i],
                        a_kv_norm[i], a_w_kv_up[i], a_w_out[i])
        else:
            m = mixer_c(h, c_w_in[i], c_cmp_pe_k[i], c_cmp_w1_k[i], c_cmp_w2_k[i],
                        c_cmp_pe_v[i], c_cmp_w1_v[i], c_cmp_w2_v[i], c_w_out[i])
        x = x + rms_norm(m, norm_mix_post[layer])
        h = rms_norm(x, norm_mlp_pre[layer])
        x = x + rms_norm(sq_relu_mlp(h, mlp_w1[layer], mlp_w2[layer]), norm_mlp_post[layer])
    return x
```

```python
import numpy as np
import ml_dtypes
from contextlib import ExitStack
import concourse.bass as bass
import concourse.mybir as mybir
from concourse.bass_utils import run_bass_kernel_spmd

F32 = mybir.dt.float32
BF16 = mybir.dt.bfloat16
ALU = mybir.AluOpType
AF = mybir.ActivationFunctionType

S = 4096
D = 1024
EPS = 1e-6
NT = 512
NTT = S // NT


class Trk:
    __slots__ = ("w", "r", "multi")

    def __init__(self, multi=False):
        self.w = {}
        self.r = {}
        self.multi = multi


class Buf:
    def __init__(self, t, multi=False, psum=False):
        self.t = t
        self.k = Trk(multi)
        self.psum = psum

    def __getitem__(self, idx):
        return self.t[idx]


class Prog:
    def __init__(self, nc):
        self.nc = nc
        self.e = {"pe": nc.tensor, "act": nc.scalar, "dve": nc.vector, "pool": nc.gpsimd, "sp": nc.sync}
        self.esem = {}
        self.dsem = {}
        self.seen = {k: {} for k in self.e}
        self.nsem = 0
        self.semtotal = {}
        self.ninst = 0
        self.bank_i = 0

    def _newsem(self, name):
        self.nsem += 1
        return (self.nsem, self.nc.alloc_semaphore("%s_%d" % (name, self.nsem)))

    def _wait(self, eng, deps):
        for key, (sem, val, src) in deps.items():
            if src == "pe" and eng == "pe":
                continue
            if src == "dma":
                val = max(val, self.semtotal[key])
            if self.seen[eng].get(key, 0) >= val:
                continue
            self.e[eng].wait_ge(sem, val)
            self.ninst += 1
            self.seen[eng][key] = val

    @staticmethod
    def _add(deps, d):
        for k, t in d.items():
            if k not in deps or deps[k][1] < t[1]:
                deps[k] = t

    def _deps(self, reads, writes, acc=False):
        deps = {}
        for b in reads:
            self._add(deps, b.k.w)
            if getattr(b, "psum", False):
                self._add(deps, b.k.r)
        for b in writes:
            self._add(deps, b.k.r)
            if not (b.k.multi or acc):
                self._add(deps, b.k.w)
        return deps

    def _commit(self, key, tok, reads, writes, acc=False):
        for b in reads:
            b.k.r[key] = tok
        for b in writes:
            if b.k.multi or acc:
                b.k.w[key] = tok
            else:
                b.k.w = {key: tok}
            b.k.r = {}

    def op(self, eng, fn, reads=(), writes=()):
        self._wait(eng, self._deps(reads, writes))
        ins = fn(self.e[eng])
        st = self.esem.get(eng)
        if st is None or st[2] >= 30000:
            k, sem = self._newsem("e" + eng)
            st = [k, sem, 0]
            self.esem[eng] = st
        st[2] += 1
        ins.then_inc(st[1], 1)
        self.ninst += 1
        self._commit(st[0], (st[1], st[2], eng), reads, writes)

    def dma(self, q, out, in_, reads, writes, chan, acc=False):
        self._wait(q, self._deps(reads, writes, acc))
        st = self.dsem.get(chan)
        if st is None or st[2] >= 30000:
            k, sem = self._newsem("d")
            st = [k, sem, 0]
            self.dsem[chan] = st
        ins = self.e[q].dma_start(out=out, in_=in_)
        st[2] += 16
        self.semtotal[st[0]] = st[2]
        ins.then_inc(st[1], 16)
        self.ninst += 1
        self._commit(st[0], (st[1], st[2], "dma"), reads, writes, acc)

    def barrier(self):
        toks = {}
        for eng, st in self.esem.items():
            toks[st[0]] = (st[1], st[2], eng)
        for ch, st in self.dsem.items():
            toks[st[0]] = (st[1], st[2], "dma")
        for eng in self.e:
            self._wait(eng, toks)

    def final_wait(self):
        toks = {}
        for ch, st in self.dsem.items():
            toks[st[0]] = (st[1], st[2], "dma")
        self._wait("sp", toks)


class Ctx:
    uid = 0

    def __init__(self, P):
        self.P = P
        self.nc = P.nc
        self.st = ExitStack()
        self.n = 0

    def sb(self, shape, dt, name="t"):
        Ctx.uid += 1
        t = self.st.enter_context(self.nc.sbuf_tensor("%s_%d" % (name, Ctx.uid), list(shape), dt))
        return Buf(t)

    def close(self):
        self.P.barrier()
        self.st.close()


def load_w_bf16(P, C, dram_ap, dst, kc, ncols, stage=None, engs=None):
    w3 = dram_ap.rearrange("(k p) c -> p k c", p=128)
    for k in range(kc):
        if stage is not None:
            P.dma("pool", dst[:, k, :], w3[:, k, :], [], [stage[k]], chan=("w", id(dst), k % 4))
        else:
            P.dma("pool", dst[:, k, :], w3[:, k, :], [], [dst], chan=("w", id(dst)), acc=True)


class Common:
    pass


def rstd_from_sumsq(P, G, bank, rstd, n_feat):
    P.op("act", lambda e: e.activation(out=rstd[:, :], in_=bank[:, :], func=AF.Sqrt, bias=G.eps[:, 0:1],
                                        scale=1.0 / n_feat), [bank, G.eps], [rstd])
    P.op("dve", lambda e: e.reciprocal(out=rstd[:, :], in_=rstd[:, :]), [rstd], [rstd])


def sumsq_bcast(P, G, src, sq, kc, bank, nparts=128):
    P.op("act", lambda e: e.activation(out=sq[:, 0:kc, :], in_=src[:, 0:kc, :], func=AF.Square), [src], [sq])
    for k in range(kc):
        P.op("pe", lambda e, k=k: e.matmul(bank[:, :], lhsT=G.ones[:, :], rhs=sq[:, k, :], start=(k == 0),
                                            stop=(k == kc - 1)), [sq, G.ones], [bank])


def phase_front_a(P, G):
    nc = P.nc
    C = Ctx(P)
    NIN = 2624
    win = C.sb([128, 8, NIN], BF16, "win")
    wq = C.sb([128, 6, 1024], BF16, "wq")
    wkv = C.sb([128, 2, 1024], BF16, "wkv")
    stage = None
    load_w_bf16(P, C, G.d["a_w_in"], win, 8, NIN, stage)
    load_w_bf16(P, C, G.d["a_w_q"], wq, 6, 1024, stage)
    load_w_bf16(P, C, G.d["a_w_kv"], wkv, 2, 1024, stage)
    xts = [C.sb([128, 8, NT], F32, "xt") for _ in range(2)]
    sqxs = [C.sb([128, 8, NT], BF16, "sqx") for _ in range(2)]
    hTs = [C.sb([128, 8, NT], BF16, "hT") for _ in range(2)]
    rstdxs = [C.sb([128, NT], F32, "rstdx") for _ in range(2)]
    sq = C.sb([128, 8, NT], BF16, "sq")
    rstd = C.sb([128, NT], F32, "rstd")
    rope = C.sb([128, NT], F32, "rope")
    u = [[C.sb([128, NT + 2], F32, "u") for _ in range(2)] for _ in range(4)]
    hvss = [C.sb([128, NT], F32, "hvs") for _ in range(2)]
    c1s = [C.sb([128, NT], F32, "c1") for _ in range(2)]
    yc = [C.sb([128, NT], BF16, "yc") for _ in range(2)]
    cq = C.sb([128, 6, NT], F32, "cq")
    cqn = C.sb([128, 6, NT], BF16, "cqn")
    ckv = C.sb([128, 2, NT], F32, "ckv")
    ckvn = C.sb([128, 2, NT], BF16, "ckvn")
    t1s = [C.sb([128, NT], F32, "t1") for _ in range(2)]
    t2s = [C.sb([128, NT], F32, "t2") for _ in range(2)]
    sqkv = C.sb([128, 2, NT], BF16, "sqkv")
    rstdkv = C.sb([128, NT], F32, "rstdkv")
    kr = C.sb([32, NT], BF16, "kr")
    qt = [C.sb([128, NT], BF16, "qt") for _ in range(2)]
    kn = [C.sb([128, NT], BF16, "kn") for _ in range(2)]
    vt = [C.sb([128, NT], BF16, "vt") for _ in range(2)]
    cst = G.cst
    xT3 = G.d["xT"].rearrange("(k p) t -> p k t", p=128)
    for cc in range(4):
        P.op("pool", lambda e, cc=cc: e.memset(u[cc][0][:, 0:2], 0.0), [], [u[cc][0]])
    ev = [0]

    def evac(out_ap, bank, in_ap, reads, writes):
        ev[0] += 1
        if ev[0] % 2:
            P.op("act", lambda e: e.activation(out=out_ap, in_=in_ap, func=AF.Copy), reads, writes)
        else:
            P.op("dve", lambda e: e.tensor_copy(out=out_ap, in_=in_ap), reads, writes)

    cur = {}

    def mm_chunk(bank, col0, m, rows=slice(0, 128)):
        hT = cur["hT"]
        for k in range(8):
            P.op("pe", lambda e, k=k: e.matmul(bank[0:m, :], lhsT=win[:, k, col0:col0 + m], rhs=hT[:, k, :],
                                                start=(k == 0), stop=(k == 7)), [win, hT], [bank])

    def load_x(ti):
        ts = slice(ti * NT, (ti + 1) * NT)
        xt = xts[ti % 2]
        P.dma("sp", xt[:, :, :], xT3[:, :, ts], [G.dr["xT"]], [xt], chan=("xt", ti % 2))

    def prenorm(ti):
        xt, sqx, hT, rstdx = xts[ti % 2], sqxs[ti % 2], hTs[ti % 2], rstdxs[ti % 2]
        bA = G.bank()
        sumsq_bcast(P, G, xt, sqx, 8, bA)
        rstd_from_sumsq(P, G, bA, rstdx, 1024)
        for k in range(8):
            P.op("dve", lambda e, k=k: e.scalar_tensor_tensor(out=hT[:, k, :], in0=xt[:, k, :],
                                                               scalar=cst[:, G.co["gpre0"] + k:G.co["gpre0"] + k + 1],
                                                               in1=rstdx[:, :], op0=ALU.mult, op1=ALU.mult),
                 [xt, rstdx, cst], [hT])

    load_x(0)
    prenorm(0)
    for ti in range(NTT):
        ts = slice(ti * NT, (ti + 1) * NT)
        cur["hT"] = hTs[ti % 2]
        if ti + 1 < NTT:
            load_x(ti + 1)
        P.dma("sp", rope[:, :], G.d["ropecs"][:, ts], [], [rope], chan="rope")
        for cc in range(4):
            b0, b1, b2 = G.bank(), G.bank(), G.bank()
            mm_chunk(b0, cc * 128, 128)
            mm_chunk(b1, 512 + cc * 128, 128)
            mm_chunk(b2, 1024 + cc * 128, 128)
            uc, un = u[cc][ti % 2], u[cc][(ti + 1) % 2]
            hvs, c1 = hvss[cc % 2], c1s[cc % 2]
            P.op("act", lambda e: e.activation(out=hvs[:, :], in_=b2[:, :], func=AF.Copy), [b2], [hvs])
            P.op("dve", lambda e: e.tensor_tensor(out=uc[:, 2:NT + 2], in0=b1[:, :], in1=hvs[:, :], op=ALU.mult),
                 [b1, hvs], [uc])
            cw = G.co["convw"] + cc * 3
            P.op("dve", lambda e: e.tensor_scalar(out=c1[:, :], in0=uc[:, 0:NT], scalar1=cst[:, cw:cw + 1],
                                                  scalar2=None, op0=ALU.mult), [uc, cst], [c1])
            P.op("dve", lambda e: e.scalar_tensor_tensor(out=c1[:, :], in0=uc[:, 1:NT + 1], scalar=cst[:, cw + 1:cw + 2],
                                                         in1=c1[:, :], op0=ALU.mult, op1=ALU.add), [uc, cst, c1], [c1])
            P.op("dve", lambda e: e.scalar_tensor_tensor(out=c1[:, :], in0=uc[:, 2:NT + 2], scalar=cst[:, cw + 2:cw + 3],
                                                         in1=c1[:, :], op0=ALU.mult, op1=ALU.add), [uc, cst, c1], [c1])
            y = yc[cc % 2]
            P.op("dve", lambda e: e.tensor_tensor(out=y[:, :], in0=b0[:, :], in1=c1[:, :], op=ALU.mult), [b0, c1], [y])
            P.op("pool", lambda e: e.tensor_copy(out=un[:, 0:2], in_=uc[:, NT:NT + 2]), [uc], [un])
            P.dma("pool", G.d["ycatT"][cc * 128:(cc + 1) * 128, ts], y[:, :], [y], [G.dr["ycatT"]], chan=("yc", cc % 2))
        if ti + 1 < NTT:
            prenorm(ti + 1)
        for j in range(2):
            b = G.bank()
            mm_chunk(b, 2304 + j * 128, 128)
            evac(ckv[:, j, :], b, b[:, :], [b], [ckv])
        for j in range(6):
            b = G.bank()
            mm_chunk(b, 1536 + j * 128, 128)
            evac(cq[:, j, :], b, b[:, :], [b], [cq])
        bk = G.bank()
        sumsq_bcast(P, G, ckv, sqkv, 2, bk)
        rstd_from_sumsq(P, G, bk, rstdkv, 256)
        for j in range(2):
            P.op("dve", lambda e, j=j: e.scalar_tensor_tensor(out=ckvn[:, j, :], in0=ckv[:, j, :],
                                                               scalar=cst[:, G.co["kvn"] + j:G.co["kvn"] + j + 1],
                                                               in1=rstdkv[:, :], op0=ALU.mult, op1=ALU.mult),
                 [ckv, rstdkv, cst], [ckvn])
        bq = G.bank()
        sumsq_bcast(P, G, cq, sq, 6, bq)
        rstd_from_sumsq(P, G, bq, rstd, 768)
        for j in range(6):
            P.op("dve", lambda e, j=j: e.scalar_tensor_tensor(out=cqn[:, j, :], in0=cq[:, j, :],
                                                               scalar=cst[:, G.co["qn"] + j:G.co["qn"] + j + 1],
                                                               in1=rstd[:, :], op0=ALU.mult, op1=ALU.mult),
                 [cq, rstd, cst], [cqn])
        t1, t2 = t1s[0], t2s[0]
        b = G.bank()
        mm_chunk(b, 2560, 64)
        P.op("dve", lambda e: e.tensor_tensor(out=t1[0:32, :], in0=b[0:32, :], in1=rope[0:32, :], op=ALU.mult),
             [b, rope], [t1])
        P.op("dve", lambda e: e.tensor_tensor(out=t2[0:32, :], in0=b[32:64, :], in1=rope[32:64, :], op=ALU.mult),
             [b, rope], [t2])
        P.op("dve", lambda e: e.tensor_tensor(out=kr[0:32, :], in0=t1[0:32, :], in1=t2[0:32, :], op=ALU.add),
             [t1, t2], [kr])
        for h in range(8):
            P.dma("pool", G.d["kT"][h, 64:96, ts], kr[0:32, :], [kr], [G.dr["kT"]], chan="kr")
        for hp in range(4):
            b = G.bank()
            for j in range(2):
                P.op("pe", lambda e, j=j: e.matmul(b[:, :], lhsT=wkv[:, j, hp * 128:(hp + 1) * 128], rhs=ckvn[:, j, :],
                                                    start=(j == 0), stop=(j == 1)), [wkv, ckvn], [b])
            kk = kn[hp % 2]
            evac(kk[:, :], b, b[:, :], [b], [kk])
            P.dma("pool", G.d["kT"][2 * hp, 0:64, ts], kk[0:64, :], [kk], [G.dr["kT"]], chan=("kn", hp % 2))
            P.dma("pool", G.d["kT"][2 * hp + 1, 0:64, ts], kk[64:128, :], [kk], [G.dr["kT"]], chan=("kn", hp % 2))
        for tb in range(4):
            b = G.bank()
            for j in range(2):
                P.op("pe", lambda e, j=j: e.matmul(b[:, :], lhsT=ckvn[:, j, tb * 128:(tb + 1) * 128],
                                                    rhs=wkv[:, j, 512:1024], start=(j == 0), stop=(j == 1)),
                     [wkv, ckvn], [b])
            vv = vt[tb % 2]
            evac(vv[:, :], b, b[:, :], [b], [vv])
            r0 = ti * NT + tb * 128
            P.dma("pool", G.d["vtok"][r0:r0 + 128, :], vv[:, :], [vv], [G.dr["vtok"]], chan=("vt", tb % 2))
        for h in range(8):
            b = G.bank()
            for j in range(6):
                P.op("pe", lambda e, j=j: e.matmul(b[:, :], lhsT=wq[:, j, h * 128:(h + 1) * 128], rhs=cqn[:, j, :],
                                                    start=(j == 0), stop=(j == 5)), [wq, cqn], [b])
            q = qt[h % 2]
            t1, t2 = t1s[h % 2], t2s[h % 2]
            P.op("act", lambda e: e.activation(out=q[0:64, :], in_=b[0:64, :], func=AF.Copy), [b], [q])
            P.op("dve", lambda e: e.tensor_tensor(out=t1[64:96, :], in0=b[64:96, :], in1=rope[64:96, :], op=ALU.mult),
                 [b, rope], [t1])
            P.op("dve", lambda e: e.tensor_tensor(out=t2[64:96, :], in0=b[96:128, :], in1=rope[96:128, :], op=ALU.mult),
                 [b, rope], [t2])
            P.op("dve", lambda e: e.tensor_tensor(out=q[64:96, :], in0=t1[64:96, :], in1=t2[64:96, :], op=ALU.add),
                 [t1, t2], [q])
            P.dma("pool", G.d["qT"][h, :, ts], q[0:96, :], [q], [G.dr["qT"]], chan=("qt", h % 2))
    C.close()


def phase_mla_attn(P, G):
    C = Ctx(P)
    scale = 96 ** -0.5
    QT = [C.sb([96, S], BF16, "QT") for _ in range(2)]
    KT = [C.sb([96, S], BF16, "KT") for _ in range(2)]
    V = [C.sb([128, 32, 128], BF16, "V") for _ in range(2)]
    E = [C.sb([128, NT], BF16, "E") for _ in range(3)]
    cm = C.sb([128, 4, NT], BF16, "cm")
    rz = C.sb([64, NT], F32, "rz")
    yo = [C.sb([64, NT], BF16, "yo") for _ in range(2)]
    P.dma("sp", cm[:, :, :], G.d["cmask"].rearrange("o p n -> p o n"), [], [cm], chan="cm")
    for i in range(2):
        P.op("pool", lambda e: e.memset(V[i][:, :, 64:128], 1.0), [], [V[i]])
    sb = G.banks[0:3]
    ob = G.banks[4:6]
    vsrc = G.d["vtok"].rearrange("(c p) f -> p c f", p=128)
    its = [(T, c) for T in range(NTT) for c in range(4 * T + 4)]
    n_o = [0]

    def load(h):
        P.dma("sp", QT[h % 2][:, :], G.d["qT"][h, :, :], [G.dr["qT"]], [QT[h % 2]], chan=("QT", h % 2))
        P.dma("sp", KT[h % 2][:, :], G.d["kT"][h, :, :], [G.dr["kT"]], [KT[h % 2]], chan=("KT", h % 2))
        P.dma("sp", V[h % 2][:, :, 0:64], vsrc[:, :, h * 64:(h + 1) * 64], [G.dr["vtok"]], [V[h % 2]],
              chan=("V", h % 2))

    load(0)
    for h in range(8):
        if h + 1 < 8:
            load(h + 1)
        q, k, v = QT[h % 2], KT[h % 2], V[h % 2]

        def emit_s(n):
            T, c = its[n]
            b = sb[n % 3]
            diag = c >= 4 * T
            P.op("pe", lambda e: e.matmul(b[:, :], lhsT=k[0:96, c * 128:(c + 1) * 128], rhs=q[0:96, T * NT:(T + 1) * NT],
                                          start=True, stop=not diag), [k, q], [b])
            if diag:
                P.op("pe", lambda e: e.matmul(b[:, :], lhsT=G.ident[:, :], rhs=cm[:, c - 4 * T, :], start=False,
                                              stop=True), [G.ident, cm], [b])

        emit_s(0)
        emit_s(1)
        for n in range(len(its)):
            T, c = its[n]
            if n + 2 < len(its):
                emit_s(n + 2)
            b = sb[n % 3]
            ee = E[n % 3]
            P.op("act", lambda e: e.activation(out=ee[:, :], in_=b[:, :], func=AF.Exp, scale=scale), [b], [ee])
            o = ob[T % 2]
            last = (c == 4 * T + 3)
            P.op("pe", lambda e: e.matmul(o[:, :], lhsT=v[:, c, :], rhs=ee[:, :], start=(c == 0), stop=last),
                 [v, ee], [o])
            if last:
                y = yo[n_o[0] % 2]
                n_o[0] += 1
                P.op("dve", lambda e: e.reciprocal(out=rz[0:64, :], in_=o[64:128, :]), [o], [rz])
                P.op("dve", lambda e: e.tensor_tensor(out=y[0:64, :], in0=o[0:64, :], in1=rz[0:64, :], op=ALU.mult),
                     [o, rz], [y])
                P.dma("pool", G.d["ycatT"][512 + h * 64:512 + (h + 1) * 64, T * NT:(T + 1) * NT], y[0:64, :], [y],
                      [G.dr["ycatT"]], chan=("yo", (n_o[0] - 1) % 2))
    C.close()


def phase_outproj(P, G, wname, yname, xin, xout, hout, gpost, gmpre):
    C = Ctx(P)
    wo = C.sb([128, 8, 1024], BF16, "wo")
    load_w_bf16(P, C, G.d[wname], wo, 8, 1024)
    yt = [C.sb([128, 8, NT], BF16, "yt") for _ in range(2)]
    xts = [C.sb([128, 8, NT], F32, "xt") for _ in range(2)]
    ms = [C.sb([128, 8, NT], F32, "m") for _ in range(2)]
    sqs = [C.sb([128, 8, NT], BF16, "sq")] * 2
    sq2s = [C.sb([128, 8, NT], BF16, "sq2")] * 2
    rstds = [C.sb([128, NT], F32, "rstd") for _ in range(2)]
    rstd2s = [C.sb([128, NT], F32, "rstd2") for _ in range(2)]
    h2s = [C.sb([128, 8, NT], BF16, "h2")] * 2
    def chunked(bufs):
        first = [Buf(bufs[0].t) for _ in range(8)]
        return [first, first if bufs[1] is bufs[0] else [Buf(bufs[1].t) for _ in range(8)]]
    xtc, mc, sqc, sq2c, h2c = chunked(xts), chunked(ms), chunked(sqs), chunked(sq2s), chunked(h2s)
    cst = G.cst
    y3 = G.d[yname].rearrange("(k p) t -> p k t", p=128)
    x3 = G.d[xin].rearrange("(k p) t -> p k t", p=128)
    xo3 = G.d[xout].rearrange("(k p) t -> p k t", p=128)
    ho3 = G.d[hout].rearrange("(k p) t -> p k t", p=128)
    mb = G.banks[0:6]
    bAs = G.banks[6]
    bBs = G.banks[7]
    nb = [0]

    def ones_mm(bank, sq, sqk, k):
        P.op("pe", lambda e: e.matmul(bank[:, :], lhsT=G.ones[:, :], rhs=sq[:, k, :], start=(k == 0), stop=(k == 7)),
             [sqk[k], G.ones], [bank])

    def rstd_ops(bank, rstd):
        P.op("act", lambda e: e.activation(out=rstd[:, :], in_=bank[:, :], func=AF.Sqrt, bias=G.eps[:, 0:1], scale=1.0 / 1024),
             [bank, G.eps], [rstd])
        P.op("dve", lambda e: e.reciprocal(out=rstd[:, :], in_=rstd[:, :]), [rstd], [rstd])

    def epi1_step(tp, k):
        xt, m, sq2, rstd = xts[tp % 2], ms[tp % 2], sq2s[tp % 2], rstds[tp % 2]
        mk, xk, s2k = mc[tp % 2][k], xtc[tp % 2][k], sq2c[tp % 2][k]
        P.op("dve", lambda e: e.scalar_tensor_tensor(out=m[:, k, :], in0=m[:, k, :],
                                                     scalar=cst[:, G.co[gpost] + k:G.co[gpost] + k + 1],
                                                     in1=rstd[:, :], op0=ALU.mult, op1=ALU.mult), [mk, rstd, cst], [mk])
        P.op("pool", lambda e: e.tensor_tensor(out=xt[:, k, :], in0=xt[:, k, :], in1=m[:, k, :], op=ALU.add), [xk, mk], [xk])
        P.op("act", lambda e: e.activation(out=sq2[:, k, :], in_=xt[:, k, :], func=AF.Square), [xk], [s2k])

    def epi2(tp):
        ts = slice(tp * NT, (tp + 1) * NT)
        xt, h2, rstd2 = xts[tp % 2], h2s[tp % 2], rstd2s[tp % 2]
        P.dma("pool", xo3[:, :, ts], xt[:, :, :], xtc[tp % 2], [G.dr[xout]], chan=("opxo", tp % 2))
        rstd_ops(bBs, rstd2)
        for k in range(8):
            P.op("dve", lambda e: e.scalar_tensor_tensor(out=h2[:, k, :], in0=xt[:, k, :],
                                                         scalar=cst[:, G.co[gmpre] + k:G.co[gmpre] + k + 1],
                                                         in1=rstd2[:, :], op0=ALU.mult, op1=ALU.mult),
                 [xtc[tp % 2][k], rstd2, cst], [h2c[tp % 2][k]])
        P.dma("pool", ho3[:, :, ts], h2[:, :, :], h2c[tp % 2], [G.dr[hout]], chan=("opho", tp % 2))

    def main(ti, prev):
        if ti is not None:
            ts = slice(ti * NT, (ti + 1) * NT)
            y, xt, m, sq = yt[ti % 2], xts[ti % 2], ms[ti % 2], sqs[ti % 2]
            P.dma("sp", y[:, :, :], y3[:, :, ts], [G.dr[yname]], [y], chan=("opy", ti % 2))
            P.dma("sp", xt[:, :, :], x3[:, :, ts], [G.dr[xin]], xtc[ti % 2], chan=("opx", ti % 2))
        if prev is not None:
            ones_mm(bAs, sqs[prev % 2], sqc[prev % 2], 7)
            rstd_ops(bAs, rstds[prev % 2])
        for oc in range(8):
            if ti is not None:
                b = mb[nb[0] % 6]
                nb[0] += 1
                for k in range(8):
                    P.op("pe", lambda e: e.matmul(b[:, :], lhsT=wo[:, k, oc * 128:(oc + 1) * 128], rhs=y[:, k, :],
                                                  start=(k == 0), stop=(k == 7)), [wo, y], [b])
                P.op("dve", lambda e: e.tensor_copy(out=m[:, oc, :], in_=b[:, :]), [b], [mc[ti % 2][oc]])
                P.op("act", lambda e: e.activation(out=sq[:, oc, :], in_=m[:, oc, :], func=AF.Square), [mc[ti % 2][oc]],
                     [sqc[ti % 2][oc]])
                if oc > 0:
                    ones_mm(bAs, sq, sqc[ti % 2], oc - 1)
            if prev is not None:
                epi1_step(prev, oc)
                if oc > 0:
                    ones_mm(bBs, sq2s[prev % 2], sq2c[prev % 2], oc - 1)
        if prev is not None:
            ones_mm(bBs, sq2s[prev % 2], sq2c[prev % 2], 7)
            epi2(prev)

    for ti in range(NTT):
        main(ti, ti - 1 if ti > 0 else None)
    main(None, NTT - 1)
    C.close()


def prefetch_w1(P, G, w1name):
    Cw = Ctx(P)
    W1 = Cw.sb([128, 8, 4096], BF16, "W1")
    load_w_bf16(P, Cw, G.d[w1name], W1, 8, 4096)
    return Cw, W1


def phase_mlp(P, G, w1name, w2name, hin, xin, xout, hout, gpost, gnext, W1=None):
    C = Ctx(P)
    MT = 256
    if W1 is None:
        W1 = C.sb([128, 8, 4096], BF16, "W1")
        load_w_bf16(P, C, G.d[w1name], W1, 8, 4096)
    W2 = C.sb([128, 32, 1024], BF16, "W2")
    W2c = [Buf(W2.t) for _ in range(32)]
    load_w_bf16(P, C, G.d[w2name], W2, 32, 1024, stage=W2c)
    ht = [C.sb([128, 8, MT], BF16, "ht") for _ in range(2)]
    xts = [C.sb([128, 8, MT], F32, "xt") for _ in range(2)]
    mos = [C.sb([128, 8, MT], F32, "mo") for _ in range(2)]
    sqs = [C.sb([128, 8, MT], BF16, "sq") for _ in range(2)]
    rstds = [C.sb([128, MT], F32, "rstd") for _ in range(2)]
    hns = [C.sb([128, 8, MT], BF16, "hn") for _ in range(2)]
    rl = [C.sb([128, MT], F32, "rl") for _ in range(2)]
    hid = [C.sb([128, MT], BF16, "hid") for _ in range(3)]
    cst = G.cst
    h3 = G.d[hin].rearrange("(k p) t -> p k t", p=128)
    x3 = G.d[xin].rearrange("(k p) t -> p k t", p=128)
    xo3 = G.d[xout].rearrange("(k p) t -> p k t", p=128)
    hb = G.banks[0:3]
    eb = G.banks[3]
    obk = G.banks[4:8]
    ntile = S // MT

    def epilogue(ti):
        ts = slice(ti * MT, (ti + 1) * MT)
        xt, mo, sq, rstd, hn = xts[ti % 2], mos[ti % 2], sqs[ti % 2], rstds[ti % 2], hns[ti % 2]
        P.op("act", lambda e: e.activation(out=sq[:, :, :], in_=mo[:, :, :], func=AF.Square), [mo], [sq])
        for k in range(8):
            P.op("pe", lambda e: e.matmul(eb[:, 0:MT], lhsT=G.ones[:, :], rhs=sq[:, k, :], start=(k == 0), stop=(k == 7)),
                 [sq, G.ones], [eb])
        P.op("act", lambda e: e.activation(out=rstd[:, :], in_=eb[:, 0:MT], func=AF.Sqrt, bias=G.eps[:, 0:1], scale=1.0 / 1024),
             [eb, G.eps], [rstd])
        P.op("dve", lambda e: e.reciprocal(out=rstd[:, :], in_=rstd[:, :]), [rstd], [rstd])
        for k in range(8):
            P.op("dve", lambda e: e.scalar_tensor_tensor(out=mo[:, k, :], in0=mo[:, k, :],
                                                         scalar=cst[:, G.co[gpost] + k:G.co[gpost] + k + 1],
                                                         in1=rstd[:, :], op0=ALU.mult, op1=ALU.mult), [mo, rstd, cst], [mo])
        P.op("pool", lambda e: e.tensor_tensor(out=xt[:, :, :], in0=xt[:, :, :], in1=mo[:, :, :], op=ALU.add), [xt, mo], [xt])
        P.dma("pool", xo3[:, :, ts], xt[:, :, :], [xt], [G.dr[xout]], chan=("mlpxo", ti % 2))

    def epilogue_b(ti):
        ts = slice(ti * MT, (ti + 1) * MT)
        xt, mo, sq, rstd, hn = xts[ti % 2], mos[ti % 2], sqs[ti % 2], rstds[ti % 2], hns[ti % 2]
        if hout is not None:
            ho3 = G.d[hout].rearrange("(k p) t -> p k t", p=128)
            P.op("act", lambda e: e.activation(out=sq[:, :, :], in_=xt[:, :, :], func=AF.Square), [xt], [sq])
            for k in range(8):
                P.op("pe", lambda e: e.matmul(eb[:, 0:MT], lhsT=G.ones[:, :], rhs=sq[:, k, :], start=(k == 0), stop=(k == 7)),
                     [sq, G.ones], [eb])
            P.op("act", lambda e: e.activation(out=rstd[:, :], in_=eb[:, 0:MT], func=AF.Sqrt, bias=G.eps[:, 0:1],
                                               scale=1.0 / 1024), [eb, G.eps], [rstd])
            P.op("dve", lambda e: e.reciprocal(out=rstd[:, :], in_=rstd[:, :]), [rstd], [rstd])
            for k in range(8):
                P.op("dve", lambda e: e.scalar_tensor_tensor(out=hn[:, k, :], in0=xt[:, k, :],
                                                             scalar=cst[:, G.co[gnext] + k:G.co[gnext] + k + 1],
                                                             in1=rstd[:, :], op0=ALU.mult, op1=ALU.mult), [xt, rstd, cst], [hn])
            P.dma("pool", ho3[:, :, ts], hn[:, :, :], [hn], [G.dr[hout]], chan=("mlpho", ti % 2))

    for ti in range(ntile):
        ts = slice(ti * MT, (ti + 1) * MT)
        h = ht[ti % 2]
        xt = xts[ti % 2]
        mo = mos[ti % 2]
        P.dma("sp", h[:, :, :], h3[:, :, ts], [G.dr[hin]], [h], chan=("mlph", ti % 2))
        P.dma("sp", xt[:, :, :], x3[:, :, ts], [G.dr[xin]], [xt], chan=("mlpx", ti % 2))

        def emit_h(f):
            b = hb[f % 3]
            for k in range(8):
                P.op("pe", lambda e: e.matmul(b[:, 0:MT], lhsT=W1[:, k, f * 128:(f + 1) * 128], rhs=h[:, k, :],
                                              start=(k == 0), stop=(k == 7)), [W1, h], [b])
        emit_h(0)
        emit_h(1)
        for f in range(32):
            if f + 2 < 32:
                emit_h(f + 2)
            b = hb[f % 3]
            r = rl[f % 2]
            hd = hid[f % 3]
            P.op("act", lambda e: e.activation(out=r[:, :], in_=b[:, 0:MT], func=AF.Relu), [b], [r])
            P.op("dve", lambda e: e.tensor_tensor(out=hd[:, :], in0=r[:, :], in1=r[:, :], op=ALU.mult), [r], [hd])
            for oc in range(8):
                o = obk[oc // 2]
                P.op("pe", lambda e: e.matmul(o[:, (oc % 2) * MT:(oc % 2 + 1) * MT], lhsT=W2[:, f, oc * 128:(oc + 1) * 128],
                                              rhs=hd[:, :], start=(f == 0 and oc % 2 == 0), stop=(f == 31),
                                              skip_group_check=True), [W2c[f], hd], [o])
            if f == 4 and ti > 0:
                epilogue(ti - 1)
            if f == 18 and ti > 0:
                epilogue_b(ti - 1)
        for ob_i in range(4):
            o = obk[ob_i]
            if ob_i % 2:
                P.op("act", lambda e: e.activation(out=mo[:, 2 * ob_i:2 * ob_i + 2, :], in_=o[:, :].rearrange("p (a n) -> p a n", a=2),
                                                   func=AF.Copy), [o], [mo])
            else:
                P.op("dve", lambda e: e.tensor_copy(out=mo[:, 2 * ob_i:2 * ob_i + 2, :], in_=o[:, :].rearrange("p (a n) -> p a n", a=2)),
                     [o], [mo])
    epilogue(ntile - 1)
    epilogue_b(ntile - 1)
    C.close()


def phase_front_c(P, G):
    C = Ctx(P)
    NIN = 2608
    win = C.sb([128, 8, NIN], BF16, "cwin")
    stage = None
    load_w_bf16(P, C, G.d["c_w_in"], win, 8, NIN, stage)
    ht = [C.sb([128, 8, NT], BF16, "ht") for _ in range(2)]
    ob = [C.sb([128, NT], BF16, "ob") for _ in range(3)]
    gt = C.sb([48, NT], BF16, "gt")
    h3 = G.d["h3T"].rearrange("(k p) t -> p k t", p=128)
    n = 0
    for ti in range(NTT):
        ts = slice(ti * NT, (ti + 1) * NT)
        h = ht[ti % 2]
        P.dma("sp", h[:, :, :], h3[:, :, ts], [G.dr["h3T"]], [h], chan=("fch", ti % 2))
        for c in range(16):
            b = G.bank()
            for k in range(8):
                P.op("pe", lambda e: e.matmul(b[:, :], lhsT=win[:, k, c * 128:(c + 1) * 128], rhs=h[:, k, :],
                                              start=(k == 0), stop=(k == 7)), [win, h], [b])
            o = ob[n % 3]
            if n % 2:
                P.op("act", lambda e: e.activation(out=o[:, :], in_=b[:, :], func=AF.Copy), [b], [o])
            else:
                P.op("dve", lambda e: e.tensor_copy(out=o[:, :], in_=b[:, :]), [b], [o])
            if c < 8:
                d0, d1, nm = G.d["qTn"][2 * c, :, ts], G.d["qTn"][2 * c + 1, :, ts], "qTn"
            else:
                wh, gp = (c - 8) // 2, (c - 8) % 2
                d0, d1, nm = G.d["kvT"][wh, 2 * gp, :, ts], G.d["kvT"][wh, 2 * gp + 1, :, ts], "kvT"
            P.dma("pool", d0, o[0:64, :], [o], [G.dr[nm]], chan=("fco", n % 3))
            P.dma("pool", d1, o[64:128, :], [o], [G.dr[nm]], chan=("fco", n % 3))
            n += 1
        b = G.bank()
        for k in range(8):
            P.op("pe", lambda e: e.matmul(b[0:48, :], lhsT=win[:, k, 2560:2608], rhs=h[:, k, :], start=(k == 0),
                                          stop=(k == 7)), [win, h], [b])
        P.op("act", lambda e: e.activation(out=gt[0:48, :], in_=b[0:48, :], func=AF.Sigmoid), [b], [gt])
        P.dma("pool", G.d["gT"][:, ts], gt[0:48, :], [gt], [G.dr["gT"]], chan="fcg")
        for tb in range(4):
            b = G.bank()
            for k in range(8):
                P.op("pe", lambda e: e.matmul(b[:, :], lhsT=h[:, k, tb * 128:(tb + 1) * 128], rhs=win[:, k, 2048:2560],
                                              start=(k == 0), stop=(k == 7)), [win, h], [b])
            o = ob[n % 3]
            if n % 2:
                P.op("act", lambda e: e.activation(out=o[:, :], in_=b[:, :], func=AF.Copy), [b], [o])
            else:
                P.op("dve", lambda e: e.tensor_copy(out=o[:, :], in_=b[:, :]), [b], [o])
            r0 = ti * NT + tb * 128
            P.dma("pool", G.d["vsw"][r0:r0 + 128, :], o[:, :], [o], [G.dr["vsw"]], chan=("fco", n % 3))
            n += 1
    C.close()


def phase_compress(P, G):
    C = Ctx(P)
    stage = C.sb([64, 2048], F32, "cstg")
    w1 = [C.sb([64, 2048], BF16, "cw1") for _ in range(2)]
    w2 = [C.sb([64, 64], BF16, "cw2") for _ in range(2)]
    peT = [C.sb([64, 32], BF16, "cpe") for _ in range(2)]
    bias = [C.sb([64, 1], F32, "cbias") for _ in range(2)]
    for kv, sfx in enumerate(("k", "v")):
        P.dma("sp", stage[:, :], G.d["cw1" + sfx][:, :], [], [stage], chan="cstg")
        P.op("dve", lambda e: e.tensor_copy(out=w1[kv][:, :], in_=stage[:, :]), [stage], [w1[kv]])
        P.dma("sp", stage[:, 0:64], G.d["cw2" + sfx][:, :], [], [stage], chan="cstg")
        P.op("dve", lambda e: e.tensor_copy(out=w2[kv][:, :], in_=stage[:, 0:64]), [stage], [w2[kv]])
        P.dma("sp", stage[:, 0:32], G.d["cpe" + sfx][:, :], [], [stage], chan="cstg")
        P.op("dve", lambda e: e.tensor_copy(out=peT[kv][:, :], in_=stage[:, 0:32]), [stage], [peT[kv]])
        b = G.bank()
        for l in range(32):
            P.op("pe", lambda e: e.matmul(b[0:64, 0:1], lhsT=w1[kv][:, l * 64:(l + 1) * 64], rhs=peT[kv][:, l:l + 1],
                                          start=(l == 0), stop=(l == 31)), [w1[kv], peT[kv]], [b])
        P.op("dve", lambda e: e.tensor_copy(out=bias[kv][:, :], in_=b[0:64, 0:1]), [b], [bias[kv]])
    zt = [C.sb([64, S], BF16, "zt") for _ in range(2)]
    xb = C.sb([64, 256], F32, "xb")
    tt = C.sb([64, 256], F32, "tt")
    sg = C.sb([64, 256], F32, "sg")
    hdn = C.sb([64, 256], BF16, "hdn")
    P.op("pool", lambda e: e.memset(hdn[:, :], 0.0), [], [hdn])
    n = 0
    for g in range(4):
        KC, VC = G.KC[g], G.VC[g]
        P.op("pool", lambda e: e.memset(KC[:, :], 0.0), [], [KC])
        P.op("pool", lambda e: e.memset(KC[64:65, :], 1.0), [], [KC])
        P.op("pool", lambda e: e.memset(VC[:, :, 0:64], 0.0), [], [VC])
        P.op("pool", lambda e: e.memset(VC[:, :, 64:128], 1.0), [], [VC])
        for kv in range(2):
            z = zt[n % 2]
            n += 1
            P.dma("sp", z[:, :], G.d["kvT"][kv, g, :, :], [G.dr["kvT"]], [z], chan=("zt", n % 2))
            b = G.bank()
            for l in range(32):
                P.op("pe", lambda e: e.matmul(b[0:64, 0:255], lhsT=w1[kv][:, l * 64:(l + 1) * 64],
                                              rhs=z[:, l:l + 16 * 254 + 1:16], start=(l == 0), stop=(l == 31)),
                     [w1[kv], z], [b])
            P.op("dve", lambda e: e.tensor_scalar(out=xb[:, 0:255], in0=b[0:64, 0:255], scalar1=bias[kv][:, 0:1],
                                                  scalar2=None, op0=ALU.add), [b, bias[kv]], [xb])
            P.op("dve", lambda e: e.tensor_tensor(out=tt[:, 0:255], in0=xb[:, 0:255], in1=xb[:, 0:255], op=ALU.mult),
                 [xb], [tt])
            P.op("dve", lambda e: e.tensor_scalar(out=tt[:, 0:255], in0=tt[:, 0:255], scalar1=0.044715, scalar2=1.0,
                                                  op0=ALU.mult, op1=ALU.add), [tt], [tt])
            P.op("dve", lambda e: e.tensor_tensor(out=tt[:, 0:255], in0=tt[:, 0:255], in1=xb[:, 0:255], op=ALU.mult),
                 [tt, xb], [tt])
            P.op("act", lambda e: e.activation(out=sg[:, 0:255], in_=tt[:, 0:255], func=AF.Sigmoid,
                                               scale=2.0 * 0.7978845608028654), [tt], [sg])
            P.op("dve", lambda e: e.tensor_tensor(out=hdn[:, 0:255], in0=xb[:, 0:255], in1=sg[:, 0:255], op=ALU.mult),
                 [xb, sg], [hdn])
            b2 = G.bank()
            if kv == 0:
                P.op("pe", lambda e: e.matmul(b2[0:64, 0:255], lhsT=w2[0][:, :], rhs=hdn[:, 0:255], start=True, stop=True),
                     [w2[0], hdn], [b2])
                P.op("act", lambda e: e.activation(out=KC[0:64, 0:255], in_=b2[0:64, 0:255], func=AF.Copy), [b2], [KC])
            else:
                for c in range(2):
                    ncol = 128 if c == 0 else 127
                    b3 = G.bank()
                    P.op("pe", lambda e: e.matmul(b3[0:ncol, 0:64], lhsT=hdn[:, c * 128:c * 128 + ncol], rhs=w2[1][:, :],
                                                  start=True, stop=True), [w2[1], hdn], [b3])
                    P.op("act", lambda e: e.activation(out=VC[0:ncol, c, 0:64], in_=b3[0:ncol, 0:64], func=AF.Copy),
                         [b3], [VC])
    C.close()


def phase_nsa(P, G, debug=False):
    C = Ctx(P)
    scale = 0.125
    Q = [C.sb([128, S], BF16, "Q") for _ in range(4)]
    Qs = [[C.sb([128, NT], BF16, "Qs") for _ in range(4)] for _ in range(2)]
    KS = C.sb([128, S], BF16, "KS")
    KW = C.sb([128, S], BF16, "KW")
    VS = C.sb([128, 32, 128], BF16, "VS")
    VW = C.sb([128, 32, 128], BF16, "VW")
    gT = C.sb([48, S], BF16, "gT")
    E = [C.sb([128, NT], BF16, "E") for _ in range(3)]
    masks = C.sb([128, 13, NT], BF16, "masks")
    selm = C.sb([48, 48 * 64], BF16, "selm")
    ov2 = C.sb([128, 2, 65], BF16, "ov2")
    btab = C.sb([128, 704], F32, "btab")
    qterm = C.sb([64, 4, NT], F32, "qterm")
    addt = C.sb([128, 4, 64], F32, "addt")
    acc = [[C.sb([64, NT], F32, "acc") for _ in range(4)] for _ in range(2)]
    imp = C.sb([128, 4, 64], F32, "imp")
    itmp = C.sb([128, 4, 64], F32, "itmp")
    zr = C.sb([128, 4, 1], F32, "zr")
    rz = C.sb([64, NT], F32, "rz")
    coefs = [C.sb([64, NT], F32, "coef") for _ in range(2)]
    ftmp = C.sb([64, NT], F32, "ftmp")
    m8 = C.sb([128, 16], F32, "m8")
    wk = C.sb([128, 64], F32, "wk")
    vv = C.sb([128, 64], F32, "vv")
    ngf = C.sb([128, 64], F32, "ngf")
    ng = C.sb([128, 64], BF16, "ng")
    yo = [C.sb([64, NT], BF16, "yo") for _ in range(2)]
    sbk = G.banks[0:3]
    obk = G.banks[3:5]
    obc = G.banks[5]
    ub = G.banks[6]
    gb = G.banks[7]
    tb = G.banks[7]
    P.dma("sp", masks[:, 0:4, :], G.d["cmask"].rearrange("o p n -> p o n"), [], [masks], chan="nsac")
    P.dma("sp", masks[:, 4:8, :], G.d["wmask"].rearrange("o p n -> p o n"), [], [masks], chan="nsac")
    P.dma("sp", masks[:, 8:13, :], G.d["cmpmask"].rearrange("o p n -> p o n"), [], [masks], chan="nsac")
    P.dma("sp", selm[:, :], G.d["selm"][:, :], [], [selm], chan="nsac")
    P.dma("sp", ov2[:, :, :], G.d["ov2"][:, :, :], [], [ov2], chan="nsac")
    P.dma("sp", btab[:, :], G.d["biastab"][:, :], [], [btab], chan="nsac")
    P.dma("sp", gT[:, :], G.d["gT"][:, :], [G.dr["gT"]], [gT], chan="nsac")
    P.dma("sp", KS[64:128, :], G.d["expand"][:, :], [], [KS], chan="nsak")
    P.op("pool", lambda e: e.memset(KW[64:128, :], 0.0), [], [KW])
    P.op("pool", lambda e: e.memset(KW[64:65, :], 1.0), [], [KW])
    for r in range(4):
        P.op("pool", lambda e: e.memset(Q[r][64:128, :], 0.0), [], [Q[r]])
    P.op("pool", lambda e: e.memset(VS[:, :, 64:128], 1.0), [], [VS])
    P.op("pool", lambda e: e.memset(VW[:, :, 64:128], 1.0), [], [VW])
    vsrc = G.d["vsw"].rearrange("(c p) f -> p c f", p=128)
    cnt = {"s": 0, "o": 0, "y": 0, "f": 0}
    NOSB = 8
    osb = [C.sb([128, NT], F32, "osb") for _ in range(NOSB)]
    TORDER = [0, 7, 1, 6, 2, 5, 3, 4]
    PAR = {T: i % 2 for i, T in enumerate(TORDER)}
    gbs = [C.sb([64, 12, NT], BF16, "gbs") for _ in range(2)]
    ngs = [C.sb([128, 64], BF16, "ngs") for _ in range(4)]

    def finalize(o, br, h, T, r, first):
        ts = slice(T * NT, (T + 1) * NT)
        col = (br * 16 + h) * 64
        os_ = osb[cnt["f"] % NOSB]
        cnt["f"] += 1
        P.op("act", lambda e: e.activation(out=os_[:, :], in_=o[:, :], func=AF.Identity, bias=G.tiny[:, 0:1], scale=1.0),
             [o, G.tiny], [os_])
        gsb = gbs[PAR[T]]
        gi = br * 4 + r
        coef = coefs[cnt["f"] % 2]
        P.op("dve", lambda e: e.reciprocal(out=coef[0:64, :], in_=os_[64:128, :]), [os_], [coef])
        P.op("dve", lambda e: e.tensor_tensor(out=coef[0:64, :], in0=gsb[0:64, gi, :], in1=coef[0:64, :], op=ALU.mult),
             [gsb, coef], [coef])
        a = acc[PAR[T]][r]
        if first:
            P.op("pool", lambda e: e.tensor_tensor(out=a[0:64, :], in0=os_[0:64, :], in1=coef[0:64, :], op=ALU.mult),
                 [os_, coef], [a])
        else:
            P.op("pool", lambda e: e.tensor_tensor(out=ftmp[0:64, :], in0=os_[0:64, :], in1=coef[0:64, :], op=ALU.mult),
                 [os_, coef], [ftmp])
            P.op("pool", lambda e: e.tensor_tensor(out=a[0:64, :], in0=a[0:64, :], in1=ftmp[0:64, :], op=ALU.add),
                 [a, ftmp], [a])
        if debug:
            P.dma("pool", G.d["dbgacc"][br, h * 64:(h + 1) * 64, ts], a[0:64, :], [a], [G.dr["dbgacc"]], chan="dbg")

    def run_stream(items):
        base = cnt["s"]
        cnt["s"] += len(items)
        index = {id(it): i for i, it in enumerate(items)}
        depi = [index[id(it["dep"])] if it.get("dep") is not None else -1 for it in items]
        st = {"emitted": 0}

        def emit_upto(limit, done):
            while st["emitted"] <= limit and st["emitted"] < len(items) and depi[st["emitted"]] <= done:
                m = st["emitted"]
                items[m]["emit_s"](items[m]["c"], sbk[(base + m) % 3])
                st["emitted"] += 1

        for n, it in enumerate(items):
            for hook in it.get("pre", ()):
                hook()
            emit_upto(n + 2, n - 1)
            assert st["emitted"] > n
            b = sbk[(base + n) % 3]
            ee = E[(base + n) % 3]
            bc = it["bc"]
            o = it["o"]
            vtile = it["v"]
            c = it["c"]
            P.op("act", lambda e: e.activation(out=ee[:, :], in_=b[:, :], func=AF.Exp, bias=btab[:, bc:bc + 1], scale=scale),
                 [b, btab], [ee])
            P.op("pe", lambda e: e.matmul(o[:, :], lhsT=vtile[:, c, :], rhs=ee[:, :], start=it["first"], stop=it["last"]),
                 [vtile, ee], [o])
            if it.get("post"):
                it["post"](ee)
            if it["last"]:
                it["fin"]()

    for g in range(4):
        KC, VC = G.KC[g], G.VC[g]
        P.dma("sp", KS[0:64, :], G.d["kvT"][2, g, :, :], [G.dr["kvT"]], [KS], chan="nsak")
        P.dma("sp", KW[0:64, :], G.d["kvT"][3, g, :, :], [G.dr["kvT"]], [KW], chan="nsak")
        P.dma("sp", VS[:, :, 0:64], vsrc[:, :, g * 64:(g + 1) * 64], [G.dr["vsw"]], [VS], chan="nsav")
        P.dma("sp", VW[:, :, 0:64], vsrc[:, :, 256 + g * 64:256 + (g + 1) * 64], [G.dr["vsw"]], [VW], chan="nsav")
        P.dma("sp", qterm[:, :, :], G.d["qterm"][4 * g:4 * g + 4, :, :].rearrange("r p n -> p r n"), [], [qterm], chan="nsaqt")
        for r in range(4):
            P.dma("sp", Q[r][0:64, :], G.d["qTn"][4 * g + r, :, :], [G.dr["qTn"]], [Q[r]], chan=("nsaq", r))
            P.dma("sp", Q[r][64:65, :], G.d["qaug"][4 * g + r, :, :], [], [Q[r]], chan=("nsaq", r))
        def cmp_items(T):
            ts = slice(T * NT, (T + 1) * NT)
            out = []
            for r in range(4):
                h = 4 * g + r
                q = Q[r]
                chunks = [0] if T <= 3 else [0, 1]
                o = obc

                def mk_emit(q):
                    def emit_cmp(c, b):
                        dl = T - 4 * c
                        P.op("pe", lambda e: e.matmul(b[:, :], lhsT=KC[:, c * 128:(c + 1) * 128], rhs=q[:, ts], start=True,
                                                      stop=(dl > 4)), [KC, q], [b])
                        if dl <= 4:
                            P.op("pe", lambda e: e.matmul(b[:, :], lhsT=G.ident[:, :], rhs=masks[:, 8 + dl, :], start=False,
                                                          stop=True), [G.ident, masks], [b])
                    return emit_cmp

                def mk_post(c, nlast):
                    def post(ee):
                        for s in range(4):
                            P.op("pe", lambda e: e.matmul(ub[:, s * 65:(s + 1) * 65], lhsT=ee[:, s * 128:(s + 1) * 128],
                                                          rhs=ov2[:, c, :], start=(c == 0 and s == 0), stop=nlast,
                                                          skip_group_check=True), [ov2, ee], [ub])
                    return post

                def mk_fin(o, h, r):
                    def fin():
                        if r == 0:
                            P.dma("sp", addt[:, :, :], G.d["addtab"][:, 4 * T:4 * T + 4, :], [], [addt], chan="addt")
                        finalize(o, 0, h, T, r, True)
                        u3 = ub[:, 0:260].rearrange("p (a n) -> p a n", a=4)
                        P.op("dve", lambda e: e.tensor_scalar(out=zr[:, :, :], in0=u3[:, :, 64:65], scalar1=1e-30, scalar2=None,
                                                              op0=ALU.max), [ub], [zr])
                        P.op("dve", lambda e: e.reciprocal(out=zr[:, :, :], in_=zr[:, :, :]), [zr], [zr])
                        if r == 0:
                            P.op("dve", lambda e: e.tensor_tensor(out=imp[:, :, :], in0=u3[:, :, 0:64],
                                                                  in1=zr[:, :, 0:1].to_broadcast([128, 4, 64]), op=ALU.mult),
                                 [ub, zr], [imp])
                        else:
                            P.op("dve", lambda e: e.tensor_tensor(out=itmp[:, :, :], in0=u3[:, :, 0:64],
                                                                  in1=zr[:, :, 0:1].to_broadcast([128, 4, 64]), op=ALU.mult),
                                 [ub, zr], [itmp])
                            P.op("pool", lambda e: e.tensor_tensor(out=imp[:, :, :], in0=imp[:, :, :], in1=itmp[:, :, :],
                                                                   op=ALU.add), [imp, itmp], [imp])
                        if r == 3:
                            topk(T)
                    return fin

                es = mk_emit(q)
                fn = mk_fin(o, h, r)
                for n, c in enumerate(chunks):
                    out.append(dict(emit_s=es, c=c, bc=512 + h * 12 + (T if c == 0 else 8 + (T - 4)), o=o, v=VC,
                                    first=(n == 0), last=(n == len(chunks) - 1), fin=fn, post=mk_post(c, n == len(chunks) - 1)))
            return out

        def load_gates(T):
            ts = slice(T * NT, (T + 1) * NT)
            gsb = gbs[PAR[T]]
            for br in range(3):
                for r in range(4):
                    row = br * 16 + 4 * g + r
                    P.dma("sp", gsb[0:64, br * 4 + r, :], G.d["gT"][row:row + 1, ts].partition_broadcast(64),
                          [G.dr["gT"]], [gsb], chan=("gbs", PAR[T]), acc=True)

        def topk(T):
            ts = slice(T * NT, (T + 1) * NT)
            if debug:
                P.dma("pool", G.d["dbgimp"][g, :, 4 * T:4 * T + 4, :], imp[:, :, :], [imp], [G.dr["dbgimp"]], chan="dbg")
            for s in range(4):
                P.op("dve", lambda e: e.tensor_tensor(out=vv[:, :], in0=imp[:, s, :], in1=addt[:, s, :], op=ALU.add),
                     [imp, addt], [vv])
                P.op("dve", lambda e: e.max(out=m8[:, 0:8], in_=vv[:, :]), [vv], [m8])
                P.op("dve", lambda e: e.match_replace(out=wk[:, :], in_to_replace=m8[:, 0:8], in_values=vv[:, :],
                                                      imm_value=-3.0e38), [m8, vv], [wk])
                P.op("dve", lambda e: e.max(out=m8[:, 8:16], in_=wk[:, :]), [wk], [m8])
                P.op("dve", lambda e: e.tensor_scalar(out=ngf[:, :], in0=vv[:, :], scalar1=m8[:, 15:16], scalar2=30000.0,
                                                      op0=ALU.is_ge, op1=ALU.mult), [vv, m8], [ngf])
                ngx = ngs[s]
                P.op("dve", lambda e: e.tensor_scalar(out=ngx[:, :], in0=ngf[:, :], scalar1=-30000.0, scalar2=None,
                                                      op0=ALU.add), [ngf], [ngx])

        def topk2(T):
            ts = slice(T * NT, (T + 1) * NT)
            for s in range(4):
                ngx = ngs[s]
                P.op("pe", lambda e: e.matmul(tb[0:64, s * 128:(s + 1) * 128], lhsT=ngx[:, :], rhs=G.ident[:, :], start=True,
                                              stop=True), [ngx, G.ident], [tb])
            qs = Qs[PAR[T]]
            for r in range(4):
                P.op("dve", lambda e: e.tensor_tensor(out=qs[r][64:128, :], in0=tb[0:64, :], in1=qterm[0:64, r, :], op=ALU.add),
                     [tb, qterm], [qs[r]])
                P.op("pool", lambda e: e.tensor_copy(out=qs[r][0:64, :], in_=Q[r][0:64, ts]), [Q[r]], [qs[r]])

        def sw_items(T, dep):
            ts = slice(T * NT, (T + 1) * NT)
            qs = Qs[PAR[T]]
            items = []
            specs = []
            for r in range(4):
                h = 4 * g + r

                def mk_slc(qq):
                    def emit_slc(c, b):
                        diag = c >= 4 * T
                        P.op("pe", lambda e: e.matmul(b[:, :], lhsT=KS[:, c * 128:(c + 1) * 128], rhs=qq[:, :], start=True,
                                                      stop=not diag), [KS, qq], [b])
                        if diag:
                            P.op("pe", lambda e: e.matmul(b[:, :], lhsT=G.ident[:, :], rhs=masks[:, c - 4 * T, :], start=False,
                                                          stop=True), [G.ident, masks], [b])
                    return emit_slc

                def mk_win(q):
                    def emit_win(c, b):
                        mi = (c - 4 * T) if c >= 4 * T else 4 + (c - (4 * T - 4))
                        P.op("pe", lambda e: e.matmul(b[:, :], lhsT=KW[:, c * 128:(c + 1) * 128], rhs=q[:, ts], start=True,
                                                      stop=False), [KW, q], [b])
                        P.op("pe", lambda e: e.matmul(b[:, :], lhsT=G.ident[:, :], rhs=masks[:, mi, :], start=False, stop=True),
                             [G.ident, masks], [b])
                    return emit_win

                def mk_fin(o, br, h, r, store):
                    def fin():
                        finalize(o, br, h, T, r, False)
                        if store:
                            y = yo[cnt["y"] % 2]
                            a = acc[PAR[T]][r]
                            P.op("pool", lambda e: e.tensor_copy(out=y[0:64, :], in_=a[0:64, :]), [a], [y])
                            P.dma("pool", G.d["ynsaT"][h * 64:(h + 1) * 64, ts], y[0:64, :], [y], [G.dr["ynsaT"]],
                                  chan=("nsay", cnt["y"] % 2))
                            cnt["y"] += 1
                    return fin

                specs.append((2, r, h, VW, mk_win(Q[r]), mk_fin, list(range(max(0, 4 * T - 4), 4 * T + 4))))
                specs.append((1, r, h, VS, mk_slc(qs[r]), mk_fin, list(range(4 * T + 4))))
            for br, r, h, vt, es, mkf, cl in sorted(specs, key=lambda s: (-s[0], s[1])):
                o = obk[cnt["o"] % 2]
                cnt["o"] += 1
                fn = mkf(o, br, h, r, br == 1)
                for n, c in enumerate(cl):
                    items.append(dict(emit_s=es, c=c, bc=h * 32 + (c - 4 * T + 28), o=o, v=vt, first=(n == 0),
                                      last=(n == len(cl) - 1), fin=fn, post=None, dep=None, slc_tile=(T if br == 1 else None)))
            return items

        DEFER = 10
        load_gates(TORDER[0])
        seq = cmp_items(TORDER[0])
        pend = (len(seq) - 1, TORDER[0])
        for ti_, T in enumerate(TORDER):
            TN = TORDER[ti_ + 1] if ti_ + 1 < NTT else None
            sw = sw_items(T, None)
            if TN is not None:
                cm = cmp_items(TN)
                groups = []
                for it in cm:
                    if it["first"]:
                        groups.append([it])
                    else:
                        groups[-1].append(it)
                step = max(1, (len(sw) * 3) // (len(groups) * 5 + 1))
                pos = step
                merged = []
                gi = 0
                for n, it in enumerate(sw):
                    merged.append(it)
                    if gi < len(groups) and n + 1 == pos:
                        merged.extend(groups[gi])
                        gi += 1
                        pos += step
                while gi < len(groups):
                    merged.extend(groups[gi])
                    gi += 1
            else:
                cm = None
                merged = sw
            start = len(seq)
            seq += merged
            if TN is not None:
                seq[start].setdefault("pre", []).append((lambda TT: (lambda: load_gates(TT)))(TN))
            li, tt = pend
            hi_ = min(li + DEFER, len(seq) - 1)
            seq[hi_].setdefault("pre", []).append((lambda TT: (lambda: topk2(TT)))(tt))
            for it in seq[start:]:
                if it.get("slc_tile") == tt:
                    it["dep"] = seq[hi_]
            if cm is not None:
                pend = (max(i for i, it in enumerate(seq) if it is cm[-1]), TN)
        run_stream(seq)
    C.close()


def build_program(nphases=99, debug=False):
    nc = bass.Bass("TRN2", target_bir_lowering=False)
    P = Prog(nc)
    G = Common()
    G.d = {}
    G.dr = {}

    def dram(name, shape, dt, kind="Internal"):
        if debug and kind == "Internal":
            kind = "ExternalOutput"
        t = nc.dram_tensor(name, list(shape), dt, kind=kind)
        G.d[name] = t.ap()
        G.dr[name] = Buf(t, multi=True)

    dram("xT", [D, S], F32, "ExternalInput")
    dram("cstv", [128, 128], F32, "ExternalInput")
    dram("ropecs", [128, S], F32, "ExternalInput")
    dram("a_w_in", [D, 2624], F32, "ExternalInput")
    dram("a_w_q", [768, 1024], F32, "ExternalInput")
    dram("a_w_kv", [256, 1024], F32, "ExternalInput")
    dram("ycatT", [D, S], BF16)
    dram("qT", [8, 96, S], BF16)
    dram("kT", [8, 96, S], BF16)
    dram("vtok", [S, 512], BF16)
    dram("cmask", [4, 128, NT], BF16, "ExternalInput")
    dram("identv", [128, 128], BF16, "ExternalInput")
    dram("a_w_out", [D, D], F32, "ExternalInput")
    dram("w1_0", [D, 4096], F32, "ExternalInput")
    dram("w2_0", [4096, D], F32, "ExternalInput")
    dram("x1T", [D, S], F32)
    dram("h2T", [D, S], BF16)
    dram("x2T", [D, S], F32)
    dram("h3T", [D, S], BF16)
    dram("c_w_in", [D, 2608], F32, "ExternalInput")
    dram("qTn", [16, 64, S], BF16)
    dram("kvT", [4, 4, 64, S], BF16)
    dram("gT", [48, S], BF16)
    dram("vsw", [S, 512], BF16)
    for sfx in ("k", "v"):
        dram("cw1" + sfx, [64, 2048], F32, "ExternalInput")
        dram("cw2" + sfx, [64, 64], F32, "ExternalInput")
        dram("cpe" + sfx, [64, 32], F32, "ExternalInput")
    dram("qaug", [16, 1, S], BF16, "ExternalInput")
    dram("biastab", [128, 704], F32, "ExternalInput")
    dram("qterm", [16, 64, NT], F32, "ExternalInput")
    dram("wmask", [4, 128, NT], BF16, "ExternalInput")
    dram("cmpmask", [5, 128, NT], BF16, "ExternalInput")
    dram("expand", [64, S], BF16, "ExternalInput")
    dram("selm", [48, 48 * 64], BF16, "ExternalInput")
    dram("ov2", [128, 2, 65], BF16, "ExternalInput")
    dram("addtab", [128, 32, 64], F32, "ExternalInput")
    dram("ynsaT", [D, S], BF16)
    dram("c_w_out", [D, D], F32, "ExternalInput")
    dram("w1_1", [D, 4096], F32, "ExternalInput")
    dram("w2_1", [4096, D], F32, "ExternalInput")
    dram("x3T", [D, S], F32)
    dram("h4T", [D, S], BF16)
    dram("outT", [D, S], F32, "ExternalOutput")
    if debug:
        dram("dbgacc", [3, D, S], F32)
        dram("dbgimp", [4, 128, 32, 64], F32)
    G.co = {"gpre0": 0, "gpost0": 8, "gmpre0": 16, "gmpost0": 24, "gpre1": 32, "gpost1": 40, "gmpre1": 48,
            "gmpost1": 56, "qn": 64, "kvn": 70, "convw": 72}
    with ExitStack() as gs:
        def gsb(name, shape, dt):
            return Buf(gs.enter_context(nc.sbuf_tensor(name, list(shape), dt)))
        G.cst = gsb("cst", [128, 128], F32)
        G.ones = gsb("ones", [128, 128], BF16)
        G.eps = gsb("eps", [128, 1], F32)
        G.tiny = gsb("tiny", [128, 1], F32)
        G.banks = [Buf(gs.enter_context(nc.psum_tensor("bank%d" % i, [128, 512], F32)), psum=True) for i in range(8)]

        def bank():
            P.bank_i = (P.bank_i + 1) % 8
            return G.banks[P.bank_i]
        G.bank = bank
        P.dma("sp", G.cst[:, :], G.d["cstv"][:, :], [], [G.cst], chan="cst")
        P.op("pool", lambda e: e.memset(G.ones[:, :], 1.0), [], [G.ones])
        P.op("pool", lambda e: e.memset(G.eps[:, :], EPS), [], [G.eps])
        P.op("pool", lambda e: e.memset(G.tiny[0:64, :], 0.0), [], [G.tiny])
        P.op("pool", lambda e: e.memset(G.tiny[64:128, :], 1e-30), [], [G.tiny])
        G.ident = gsb("ident", [128, 128], BF16)
        P.dma("sp", G.ident[:, :], G.d["identv"][:, :], [], [G.ident], chan="cst")
        if nphases >= 1:
            phase_front_a(P, G)
        if nphases >= 2:
            phase_mla_attn(P, G)
        if nphases >= 3:
            Cw, W1 = prefetch_w1(P, G, "w1_0") if nphases >= 4 else (None, None)
            phase_outproj(P, G, "a_w_out", "ycatT", "xT", "x1T", "h2T", "gpost0", "gmpre0")
        if nphases >= 4:
            phase_mlp(P, G, "w1_0", "w2_0", "h2T", "x1T", "x2T", "h3T", "gmpost0", "gpre1", W1=W1)
            Cw.close()
        G.KC = [gsb("KC%d" % g, [128, 256], BF16) for g in range(4)]
        G.VC = [gsb("VC%d" % g, [128, 2, 128], BF16) for g in range(4)]
        if nphases >= 5:
            phase_front_c(P, G)
        if nphases >= 6:
            phase_compress(P, G)
        if nphases >= 7:
            phase_nsa(P, G, debug)
        if nphases >= 8:
            Cw, W1 = prefetch_w1(P, G, "w1_1") if nphases >= 9 else (None, None)
            phase_outproj(P, G, "c_w_out", "ynsaT", "x2T", "x3T", "h4T", "gpost1", "gmpre1")
        if nphases >= 9:
            phase_mlp(P, G, "w1_1", "w2_1", "h4T", "x3T", "outT", None, "gmpost1", None, W1=W1)
            Cw.close()
        P.barrier()
        P.final_wait()
    print("instructions:", P.ninst, "sems:", P.nsem)
    return nc


def pack_cols(v):
    return np.ascontiguousarray(np.asarray(v, np.float32).reshape(-1, 128).T)


def host_consts(inp):
    c = {}
    cst = np.zeros((128, 128), np.float32)
    cols = [inp["norm_mix_pre"][0], inp["norm_mix_post"][0], inp["norm_mlp_pre"][0], inp["norm_mlp_post"][0],
            inp["norm_mix_pre"][1], inp["norm_mix_post"][1], inp["norm_mlp_pre"][1], inp["norm_mlp_post"][1]]
    for i, v in enumerate(cols):
        cst[:, i * 8:(i + 1) * 8] = pack_cols(v)
    cst[:, 64:70] = pack_cols(inp["a_q_norm"][0])
    cst[:, 70:72] = pack_cols(inp["a_kv_norm"][0])
    cw = np.asarray(inp["a_conv_w"][0], np.float32)
    for cc in range(4):
        for k in range(3):
            cst[:, 72 + cc * 3 + k] = cw[k, cc * 128:(cc + 1) * 128]
    c["cstv"] = cst
    half = 16
    inv = (10000.0 ** (-np.arange(half, dtype=np.float32) / half)).astype(np.float32)
    ang = np.arange(S, dtype=np.float32)[None, :] * inv[:, None]
    cs, sn = np.cos(ang).astype(np.float32), np.sin(ang).astype(np.float32)
    Cm = np.concatenate([cs, cs], 0)
    Sm = np.concatenate([-sn, sn], 0)
    c["ropecs"] = np.ascontiguousarray(np.concatenate([Cm, Sm, Cm, Sm], 0))
    w = np.asarray(inp["a_w_in"][0], np.float32)
    c["a_w_in"] = np.ascontiguousarray(np.concatenate([w, w[:, 2576:2592], w[:, 2560:2576]], 1))
    wq = np.asarray(inp["a_w_q_up"][0], np.float32)
    cols = []
    for h in range(8):
        b = h * 96
        cols += [wq[:, b:b + 96], wq[:, b + 80:b + 96], wq[:, b + 64:b + 80]]
    c["a_w_q"] = np.ascontiguousarray(np.concatenate(cols, 1))
    wkv = np.asarray(inp["a_w_kv_up"][0], np.float32).reshape(256, 8, 128)
    c["a_w_kv"] = np.ascontiguousarray(np.concatenate([wkv[:, :, :64].reshape(256, 512), wkv[:, :, 64:].reshape(256, 512)], 1))
    NEGM = -30000.0
    cm = np.zeros((4, 128, NT), np.float32)
    ii = np.arange(128)[:, None]
    jj = np.arange(NT)[None, :]
    for off in range(4):
        cm[off] = np.where(ii + 128 * off <= jj, 0.0, NEGM)
    c["cmask"] = cm.astype(ml_dtypes.bfloat16)
    c["identv"] = np.eye(128, dtype=np.float32).astype(ml_dtypes.bfloat16)
    c["a_w_out"] = np.ascontiguousarray(inp["a_w_out"][0], np.float32)
    c["w1_0"] = np.ascontiguousarray(inp["mlp_w1"][0], np.float32)
    c["w2_0"] = np.ascontiguousarray(inp["mlp_w2"][0], np.float32)
    bf = ml_dtypes.bfloat16
    cw = np.asarray(inp["c_w_in"][0], np.float32)
    c["c_w_in"] = np.ascontiguousarray(np.concatenate(
        [cw[:, 0:1536], cw[:, 1536:1792], cw[:, 2048:2304], cw[:, 1792:2048], cw[:, 2304:2560], cw[:, 2560:2608]], 1))
    for sfx in ("k", "v"):
        w1 = np.asarray(inp["c_cmp_w1_" + sfx][0], np.float32)
        c["cw1" + sfx] = np.ascontiguousarray(w1.transpose(1, 0, 2).reshape(64, 2048))
        c["cw2" + sfx] = np.ascontiguousarray(inp["c_cmp_w2_" + sfx][0], np.float32)
        c["cpe" + sfx] = np.ascontiguousarray(np.asarray(inp["c_cmp_pe_" + sfx][0], np.float32).T)

    slopes = [float(np.float32(2.0 ** (-8.0 * (h + 1) / 16))) for h in range(16)]
    qaug = np.zeros((16, 1, S), np.float32)
    qterm = np.zeros((16, 64, NT), np.float32)
    btab = np.zeros((128, 704), np.float64)
    pp = np.arange(128, dtype=np.float64)
    for h in range(16):
        sp = 8.0 * slopes[h]
        qaug[h, 0] = -sp * (np.arange(S) % NT)
        qterm[h] = (-sp * np.arange(NT))[None, :]
        for dlt in range(-28, 4):
            btab[:, h * 32 + dlt + 28] = slopes[h] * (pp + 128.0 * dlt)
        for T in range(8):
            btab[:, 512 + h * 12 + T] = slopes[h] * (16.0 * pp + 15.5 - 512.0 * T)
        for T in range(4, 8):
            btab[:, 512 + h * 12 + 8 + (T - 4)] = slopes[h] * (16.0 * pp + 2048.0 + 15.5 - 512.0 * T)
    c["qaug"] = qaug.astype(bf)
    c["qterm"] = qterm
    c["biastab"] = btab.astype(np.float32)
    ii = np.arange(128)[:, None]
    jj2 = np.arange(NT)[None, :]
    wm = np.zeros((4, 128, NT), np.float32)
    cpm = np.zeros((5, 128, NT), np.float32)
    for o in range(4):
        wm[o] = np.where(ii > jj2 - 128 * o, 0.0, NEGM)
    for o in range(5):
        cpm[o] = np.where(16 * ii + 31 <= 512 * o + jj2, 0.0, NEGM)
    c["wmask"] = wm.astype(bf)
    c["cmpmask"] = cpm.astype(bf)
    c["expand"] = (np.arange(64)[:, None] == (np.arange(S)[None, :] // 64)).astype(np.float32).astype(bf)
    selm = np.zeros((48, 48, 64), np.float32)
    for k in range(48):
        selm[k, k, :] = 1.0
    c["selm"] = selm.reshape(48, 48 * 64).astype(bf)
    starts = np.arange(255) * 16
    ss = np.arange(64) * 64
    ov = np.clip(np.minimum(starts[:, None] + 32, ss[None, :] + 64) - np.maximum(starts[:, None], ss[None, :]), 0, None) / 32.0
    ov2 = np.zeros((256, 65), np.float32)
    ov2[:255, :64] = ov
    ov2[:255, 64] = 1.0
    c["ov2"] = np.ascontiguousarray(ov2.reshape(2, 128, 65).transpose(1, 0, 2)).astype(bf)
    t = np.arange(S)
    cur = t // 64
    jb = np.arange(64)[None, :]
    forced = (jb == 0) | (jb == cur[:, None]) | (jb == cur[:, None] - 1)
    add = np.where(jb > cur[:, None], -1.0e9, 1.0e4 * forced).astype(np.float32)
    c["addtab"] = np.ascontiguousarray(add.reshape(32, 128, 64).transpose(1, 0, 2))
    c["c_w_out"] = np.ascontiguousarray(inp["c_w_out"][0], np.float32)
    c["w1_1"] = np.ascontiguousarray(inp["mlp_w1"][1], np.float32)
    c["w2_1"] = np.ascontiguousarray(inp["mlp_w2"][1], np.float32)
    return c


def kernel(**inp):
    x = np.asarray(inp["x"], np.float32)
    c = host_consts(inp)
    nc = build_program()
    in_maps = []
    for b in range(8):
        m = dict(c)
        m["xT"] = np.ascontiguousarray(x[b].T)
        in_maps.append(m)
    res = run_bass_kernel_spmd(nc, in_maps, core_ids=list(range(8)))
    out = np.stack([np.ascontiguousarray(r["outT"].T) for r in res.results], 0)
    return out.astype(np.float32)
```

```python
import numpy as np
import ml_dtypes
from contextlib import ExitStack
import concourse.bass as bass
import concourse.mybir as mybir
from concourse.bass_utils import run_bass_kernel_spmd

F32 = mybir.dt.float32
BF16 = mybir.dt.bfloat16
ALU = mybir.AluOpType
AF = mybir.ActivationFunctionType

S = 4096
D = 1024
EPS = 1e-6
NT = 512
NTT = S // NT


class Trk:
    __slots__ = ("w", "r", "multi")

    def __init__(self, multi=False):
        self.w = {}
        self.r = {}
        self.multi = multi


class Buf:
    def __init__(self, t, multi=False, psum=False):
        self.t = t
        self.k = Trk(multi)
        self.psum = psum

    def __getitem__(self, idx):
        return self.t[idx]


class Prog:
    def __init__(self, nc):
        self.nc = nc
        self.e = {"pe": nc.tensor, "act": nc.scalar, "dve": nc.vector, "pool": nc.gpsimd, "sp": nc.sync}
        self.esem = {}
        self.dsem = {}
        self.seen = {k: {} for k in self.e}
        self.nsem = 0
        self.semtotal = {}
        self.ninst = 0
        self.bank_i = 0

    def _newsem(self, name):
        self.nsem += 1
        return (self.nsem, self.nc.alloc_semaphore("%s_%d" % (name, self.nsem)))

    def _wait(self, eng, deps):
        for key, (sem, val, src) in deps.items():
            if src == "pe" and eng == "pe":
                continue
            if src == "dma":
                val = max(val, self.semtotal[key])
            if self.seen[eng].get(key, 0) >= val:
                continue
            self.e[eng].wait_ge(sem, val)
            self.ninst += 1
            self.seen[eng][key] = val

    @staticmethod
    def _add(deps, d):
        for k, t in d.items():
            if k not in deps or deps[k][1] < t[1]:
                deps[k] = t

    def _deps(self, reads, writes, acc=False):
        deps = {}
        for b in reads:
            self._add(deps, b.k.w)
            if getattr(b, "psum", False):
                self._add(deps, b.k.r)
        for b in writes:
            self._add(deps, b.k.r)
            if not (b.k.multi or acc):
                self._add(deps, b.k.w)
        return deps

    def _commit(self, key, tok, reads, writes, acc=False):
        for b in reads:
            b.k.r[key] = tok
        for b in writes:
            if b.k.multi or acc:
                b.k.w[key] = tok
            else:
                b.k.w = {key: tok}
            b.k.r = {}

    def op(self, eng, fn, reads=(), writes=()):
        self._wait(eng, self._deps(reads, writes))
        ins = fn(self.e[eng])
        st = self.esem.get(eng)
        if st is None or st[2] >= 30000:
            k, sem = self._newsem("e" + eng)
            st = [k, sem, 0]
            self.esem[eng] = st
        st[2] += 1
        ins.then_inc(st[1], 1)
        self.ninst += 1
        self._commit(st[0], (st[1], st[2], eng), reads, writes)

    def dma(self, q, out, in_, reads, writes, chan, acc=False):
        self._wait(q, self._deps(reads, writes, acc))
        st = self.dsem.get(chan)
        if st is None or st[2] >= 30000:
            k, sem = self._newsem("d")
            st = [k, sem, 0]
            self.dsem[chan] = st
        ins = self.e[q].dma_start(out=out, in_=in_)
        st[2] += 16
        self.semtotal[st[0]] = st[2]
        ins.then_inc(st[1], 16)
        self.ninst += 1
        self._commit(st[0], (st[1], st[2], "dma"), reads, writes, acc)

    def barrier(self):
        toks = {}
        for eng, st in self.esem.items():
            toks[st[0]] = (st[1], st[2], eng)
        for ch, st in self.dsem.items():
            toks[st[0]] = (st[1], st[2], "dma")
        for eng in self.e:
            self._wait(eng, toks)

    def final_wait(self):
        toks = {}
        for ch, st in self.dsem.items():
            toks[st[0]] = (st[1], st[2], "dma")
        self._wait("sp", toks)


class Ctx:
    uid = 0

    def __init__(self, P):
        self.P = P
        self.nc = P.nc
        self.st = ExitStack()
        self.n = 0

    def sb(self, shape, dt, name="t"):
        Ctx.uid += 1
        t = self.st.enter_context(self.nc.sbuf_tensor("%s_%d" % (name, Ctx.uid), list(shape), dt))
        return Buf(t)

    def close(self):
        self.P.barrier()
        self.st.close()


def load_w_bf16(P, C, dram_ap, dst, kc, ncols, stage=None, engs=None):
    w3 = dram_ap.rearrange("(k p) c -> p k c", p=128)
    for k in range(kc):
        if stage is not None:
            P.dma("pool", dst[:, k, :], w3[:, k, :], [], [stage[k]], chan=("w", id(dst), k % 4))
        else:
            P.dma("pool", dst[:, k, :], w3[:, k, :], [], [dst], chan=("w", id(dst)), acc=True)


class Common:
    pass


def rstd_from_sumsq(P, G, bank, rstd, n_feat):
    P.op("act", lambda e: e.activation(out=rstd[:, :], in_=bank[:, :], func=AF.Sqrt, bias=G.eps[:, 0:1],
                                        scale=1.0 / n_feat), [bank, G.eps], [rstd])
    P.op("dve", lambda e: e.reciprocal(out=rstd[:, :], in_=rstd[:, :]), [rstd], [rstd])


def sumsq_bcast(P, G, src, sq, kc, bank, nparts=128):
    P.op("act", lambda e: e.activation(out=sq[:, 0:kc, :], in_=src[:, 0:kc, :], func=AF.Square), [src], [sq])
    for k in range(kc):
        P.op("pe", lambda e, k=k: e.matmul(bank[:, :], lhsT=G.ones[:, :], rhs=sq[:, k, :], start=(k == 0),
                                            stop=(k == kc - 1)), [sq, G.ones], [bank])


def phase_front_a(P, G):
    nc = P.nc
    C = Ctx(P)
    NIN = 2624
    win = C.sb([128, 8, NIN], BF16, "win")
    wq = C.sb([128, 6, 1024], BF16, "wq")
    wkv = C.sb([128, 2, 1024], BF16, "wkv")
    stage = None
    load_w_bf16(P, C, G.d["a_w_in"], win, 8, NIN, stage)
    load_w_bf16(P, C, G.d["a_w_q"], wq, 6, 1024, stage)
    load_w_bf16(P, C, G.d["a_w_kv"], wkv, 2, 1024, stage)
    xts = [C.sb([128, 8, NT], F32, "xt") for _ in range(2)]
    sqxs = [C.sb([128, 8, NT], BF16, "sqx") for _ in range(2)]
    hTs = [C.sb([128, 8, NT], BF16, "hT") for _ in range(2)]
    rstdxs = [C.sb([128, NT], F32, "rstdx") for _ in range(2)]
    sq = C.sb([128, 8, NT], BF16, "sq")
    rstd = C.sb([128, NT], F32, "rstd")
    rope = C.sb([128, NT], F32, "rope")
    u = [[C.sb([128, NT + 2], F32, "u") for _ in range(2)] for _ in range(4)]
    hvss = [C.sb([128, NT], F32, "hvs") for _ in range(2)]
    c1s = [C.sb([128, NT], F32, "c1") for _ in range(2)]
    yc = [C.sb([128, NT], BF16, "yc") for _ in range(2)]
    cq = C.sb([128, 6, NT], F32, "cq")
    cqn = C.sb([128, 6, NT], BF16, "cqn")
    ckv = C.sb([128, 2, NT], F32, "ckv")
    ckvn = C.sb([128, 2, NT], BF16, "ckvn")
    t1s = [C.sb([128, NT], F32, "t1") for _ in range(2)]
    t2s = [C.sb([128, NT], F32, "t2") for _ in range(2)]
    sqkv = C.sb([128, 2, NT], BF16, "sqkv")
    rstdkv = C.sb([128, NT], F32, "rstdkv")
    kr = C.sb([32, NT], BF16, "kr")
    qt = [C.sb([128, NT], BF16, "qt") for _ in range(2)]
    kn = [C.sb([128, NT], BF16, "kn") for _ in range(2)]
    vt = [C.sb([128, NT], BF16, "vt") for _ in range(2)]
    cst = G.cst
    xT3 = G.d["xT"].rearrange("(k p) t -> p k t", p=128)
    for cc in range(4):
        P.op("pool", lambda e, cc=cc: e.memset(u[cc][0][:, 0:2], 0.0), [], [u[cc][0]])
    ev = [0]

    def evac(out_ap, bank, in_ap, reads, writes):
        ev[0] += 1
        if ev[0] % 2:
            P.op("act", lambda e: e.activation(out=out_ap, in_=in_ap, func=AF.Copy), reads, writes)
        else:
            P.op("dve", lambda e: e.tensor_copy(out=out_ap, in_=in_ap), reads, writes)

    cur = {}

    def mm_chunk(bank, col0, m, rows=slice(0, 128)):
        hT = cur["hT"]
        for k in range(8):
            P.op("pe", lambda e, k=k: e.matmul(bank[0:m, :], lhsT=win[:, k, col0:col0 + m], rhs=hT[:, k, :],
                                                start=(k == 0), stop=(k == 7)), [win, hT], [bank])

    def load_x(ti):
        ts = slice(ti * NT, (ti + 1) * NT)
        xt = xts[ti % 2]
        P.dma("sp", xt[:, :, :], xT3[:, :, ts], [G.dr["xT"]], [xt], chan=("xt", ti % 2))

    def prenorm(ti):
        xt, sqx, hT, rstdx = xts[ti % 2], sqxs[ti % 2], hTs[ti % 2], rstdxs[ti % 2]
        bA = G.bank()
        sumsq_bcast(P, G, xt, sqx, 8, bA)
        rstd_from_sumsq(P, G, bA, rstdx, 1024)
        for k in range(8):
            P.op("dve", lambda e, k=k: e.scalar_tensor_tensor(out=hT[:, k, :], in0=xt[:, k, :],
                                                               scalar=cst[:, G.co["gpre0"] + k:G.co["gpre0"] + k + 1],
                                                               in1=rstdx[:, :], op0=ALU.mult, op1=ALU.mult),
                 [xt, rstdx, cst], [hT])

    load_x(0)
    prenorm(0)
    for ti in range(NTT):
        ts = slice(ti * NT, (ti + 1) * NT)
        cur["hT"] = hTs[ti % 2]
        if ti + 1 < NTT:
            load_x(ti + 1)
        P.dma("sp", rope[:, :], G.d["ropecs"][:, ts], [], [rope], chan="rope")
        for cc in range(4):
            b0, b1, b2 = G.bank(), G.bank(), G.bank()
            mm_chunk(b0, cc * 128, 128)
            mm_chunk(b1, 512 + cc * 128, 128)
            mm_chunk(b2, 1024 + cc * 128, 128)
            uc, un = u[cc][ti % 2], u[cc][(ti + 1) % 2]
            hvs, c1 = hvss[cc % 2], c1s[cc % 2]
            P.op("act", lambda e: e.activation(out=hvs[:, :], in_=b2[:, :], func=AF.Copy), [b2], [hvs])
            P.op("dve", lambda e: e.tensor_tensor(out=uc[:, 2:NT + 2], in0=b1[:, :], in1=hvs[:, :], op=ALU.mult),
                 [b1, hvs], [uc])
            cw = G.co["convw"] + cc * 3
            P.op("dve", lambda e: e.tensor_scalar(out=c1[:, :], in0=uc[:, 0:NT], scalar1=cst[:, cw:cw + 1],
                                                  scalar2=None, op0=ALU.mult), [uc, cst], [c1])
            P.op("dve", lambda e: e.scalar_tensor_tensor(out=c1[:, :], in0=uc[:, 1:NT + 1], scalar=cst[:, cw + 1:cw + 2],
                                                         in1=c1[:, :], op0=ALU.mult, op1=ALU.add), [uc, cst, c1], [c1])
            P.op("dve", lambda e: e.scalar_tensor_tensor(out=c1[:, :], in0=uc[:, 2:NT + 2], scalar=cst[:, cw + 2:cw + 3],
                                                         in1=c1[:, :], op0=ALU.mult, op1=ALU.add), [uc, cst, c1], [c1])
            y = yc[cc % 2]
            P.op("dve", lambda e: e.tensor_tensor(out=y[:, :], in0=b0[:, :], in1=c1[:, :], op=ALU.mult), [b0, c1], [y])
            P.op("pool", lambda e: e.tensor_copy(out=un[:, 0:2], in_=uc[:, NT:NT + 2]), [uc], [un])
            P.dma("pool", G.d["ycatT"][cc * 128:(cc + 1) * 128, ts], y[:, :], [y], [G.dr["ycatT"]], chan=("yc", cc % 2))
        if ti + 1 < NTT:
            prenorm(ti + 1)
        for j in range(2):
            b = G.bank()
            mm_chunk(b, 2304 + j * 128, 128)
            evac(ckv[:, j, :], b, b[:, :], [b], [ckv])
        for j in range(6):
            b = G.bank()
            mm_chunk(b, 1536 + j * 128, 128)
            evac(cq[:, j, :], b, b[:, :], [b], [cq])
        bk = G.bank()
        sumsq_bcast(P, G, ckv, sqkv, 2, bk)
        rstd_from_sumsq(P, G, bk, rstdkv, 256)
        for j in range(2):
            P.op("dve", lambda e, j=j: e.scalar_tensor_tensor(out=ckvn[:, j, :], in0=ckv[:, j, :],
                                                               scalar=cst[:, G.co["kvn"] + j:G.co["kvn"] + j + 1],
                                                               in1=rstdkv[:, :], op0=ALU.mult, op1=ALU.mult),
                 [ckv, rstdkv, cst], [ckvn])
        bq = G.bank()
        sumsq_bcast(P, G, cq, sq, 6, bq)
        rstd_from_sumsq(P, G, bq, rstd, 768)
        for j in range(6):
            P.op("dve", lambda e, j=j: e.scalar_tensor_tensor(out=cqn[:, j, :], in0=cq[:, j, :],
                                                               scalar=cst[:, G.co["qn"] + j:G.co["qn"] + j + 1],
                                                               in1=rstd[:, :], op0=ALU.mult, op1=ALU.mult),
                 [cq, rstd, cst], [cqn])
        t1, t2 = t1s[0], t2s[0]
        b = G.bank()
        mm_chunk(b, 2560, 64)
        P.op("dve", lambda e: e.tensor_tensor(out=t1[0:32, :], in0=b[0:32, :], in1=rope[0:32, :], op=ALU.mult),
             [b, rope], [t1])
        P.op("dve", lambda e: e.tensor_tensor(out=t2[0:32, :], in0=b[32:64, :], in1=rope[32:64, :], op=ALU.mult),
             [b, rope], [t2])
        P.op("dve", lambda e: e.tensor_tensor(out=kr[0:32, :], in0=t1[0:32, :], in1=t2[0:32, :], op=ALU.add),
             [t1, t2], [kr])
        for h in range(8):
            P.dma("pool", G.d["kT"][h, 64:96, ts], kr[0:32, :], [kr], [G.dr["kT"]], chan="kr")
        for hp in range(4):
            b = G.bank()
            for j in range(2):
                P.op("pe", lambda e, j=j: e.matmul(b[:, :], lhsT=wkv[:, j, hp * 128:(hp + 1) * 128], rhs=ckvn[:, j, :],
                                                    start=(j == 0), stop=(j == 1)), [wkv, ckvn], [b])
            kk = kn[hp % 2]
            evac(kk[:, :], b, b[:, :], [b], [kk])
            P.dma("pool", G.d["kT"][2 * hp, 0:64, ts], kk[0:64, :], [kk], [G.dr["kT"]], chan=("kn", hp % 2))
            P.dma("pool", G.d["kT"][2 * hp + 1, 0:64, ts], kk[64:128, :], [kk], [G.dr["kT"]], chan=("kn", hp % 2))
        for tb in range(4):
            b = G.bank()
            for j in range(2):
                P.op("pe", lambda e, j=j: e.matmul(b[:, :], lhsT=ckvn[:, j, tb * 128:(tb + 1) * 128],
                                                    rhs=wkv[:, j, 512:1024], start=(j == 0), stop=(j == 1)),
                     [wkv, ckvn], [b])
            vv = vt[tb % 2]
            evac(vv[:, :], b, b[:, :], [b], [vv])
            r0 = ti * NT + tb * 128
            P.dma("pool", G.d["vtok"][r0:r0 + 128, :], vv[:, :], [vv], [G.dr["vtok"]], chan=("vt", tb % 2))
        for h in range(8):
            b = G.bank()
            for j in range(6):
                P.op("pe", lambda e, j=j: e.matmul(b[:, :], lhsT=wq[:, j, h * 128:(h + 1) * 128], rhs=cqn[:, j, :],
                                                    start=(j == 0), stop=(j == 5)), [wq, cqn], [b])
            q = qt[h % 2]
            t1, t2 = t1s[h % 2], t2s[h % 2]
            P.op("act", lambda e: e.activation(out=q[0:64, :], in_=b[0:64, :], func=AF.Copy), [b], [q])
            P.op("dve", lambda e: e.tensor_tensor(out=t1[64:96, :], in0=b[64:96, :], in1=rope[64:96, :], op=ALU.mult),
                 [b, rope], [t1])
            P.op("dve", lambda e: e.tensor_tensor(out=t2[64:96, :], in0=b[96:128, :], in1=rope[96:128, :], op=ALU.mult),
                 [b, rope], [t2])
            P.op("dve", lambda e: e.tensor_tensor(out=q[64:96, :], in0=t1[64:96, :], in1=t2[64:96, :], op=ALU.add),
                 [t1, t2], [q])
            P.dma("pool", G.d["qT"][h, :, ts], q[0:96, :], [q], [G.dr["qT"]], chan=("qt", h % 2))
    C.close()


def phase_mla_attn(P, G):
    C = Ctx(P)
    scale = 96 ** -0.5
    QT = [C.sb([96, S], BF16, "QT") for _ in range(2)]
    KT = [C.sb([96, S], BF16, "KT") for _ in range(2)]
    V = [C.sb([128, 32, 128], BF16, "V") for _ in range(2)]
    E = [C.sb([128, NT], BF16, "E") for _ in range(3)]
    cm = C.sb([128, 4, NT], BF16, "cm")
    rz = C.sb([64, NT], F32, "rz")
    yo = [C.sb([64, NT], BF16, "yo") for _ in range(2)]
    P.dma("sp", cm[:, :, :], G.d["cmask"].rearrange("o p n -> p o n"), [], [cm], chan="cm")
    for i in range(2):
        P.op("pool", lambda e: e.memset(V[i][:, :, 64:128], 1.0), [], [V[i]])
    sb = G.banks[0:3]
    ob = G.banks[4:6]
    vsrc = G.d["vtok"].rearrange("(c p) f -> p c f", p=128)
    its = [(T, c) for T in range(NTT) for c in range(4 * T + 4)]
    n_o = [0]

    def load(h):
        P.dma("sp", QT[h % 2][:, :], G.d["qT"][h, :, :], [G.dr["qT"]], [QT[h % 2]], chan=("QT", h % 2))
        P.dma("sp", KT[h % 2][:, :], G.d["kT"][h, :, :], [G.dr["kT"]], [KT[h % 2]], chan=("KT", h % 2))
        P.dma("sp", V[h % 2][:, :, 0:64], vsrc[:, :, h * 64:(h + 1) * 64], [G.dr["vtok"]], [V[h % 2]],
              chan=("V", h % 2))

    load(0)
    for h in range(8):
        if h + 1 < 8:
            load(h + 1)
        q, k, v = QT[h % 2], KT[h % 2], V[h % 2]

        def emit_s(n):
            T, c = its[n]
            b = sb[n % 3]
            diag = c >= 4 * T
            j0 = 128 * (c - 4 * T) if diag else 0
            P.op("pe", lambda e: e.matmul(b[:, j0:NT], lhsT=k[0:96, c * 128:(c + 1) * 128],
                                          rhs=q[0:96, T * NT + j0:(T + 1) * NT], start=True, stop=not diag), [k, q], [b])
            if diag:
                P.op("pe", lambda e: e.matmul(b[:, j0:NT], lhsT=G.ident[:, :], rhs=cm[:, c - 4 * T, j0:NT], start=False,
                                              stop=True), [G.ident, cm], [b])

        emit_s(0)
        emit_s(1)
        for n in range(len(its)):
            T, c = its[n]
            if n + 2 < len(its):
                emit_s(n + 2)
            b = sb[n % 3]
            ee = E[n % 3]
            j0 = 128 * (c - 4 * T) if c >= 4 * T else 0
            P.op("act", lambda e: e.activation(out=ee[:, j0:NT], in_=b[:, j0:NT], func=AF.Exp, scale=scale), [b], [ee])
            o = ob[T % 2]
            last = (c == 4 * T + 3)
            P.op("pe", lambda e: e.matmul(o[:, j0:NT], lhsT=v[:, c, :], rhs=ee[:, j0:NT], start=(c == 0), stop=last,
                                          skip_group_check=True), [v, ee], [o])
            if last:
                y = yo[n_o[0] % 2]
                n_o[0] += 1
                P.op("dve", lambda e: e.reciprocal(out=rz[0:64, :], in_=o[64:128, :]), [o], [rz])
                P.op("dve", lambda e: e.tensor_tensor(out=y[0:64, :], in0=o[0:64, :], in1=rz[0:64, :], op=ALU.mult),
                     [o, rz], [y])
                P.dma("pool", G.d["ycatT"][512 + h * 64:512 + (h + 1) * 64, T * NT:(T + 1) * NT], y[0:64, :], [y],
                      [G.dr["ycatT"]], chan=("yo", (n_o[0] - 1) % 2))
    C.close()


def phase_outproj(P, G, wname, yname, xin, xout, hout, gpost, gmpre):
    C = Ctx(P)
    wo = C.sb([128, 8, 1024], BF16, "wo")
    load_w_bf16(P, C, G.d[wname], wo, 8, 1024)
    yt = [C.sb([128, 8, NT], BF16, "yt") for _ in range(2)]
    xts = [C.sb([128, 8, NT], F32, "xt") for _ in range(2)]
    ms = [C.sb([128, 8, NT], F32, "m") for _ in range(2)]
    sqs = [C.sb([128, 8, NT], BF16, "sq")] * 2
    sq2s = [C.sb([128, 8, NT], BF16, "sq2")] * 2
    rstds = [C.sb([128, NT], F32, "rstd") for _ in range(2)]
    rstd2s = [C.sb([128, NT], F32, "rstd2") for _ in range(2)]
    h2s = [C.sb([128, 8, NT], BF16, "h2")] * 2
    def chunked(bufs):
        first = [Buf(bufs[0].t) for _ in range(8)]
        return [first, first if bufs[1] is bufs[0] else [Buf(bufs[1].t) for _ in range(8)]]
    xtc, mc, sqc, sq2c, h2c = chunked(xts), chunked(ms), chunked(sqs), chunked(sq2s), chunked(h2s)
    cst = G.cst
    y3 = G.d[yname].rearrange("(k p) t -> p k t", p=128)
    x3 = G.d[xin].rearrange("(k p) t -> p k t", p=128)
    xo3 = G.d[xout].rearrange("(k p) t -> p k t", p=128)
    ho3 = G.d[hout].rearrange("(k p) t -> p k t", p=128)
    mb = G.banks[0:6]
    bAs = G.banks[6]
    bBs = G.banks[7]
    nb = [0]

    def ones_mm(bank, sq, sqk, k):
        P.op("pe", lambda e: e.matmul(bank[:, :], lhsT=G.ones[:, :], rhs=sq[:, k, :], start=(k == 0), stop=(k == 7)),
             [sqk[k], G.ones], [bank])

    def rstd_ops(bank, rstd):
        P.op("act", lambda e: e.activation(out=rstd[:, :], in_=bank[:, :], func=AF.Sqrt, bias=G.eps[:, 0:1], scale=1.0 / 1024),
             [bank, G.eps], [rstd])
        P.op("dve", lambda e: e.reciprocal(out=rstd[:, :], in_=rstd[:, :]), [rstd], [rstd])

    def epi1_step(tp, k):
        xt, m, sq2, rstd = xts[tp % 2], ms[tp % 2], sq2s[tp % 2], rstds[tp % 2]
        mk, xk, s2k = mc[tp % 2][k], xtc[tp % 2][k], sq2c[tp % 2][k]
        P.op("dve", lambda e: e.scalar_tensor_tensor(out=m[:, k, :], in0=m[:, k, :],
                                                     scalar=cst[:, G.co[gpost] + k:G.co[gpost] + k + 1],
                                                     in1=rstd[:, :], op0=ALU.mult, op1=ALU.mult), [mk, rstd, cst], [mk])
        P.op("pool", lambda e: e.tensor_tensor(out=xt[:, k, :], in0=xt[:, k, :], in1=m[:, k, :], op=ALU.add), [xk, mk], [xk])
        P.op("act", lambda e: e.activation(out=sq2[:, k, :], in_=xt[:, k, :], func=AF.Square), [xk], [s2k])

    def epi2(tp):
        ts = slice(tp * NT, (tp + 1) * NT)
        xt, h2, rstd2 = xts[tp % 2], h2s[tp % 2], rstd2s[tp % 2]
        P.dma("pool", xo3[:, :, ts], xt[:, :, :], xtc[tp % 2], [G.dr[xout]], chan=("opxo", tp % 2))
        rstd_ops(bBs, rstd2)
        for k in range(8):
            P.op("dve", lambda e: e.scalar_tensor_tensor(out=h2[:, k, :], in0=xt[:, k, :],
                                                         scalar=cst[:, G.co[gmpre] + k:G.co[gmpre] + k + 1],
                                                         in1=rstd2[:, :], op0=ALU.mult, op1=ALU.mult),
                 [xtc[tp % 2][k], rstd2, cst], [h2c[tp % 2][k]])
        P.dma("pool", ho3[:, :, ts], h2[:, :, :], h2c[tp % 2], [G.dr[hout]], chan=("opho", tp % 2))

    def main(ti, prev):
        if ti is not None:
            ts = slice(ti * NT, (ti + 1) * NT)
            y, xt, m, sq = yt[ti % 2], xts[ti % 2], ms[ti % 2], sqs[ti % 2]
            P.dma("sp", y[:, :, :], y3[:, :, ts], [G.dr[yname]], [y], chan=("opy", ti % 2))
            P.dma("sp", xt[:, :, :], x3[:, :, ts], [G.dr[xin]], xtc[ti % 2], chan=("opx", ti % 2))
        if prev is not None:
            ones_mm(bAs, sqs[prev % 2], sqc[prev % 2], 7)
            rstd_ops(bAs, rstds[prev % 2])
        for oc in range(8):
            if ti is not None:
                b = mb[nb[0] % 6]
                nb[0] += 1
                for k in range(8):
                    P.op("pe", lambda e: e.matmul(b[:, :], lhsT=wo[:, k, oc * 128:(oc + 1) * 128], rhs=y[:, k, :],
                                                  start=(k == 0), stop=(k == 7)), [wo, y], [b])
                P.op("dve", lambda e: e.tensor_copy(out=m[:, oc, :], in_=b[:, :]), [b], [mc[ti % 2][oc]])
                P.op("act", lambda e: e.activation(out=sq[:, oc, :], in_=m[:, oc, :], func=AF.Square), [mc[ti % 2][oc]],
                     [sqc[ti % 2][oc]])
                if oc > 0:
                    ones_mm(bAs, sq, sqc[ti % 2], oc - 1)
            if prev is not None:
                epi1_step(prev, oc)
                if oc > 1:
                    ones_mm(bBs, sq2s[prev % 2], sq2c[prev % 2], oc - 2)
        if prev is not None:
            ones_mm(bBs, sq2s[prev % 2], sq2c[prev % 2], 6)
            ones_mm(bBs, sq2s[prev % 2], sq2c[prev % 2], 7)
            epi2(prev)

    for ti in range(NTT):
        main(ti, ti - 1 if ti > 0 else None)
    main(None, NTT - 1)
    C.close()


def prefetch_w1(P, G, w1name):
    Cw = Ctx(P)
    W1 = Cw.sb([128, 8, 4096], BF16, "W1")
    load_w_bf16(P, Cw, G.d[w1name], W1, 8, 4096)
    return Cw, W1


def phase_mlp(P, G, w1name, w2name, hin, xin, xout, hout, gpost, gnext, W1=None):
    C = Ctx(P)
    MT = 256
    if W1 is None:
        W1 = C.sb([128, 8, 4096], BF16, "W1")
        load_w_bf16(P, C, G.d[w1name], W1, 8, 4096)
    W2 = C.sb([128, 32, 1024], BF16, "W2")
    W2c = [Buf(W2.t) for _ in range(32)]
    load_w_bf16(P, C, G.d[w2name], W2, 32, 1024, stage=W2c)
    ht = [C.sb([128, 8, MT], BF16, "ht") for _ in range(2)]
    xts = [C.sb([128, 8, MT], F32, "xt") for _ in range(2)]
    mos = [C.sb([128, 8, MT], F32, "mo") for _ in range(2)]
    sqs = [C.sb([128, 8, MT], BF16, "sq") for _ in range(2)]
    rstds = [C.sb([128, MT], F32, "rstd") for _ in range(2)]
    hns = [C.sb([128, 8, MT], BF16, "hn") for _ in range(2)]
    rl = [C.sb([128, MT], F32, "rl") for _ in range(2)]
    hid = [C.sb([128, MT], BF16, "hid") for _ in range(3)]
    cst = G.cst
    h3 = G.d[hin].rearrange("(k p) t -> p k t", p=128)
    x3 = G.d[xin].rearrange("(k p) t -> p k t", p=128)
    xo3 = G.d[xout].rearrange("(k p) t -> p k t", p=128)
    hb = G.banks[0:3]
    eb = G.banks[3]
    obk = G.banks[4:8]
    ntile = S // MT

    def epilogue(ti):
        ts = slice(ti * MT, (ti + 1) * MT)
        xt, mo, sq, rstd, hn = xts[ti % 2], mos[ti % 2], sqs[ti % 2], rstds[ti % 2], hns[ti % 2]
        P.op("act", lambda e: e.activation(out=sq[:, :, :], in_=mo[:, :, :], func=AF.Square), [mo], [sq])
        for k in range(8):
            P.op("pe", lambda e: e.matmul(eb[:, 0:MT], lhsT=G.ones[:, :], rhs=sq[:, k, :], start=(k == 0), stop=(k == 7)),
                 [sq, G.ones], [eb])
        P.op("act", lambda e: e.activation(out=rstd[:, :], in_=eb[:, 0:MT], func=AF.Sqrt, bias=G.eps[:, 0:1], scale=1.0 / 1024),
             [eb, G.eps], [rstd])
        P.op("dve", lambda e: e.reciprocal(out=rstd[:, :], in_=rstd[:, :]), [rstd], [rstd])
        for k in range(8):
            P.op("dve", lambda e: e.scalar_tensor_tensor(out=mo[:, k, :], in0=mo[:, k, :],
                                                         scalar=cst[:, G.co[gpost] + k:G.co[gpost] + k + 1],
                                                         in1=rstd[:, :], op0=ALU.mult, op1=ALU.mult), [mo, rstd, cst], [mo])
        P.op("pool", lambda e: e.tensor_tensor(out=xt[:, :, :], in0=xt[:, :, :], in1=mo[:, :, :], op=ALU.add), [xt, mo], [xt])
        P.dma("pool", xo3[:, :, ts], xt[:, :, :], [xt], [G.dr[xout]], chan=("mlpxo", ti % 2))

    def epilogue_b(ti):
        ts = slice(ti * MT, (ti + 1) * MT)
        xt, mo, sq, rstd, hn = xts[ti % 2], mos[ti % 2], sqs[ti % 2], rstds[ti % 2], hns[ti % 2]
        if hout is not None:
            ho3 = G.d[hout].rearrange("(k p) t -> p k t", p=128)
            P.op("act", lambda e: e.activation(out=sq[:, :, :], in_=xt[:, :, :], func=AF.Square), [xt], [sq])
            for k in range(8):
                P.op("pe", lambda e: e.matmul(eb[:, 0:MT], lhsT=G.ones[:, :], rhs=sq[:, k, :], start=(k == 0), stop=(k == 7)),
                     [sq, G.ones], [eb])
            P.op("act", lambda e: e.activation(out=rstd[:, :], in_=eb[:, 0:MT], func=AF.Sqrt, bias=G.eps[:, 0:1],
                                               scale=1.0 / 1024), [eb, G.eps], [rstd])
            P.op("dve", lambda e: e.reciprocal(out=rstd[:, :], in_=rstd[:, :]), [rstd], [rstd])
            for k in range(8):
                P.op("dve", lambda e: e.scalar_tensor_tensor(out=hn[:, k, :], in0=xt[:, k, :],
                                                             scalar=cst[:, G.co[gnext] + k:G.co[gnext] + k + 1],
                                                             in1=rstd[:, :], op0=ALU.mult, op1=ALU.mult), [xt, rstd, cst], [hn])
            P.dma("pool", ho3[:, :, ts], hn[:, :, :], [hn], [G.dr[hout]], chan=("mlpho", ti % 2))

    for ti in range(ntile):
        ts = slice(ti * MT, (ti + 1) * MT)
        h = ht[ti % 2]
        xt = xts[ti % 2]
        mo = mos[ti % 2]
        P.dma("sp", h[:, :, :], h3[:, :, ts], [G.dr[hin]], [h], chan=("mlph", ti % 2))
        P.dma("sp", xt[:, :, :], x3[:, :, ts], [G.dr[xin]], [xt], chan=("mlpx", ti % 2))

        def emit_h(f):
            b = hb[f % 3]
            for k in range(8):
                P.op("pe", lambda e: e.matmul(b[:, 0:MT], lhsT=W1[:, k, f * 128:(f + 1) * 128], rhs=h[:, k, :],
                                              start=(k == 0), stop=(k == 7)), [W1, h], [b])
        emit_h(0)
        emit_h(1)
        for f in range(32):
            if f + 2 < 32:
                emit_h(f + 2)
            b = hb[f % 3]
            r = rl[f % 2]
            hd = hid[f % 3]
            P.op("act", lambda e: e.activation(out=r[:, :], in_=b[:, 0:MT], func=AF.Relu), [b], [r])
            P.op("dve", lambda e: e.tensor_tensor(out=hd[:, :], in0=r[:, :], in1=r[:, :], op=ALU.mult), [r], [hd])
            for oc in range(8):
                o = obk[oc // 2]
                P.op("pe", lambda e: e.matmul(o[:, (oc % 2) * MT:(oc % 2 + 1) * MT], lhsT=W2[:, f, oc * 128:(oc + 1) * 128],
                                              rhs=hd[:, :], start=(f == 0 and oc % 2 == 0), stop=(f == 31),
                                              skip_group_check=True), [W2c[f], hd], [o])
            if f == 4 and ti > 0:
                epilogue(ti - 1)
            if f == 18 and ti > 0:
                epilogue_b(ti - 1)
        for ob_i in range(4):
            o = obk[ob_i]
            if ob_i % 2:
                P.op("act", lambda e: e.activation(out=mo[:, 2 * ob_i:2 * ob_i + 2, :], in_=o[:, :].rearrange("p (a n) -> p a n", a=2),
                                                   func=AF.Copy), [o], [mo])
            else:
                P.op("dve", lambda e: e.tensor_copy(out=mo[:, 2 * ob_i:2 * ob_i + 2, :], in_=o[:, :].rearrange("p (a n) -> p a n", a=2)),
                     [o], [mo])
    epilogue(ntile - 1)
    epilogue_b(ntile - 1)
    C.close()


def phase_front_c(P, G):
    C = Ctx(P)
    NIN = 2608
    win = C.sb([128, 8, NIN], BF16, "cwin")
    stage = None
    load_w_bf16(P, C, G.d["c_w_in"], win, 8, NIN, stage)
    ht = [C.sb([128, 8, NT], BF16, "ht") for _ in range(2)]
    ob = [C.sb([128, NT], BF16, "ob") for _ in range(3)]
    gt = C.sb([48, NT], BF16, "gt")
    h3 = G.d["h3T"].rearrange("(k p) t -> p k t", p=128)
    n = 0
    for ti in range(NTT):
        ts = slice(ti * NT, (ti + 1) * NT)
        h = ht[ti % 2]
        P.dma("sp", h[:, :, :], h3[:, :, ts], [G.dr["h3T"]], [h], chan=("fch", ti % 2))
        for c in range(16):
            b = G.bank()
            for k in range(8):
                P.op("pe", lambda e: e.matmul(b[:, :], lhsT=win[:, k, c * 128:(c + 1) * 128], rhs=h[:, k, :],
                                              start=(k == 0), stop=(k == 7)), [win, h], [b])
            o = ob[n % 3]
            if n % 2:
                P.op("act", lambda e: e.activation(out=o[:, :], in_=b[:, :], func=AF.Copy), [b], [o])
            else:
                P.op("dve", lambda e: e.tensor_copy(out=o[:, :], in_=b[:, :]), [b], [o])
            if c < 8:
                d0, d1, nm = G.d["qTn"][2 * c, :, ts], G.d["qTn"][2 * c + 1, :, ts], "qTn"
            else:
                wh, gp = (c - 8) // 2, (c - 8) % 2
                d0, d1, nm = G.d["kvT"][wh, 2 * gp, :, ts], G.d["kvT"][wh, 2 * gp + 1, :, ts], "kvT"
            P.dma("pool", d0, o[0:64, :], [o], [G.dr[nm]], chan=("fco", n % 3))
            P.dma("pool", d1, o[64:128, :], [o], [G.dr[nm]], chan=("fco", n % 3))
            n += 1
        b = G.bank()
        for k in range(8):
            P.op("pe", lambda e: e.matmul(b[0:48, :], lhsT=win[:, k, 2560:2608], rhs=h[:, k, :], start=(k == 0),
                                          stop=(k == 7)), [win, h], [b])
        P.op("act", lambda e: e.activation(out=gt[0:48, :], in_=b[0:48, :], func=AF.Sigmoid), [b], [gt])
        P.dma("pool", G.d["gT"][:, ts], gt[0:48, :], [gt], [G.dr["gT"]], chan="fcg")
        for tb in range(4):
            b = G.bank()
            for k in range(8):
                P.op("pe", lambda e: e.matmul(b[:, :], lhsT=h[:, k, tb * 128:(tb + 1) * 128], rhs=win[:, k, 2048:2560],
                                              start=(k == 0), stop=(k == 7)), [win, h], [b])
            o = ob[n % 3]
            if n % 2:
                P.op("act", lambda e: e.activation(out=o[:, :], in_=b[:, :], func=AF.Copy), [b], [o])
            else:
                P.op("dve", lambda e: e.tensor_copy(out=o[:, :], in_=b[:, :]), [b], [o])
            r0 = ti * NT + tb * 128
            P.dma("pool", G.d["vsw"][r0:r0 + 128, :], o[:, :], [o], [G.dr["vsw"]], chan=("fco", n % 3))
            n += 1
    C.close()


def phase_compress(P, G):
    C = Ctx(P)
    stage = C.sb([64, 2048], F32, "cstg")
    w1 = [C.sb([64, 2048], BF16, "cw1") for _ in range(2)]
    w2 = [C.sb([64, 64], BF16, "cw2") for _ in range(2)]
    peT = [C.sb([64, 32], BF16, "cpe") for _ in range(2)]
    bias = [C.sb([64, 1], F32, "cbias") for _ in range(2)]
    for kv, sfx in enumerate(("k", "v")):
        P.dma("sp", stage[:, :], G.d["cw1" + sfx][:, :], [], [stage], chan="cstg")
        P.op("dve", lambda e: e.tensor_copy(out=w1[kv][:, :], in_=stage[:, :]), [stage], [w1[kv]])
        P.dma("sp", stage[:, 0:64], G.d["cw2" + sfx][:, :], [], [stage], chan="cstg")
        P.op("dve", lambda e: e.tensor_copy(out=w2[kv][:, :], in_=stage[:, 0:64]), [stage], [w2[kv]])
        P.dma("sp", stage[:, 0:32], G.d["cpe" + sfx][:, :], [], [stage], chan="cstg")
        P.op("dve", lambda e: e.tensor_copy(out=peT[kv][:, :], in_=stage[:, 0:32]), [stage], [peT[kv]])
        b = G.bank()
        for l in range(32):
            P.op("pe", lambda e: e.matmul(b[0:64, 0:1], lhsT=w1[kv][:, l * 64:(l + 1) * 64], rhs=peT[kv][:, l:l + 1],
                                          start=(l == 0), stop=(l == 31)), [w1[kv], peT[kv]], [b])
        P.op("dve", lambda e: e.tensor_copy(out=bias[kv][:, :], in_=b[0:64, 0:1]), [b], [bias[kv]])
    zt = [C.sb([64, S], BF16, "zt") for _ in range(2)]
    xb = C.sb([64, 256], F32, "xb")
    tt = C.sb([64, 256], F32, "tt")
    sg = C.sb([64, 256], F32, "sg")
    hdn = C.sb([64, 256], BF16, "hdn")
    P.op("pool", lambda e: e.memset(hdn[:, :], 0.0), [], [hdn])
    n = 0
    for g in range(4):
        KC, VC = G.KC[g], G.VC[g]
        P.op("pool", lambda e: e.memset(KC[:, :], 0.0), [], [KC])
        P.op("pool", lambda e: e.memset(KC[64:65, :], 1.0), [], [KC])
        P.op("pool", lambda e: e.memset(VC[:, :, 0:64], 0.0), [], [VC])
        P.op("pool", lambda e: e.memset(VC[:, :, 64:128], 1.0), [], [VC])
        for kv in range(2):
            z = zt[n % 2]
            n += 1
            P.dma("sp", z[:, :], G.d["kvT"][kv, g, :, :], [G.dr["kvT"]], [z], chan=("zt", n % 2))
            b = G.bank()
            for l in range(32):
                P.op("pe", lambda e: e.matmul(b[0:64, 0:255], lhsT=w1[kv][:, l * 64:(l + 1) * 64],
                                              rhs=z[:, l:l + 16 * 254 + 1:16], start=(l == 0), stop=(l == 31)),
                     [w1[kv], z], [b])
            P.op("dve", lambda e: e.tensor_scalar(out=xb[:, 0:255], in0=b[0:64, 0:255], scalar1=bias[kv][:, 0:1],
                                                  scalar2=None, op0=ALU.add), [b, bias[kv]], [xb])
            P.op("dve", lambda e: e.tensor_tensor(out=tt[:, 0:255], in0=xb[:, 0:255], in1=xb[:, 0:255], op=ALU.mult),
                 [xb], [tt])
            P.op("dve", lambda e: e.tensor_scalar(out=tt[:, 0:255], in0=tt[:, 0:255], scalar1=0.044715, scalar2=1.0,
                                                  op0=ALU.mult, op1=ALU.add), [tt], [tt])
            P.op("dve", lambda e: e.tensor_tensor(out=tt[:, 0:255], in0=tt[:, 0:255], in1=xb[:, 0:255], op=ALU.mult),
                 [tt, xb], [tt])
            P.op("act", lambda e: e.activation(out=sg[:, 0:255], in_=tt[:, 0:255], func=AF.Sigmoid,
                                               scale=2.0 * 0.7978845608028654), [tt], [sg])
            P.op("dve", lambda e: e.tensor_tensor(out=hdn[:, 0:255], in0=xb[:, 0:255], in1=sg[:, 0:255], op=ALU.mult),
                 [xb, sg], [hdn])
            b2 = G.bank()
            if kv == 0:
                P.op("pe", lambda e: e.matmul(b2[0:64, 0:255], lhsT=w2[0][:, :], rhs=hdn[:, 0:255], start=True, stop=True),
                     [w2[0], hdn], [b2])
                P.op("act", lambda e: e.activation(out=KC[0:64, 0:255], in_=b2[0:64, 0:255], func=AF.Copy), [b2], [KC])
            else:
                for c in range(2):
                    ncol = 128 if c == 0 else 127
                    b3 = G.bank()
                    P.op("pe", lambda e: e.matmul(b3[0:ncol, 0:64], lhsT=hdn[:, c * 128:c * 128 + ncol], rhs=w2[1][:, :],
                                                  start=True, stop=True), [w2[1], hdn], [b3])
                    P.op("act", lambda e: e.activation(out=VC[0:ncol, c, 0:64], in_=b3[0:ncol, 0:64], func=AF.Copy),
                         [b3], [VC])
    C.close()


def phase_nsa(P, G, debug=False):
    C = Ctx(P)
    scale = 0.125
    Q = [C.sb([128, S], BF16, "Q") for _ in range(4)]
    Qs = [[C.sb([128, NT], BF16, "Qs") for _ in range(4)] for _ in range(2)]
    KS = C.sb([128, S], BF16, "KS")
    KW = C.sb([128, S], BF16, "KW")
    VS = C.sb([128, 32, 128], BF16, "VS")
    VW = C.sb([128, 32, 128], BF16, "VW")
    gT = C.sb([48, S], BF16, "gT")
    E = [C.sb([128, NT], BF16, "E") for _ in range(3)]
    masks = C.sb([128, 13, NT], BF16, "masks")
    selm = C.sb([48, 48 * 64], BF16, "selm")
    ov2 = C.sb([128, 2, 65], BF16, "ov2")
    btab = C.sb([128, 704], F32, "btab")
    qterm = C.sb([64, 4, NT], F32, "qterm")
    addt = C.sb([128, 4, 64], F32, "addt")
    acc = [[C.sb([64, NT], F32, "acc") for _ in range(4)] for _ in range(2)]
    imp = C.sb([128, 4, 64], F32, "imp")
    itmp = C.sb([128, 4, 64], F32, "itmp")
    zr = C.sb([128, 4, 1], F32, "zr")
    rz = C.sb([64, NT], F32, "rz")
    coefs = [C.sb([64, NT], F32, "coef") for _ in range(2)]
    ftmp = C.sb([64, NT], F32, "ftmp")
    m8 = C.sb([128, 16], F32, "m8")
    wk = C.sb([128, 64], F32, "wk")
    vv = C.sb([128, 64], F32, "vv")
    ngf = C.sb([128, 64], F32, "ngf")
    ng = C.sb([128, 64], BF16, "ng")
    yo = [C.sb([64, NT], BF16, "yo") for _ in range(2)]
    sbk = G.banks[0:3]
    obk = G.banks[3:5]
    obc = G.banks[5]
    ub = G.banks[6]
    gb = G.banks[7]
    tb = G.banks[7]
    P.dma("sp", masks[:, 0:4, :], G.d["cmask"].rearrange("o p n -> p o n"), [], [masks], chan="nsac")
    P.dma("sp", masks[:, 4:8, :], G.d["wmask"].rearrange("o p n -> p o n"), [], [masks], chan="nsac")
    P.dma("sp", masks[:, 8:13, :], G.d["cmpmask"].rearrange("o p n -> p o n"), [], [masks], chan="nsac")
    P.dma("sp", selm[:, :], G.d["selm"][:, :], [], [selm], chan="nsac")
    P.dma("sp", ov2[:, :, :], G.d["ov2"][:, :, :], [], [ov2], chan="nsac")
    P.dma("sp", btab[:, :], G.d["biastab"][:, :], [], [btab], chan="nsac")
    P.dma("sp", gT[:, :], G.d["gT"][:, :], [G.dr["gT"]], [gT], chan="nsac")
    P.dma("sp", KS[64:128, :], G.d["expand"][:, :], [], [KS], chan="nsak")
    P.op("pool", lambda e: e.memset(KW[64:128, :], 0.0), [], [KW])
    P.op("pool", lambda e: e.memset(KW[64:65, :], 1.0), [], [KW])
    for r in range(4):
        P.op("pool", lambda e: e.memset(Q[r][64:128, :], 0.0), [], [Q[r]])
    P.op("pool", lambda e: e.memset(VS[:, :, 64:128], 1.0), [], [VS])
    P.op("pool", lambda e: e.memset(VW[:, :, 64:128], 1.0), [], [VW])
    vsrc = G.d["vsw"].rearrange("(c p) f -> p c f", p=128)
    cnt = {"s": 0, "o": 0, "y": 0, "f": 0}
    NOSB = 8
    osb = [C.sb([128, NT], F32, "osb") for _ in range(NOSB)]
    TORDER = [0, 7, 1, 6, 2, 5, 3, 4]
    PAR = {T: i % 2 for i, T in enumerate(TORDER)}
    gbs = [C.sb([64, 12, NT], BF16, "gbs") for _ in range(2)]
    ngs = [C.sb([128, 64], BF16, "ngs") for _ in range(4)]

    def finalize(o, br, h, T, r, first):
        ts = slice(T * NT, (T + 1) * NT)
        col = (br * 16 + h) * 64
        os_ = osb[cnt["f"] % NOSB]
        cnt["f"] += 1
        if T >= 4:
            P.op("dve", lambda e: e.tensor_scalar(out=os_[:, :], in0=o[:, :], scalar1=G.tiny[:, 0:1], scalar2=None, op0=ALU.add),
                 [o, G.tiny], [os_])
        else:
            P.op("act", lambda e: e.activation(out=os_[:, :], in_=o[:, :], func=AF.Identity, bias=G.tiny[:, 0:1], scale=1.0),
                 [o, G.tiny], [os_])
        gsb = gbs[PAR[T]]
        gi = br * 4 + r
        coef = coefs[cnt["f"] % 2]
        P.op("dve", lambda e: e.reciprocal(out=coef[0:64, :], in_=os_[64:128, :]), [os_], [coef])
        P.op("dve", lambda e: e.tensor_tensor(out=coef[0:64, :], in0=gsb[0:64, gi, :], in1=coef[0:64, :], op=ALU.mult),
             [gsb, coef], [coef])
        a = acc[PAR[T]][r]
        if first:
            P.op("pool", lambda e: e.tensor_tensor(out=a[0:64, :], in0=os_[0:64, :], in1=coef[0:64, :], op=ALU.mult),
                 [os_, coef], [a])
        else:
            P.op("pool", lambda e: e.tensor_tensor(out=ftmp[0:64, :], in0=os_[0:64, :], in1=coef[0:64, :], op=ALU.mult),
                 [os_, coef], [ftmp])
            P.op("pool", lambda e: e.tensor_tensor(out=a[0:64, :], in0=a[0:64, :], in1=ftmp[0:64, :], op=ALU.add),
                 [a, ftmp], [a])
        if debug:
            P.dma("pool", G.d["dbgacc"][br, h * 64:(h + 1) * 64, ts], a[0:64, :], [a], [G.dr["dbgacc"]], chan="dbg")

    def run_stream(items):
        base = cnt["s"]
        cnt["s"] += len(items)
        index = {id(it): i for i, it in enumerate(items)}
        depi = [index[id(it["dep"])] if it.get("dep") is not None else -1 for it in items]
        st = {"emitted": 0}

        def emit_upto(limit, done):
            while st["emitted"] <= limit and st["emitted"] < len(items) and depi[st["emitted"]] <= done:
                m = st["emitted"]
                items[m]["emit_s"](items[m]["c"], sbk[(base + m) % 3], items[m].get("cols", (0, NT)))
                st["emitted"] += 1

        for n, it in enumerate(items):
            for hook in it.get("pre", ()):
                hook()
            emit_upto(n + 2, n - 1)
            assert st["emitted"] > n
            b = sbk[(base + n) % 3]
            ee = E[(base + n) % 3]
            bc = it["bc"]
            o = it["o"]
            vtile = it["v"]
            c = it["c"]
            j0, j1 = it.get("cols", (0, NT))
            P.op("act", lambda e: e.activation(out=ee[:, j0:j1], in_=b[:, j0:j1], func=AF.Exp, bias=btab[:, bc:bc + 1],
                                               scale=scale), [b, btab], [ee])
            P.op("pe", lambda e: e.matmul(o[:, j0:j1], lhsT=vtile[:, c, :], rhs=ee[:, j0:j1], start=it["first"],
                                          stop=it["last"], skip_group_check=True), [vtile, ee], [o])
            if it.get("post"):
                it["post"](ee)
            if it["last"]:
                it["fin"]()

    for g in range(4):
        KC, VC = G.KC[g], G.VC[g]
        P.dma("sp", KS[0:64, :], G.d["kvT"][2, g, :, :], [G.dr["kvT"]], [KS], chan="nsak")
        P.dma("sp", KW[0:64, :], G.d["kvT"][3, g, :, :], [G.dr["kvT"]], [KW], chan="nsak")
        P.dma("sp", VS[:, :, 0:64], vsrc[:, :, g * 64:(g + 1) * 64], [G.dr["vsw"]], [VS], chan="nsav")
        P.dma("sp", VW[:, :, 0:64], vsrc[:, :, 256 + g * 64:256 + (g + 1) * 64], [G.dr["vsw"]], [VW], chan="nsav")
        P.dma("sp", qterm[:, :, :], G.d["qterm"][4 * g:4 * g + 4, :, :].rearrange("r p n -> p r n"), [], [qterm], chan="nsaqt")
        for r in range(4):
            P.dma("sp", Q[r][0:64, :], G.d["qTn"][4 * g + r, :, :], [G.dr["qTn"]], [Q[r]], chan=("nsaq", r))
            P.dma("sp", Q[r][64:65, :], G.d["qaug"][4 * g + r, :, :], [], [Q[r]], chan=("nsaq", r))
        def cmp_items(T):
            ts = slice(T * NT, (T + 1) * NT)
            out = []
            for r in range(4):
                h = 4 * g + r
                q = Q[r]
                chunks = [0] if T <= 3 else [0, 1]
                o = obc

                def mk_emit(q):
                    def emit_cmp(c, b, cols=None):
                        dl = T - 4 * c
                        P.op("pe", lambda e: e.matmul(b[:, :], lhsT=KC[:, c * 128:(c + 1) * 128], rhs=q[:, ts], start=True,
                                                      stop=(dl > 4)), [KC, q], [b])
                        if dl <= 4:
                            P.op("pe", lambda e: e.matmul(b[:, :], lhsT=G.ident[:, :], rhs=masks[:, 8 + dl, :], start=False,
                                                          stop=True), [G.ident, masks], [b])
                    return emit_cmp

                def mk_post(c, nlast):
                    def post(ee):
                        for s in range(4):
                            P.op("pe", lambda e: e.matmul(ub[:, s * 65:(s + 1) * 65], lhsT=ee[:, s * 128:(s + 1) * 128],
                                                          rhs=ov2[:, c, :], start=(c == 0 and s == 0), stop=nlast,
                                                          skip_group_check=True), [ov2, ee], [ub])
                    return post

                def mk_fin(o, h, r):
                    def fin():
                        if r == 0:
                            P.dma("sp", addt[:, :, :], G.d["addtab"][:, 4 * T:4 * T + 4, :], [], [addt], chan="addt")
                        finalize(o, 0, h, T, r, True)
                        u3 = ub[:, 0:260].rearrange("p (a n) -> p a n", a=4)
                        P.op("dve", lambda e: e.tensor_scalar(out=zr[:, :, :], in0=u3[:, :, 64:65], scalar1=1e-30, scalar2=None,
                                                              op0=ALU.max), [ub], [zr])
                        P.op("dve", lambda e: e.reciprocal(out=zr[:, :, :], in_=zr[:, :, :]), [zr], [zr])
                        if r == 0:
                            P.op("dve", lambda e: e.tensor_tensor(out=imp[:, :, :], in0=u3[:, :, 0:64],
                                                                  in1=zr[:, :, 0:1].to_broadcast([128, 4, 64]), op=ALU.mult),
                                 [ub, zr], [imp])
                        else:
                            P.op("dve", lambda e: e.tensor_tensor(out=itmp[:, :, :], in0=u3[:, :, 0:64],
                                                                  in1=zr[:, :, 0:1].to_broadcast([128, 4, 64]), op=ALU.mult),
                                 [ub, zr], [itmp])
                            P.op("pool", lambda e: e.tensor_tensor(out=imp[:, :, :], in0=imp[:, :, :], in1=itmp[:, :, :],
                                                                   op=ALU.add), [imp, itmp], [imp])
                        if r == 3:
                            topk(T)
                    return fin

                es = mk_emit(q)
                fn = mk_fin(o, h, r)
                for n, c in enumerate(chunks):
                    out.append(dict(emit_s=es, c=c, bc=512 + h * 12 + (T if c == 0 else 8 + (T - 4)), o=o, v=VC,
                                    first=(n == 0), last=(n == len(chunks) - 1), fin=fn, post=mk_post(c, n == len(chunks) - 1)))
            return out

        def load_gates(T):
            ts = slice(T * NT, (T + 1) * NT)
            gsb = gbs[PAR[T]]
            for br in range(3):
                for r in range(4):
                    row = br * 16 + 4 * g + r
                    P.dma("sp", gsb[0:64, br * 4 + r, :], G.d["gT"][row:row + 1, ts].partition_broadcast(64),
                          [G.dr["gT"]], [gsb], chan=("gbs", PAR[T]), acc=True)

        def topk(T):
            ts = slice(T * NT, (T + 1) * NT)
            if debug:
                P.dma("pool", G.d["dbgimp"][g, :, 4 * T:4 * T + 4, :], imp[:, :, :], [imp], [G.dr["dbgimp"]], chan="dbg")
            for s in range(4):
                P.op("dve", lambda e: e.tensor_tensor(out=vv[:, :], in0=imp[:, s, :], in1=addt[:, s, :], op=ALU.add),
                     [imp, addt], [vv])
                P.op("dve", lambda e: e.max(out=m8[:, 0:8], in_=vv[:, :]), [vv], [m8])
                P.op("dve", lambda e: e.match_replace(out=wk[:, :], in_to_replace=m8[:, 0:8], in_values=vv[:, :],
                                                      imm_value=-3.0e38), [m8, vv], [wk])
                P.op("dve", lambda e: e.max(out=m8[:, 8:16], in_=wk[:, :]), [wk], [m8])
                P.op("dve", lambda e: e.tensor_scalar(out=ngf[:, :], in0=vv[:, :], scalar1=m8[:, 15:16], scalar2=30000.0,
                                                      op0=ALU.is_ge, op1=ALU.mult), [vv, m8], [ngf])
                ngx = ngs[s]
                P.op("dve", lambda e: e.tensor_scalar(out=ngx[:, :], in0=ngf[:, :], scalar1=-30000.0, scalar2=None,
                                                      op0=ALU.add), [ngf], [ngx])

        def topk2(T):
            ts = slice(T * NT, (T + 1) * NT)
            for s in range(4):
                ngx = ngs[s]
                P.op("pe", lambda e: e.matmul(tb[0:64, s * 128:(s + 1) * 128], lhsT=ngx[:, :], rhs=G.ident[:, :], start=True,
                                              stop=True), [ngx, G.ident], [tb])
            qs = Qs[PAR[T]]
            for r in range(4):
                P.op("dve", lambda e: e.tensor_tensor(out=qs[r][64:128, :], in0=tb[0:64, :], in1=qterm[0:64, r, :], op=ALU.add),
                     [tb, qterm], [qs[r]])
                P.op("pool", lambda e: e.tensor_copy(out=qs[r][0:64, :], in_=Q[r][0:64, ts]), [Q[r]], [qs[r]])

        def sw_items(T, dep):
            ts = slice(T * NT, (T + 1) * NT)
            qs = Qs[PAR[T]]
            items = []
            specs = []
            for r in range(4):
                h = 4 * g + r

                def mk_slc(qq):
                    def emit_slc(c, b, cols):
                        diag = c >= 4 * T
                        j0, j1 = cols
                        P.op("pe", lambda e: e.matmul(b[:, j0:j1], lhsT=KS[:, c * 128:(c + 1) * 128], rhs=qq[:, j0:j1], start=True,
                                                      stop=not diag), [KS, qq], [b])
                        if diag:
                            P.op("pe", lambda e: e.matmul(b[:, j0:j1], lhsT=G.ident[:, :], rhs=masks[:, c - 4 * T, j0:j1],
                                                          start=False, stop=True), [G.ident, masks], [b])
                    return emit_slc

                def mk_win(q):
                    def emit_win(c, b, cols):
                        mi = (c - 4 * T) if c >= 4 * T else 4 + (c - (4 * T - 4))
                        j0, j1 = cols
                        P.op("pe", lambda e: e.matmul(b[:, j0:j1], lhsT=KW[:, c * 128:(c + 1) * 128],
                                                      rhs=q[:, T * NT + j0:T * NT + j1], start=True, stop=False), [KW, q], [b])
                        P.op("pe", lambda e: e.matmul(b[:, j0:j1], lhsT=G.ident[:, :], rhs=masks[:, mi, j0:j1], start=False,
                                                      stop=True), [G.ident, masks], [b])
                    return emit_win

                def mk_fin(o, br, h, r, store):
                    def fin():
                        finalize(o, br, h, T, r, False)
                        if store:
                            y = yo[cnt["y"] % 2]
                            a = acc[PAR[T]][r]
                            P.op("pool", lambda e: e.tensor_copy(out=y[0:64, :], in_=a[0:64, :]), [a], [y])
                            P.dma("pool", G.d["ynsaT"][h * 64:(h + 1) * 64, ts], y[0:64, :], [y], [G.dr["ynsaT"]],
                                  chan=("nsay", cnt["y"] % 2))
                            cnt["y"] += 1
                    return fin

                specs.append((2, r, h, VW, mk_win(Q[r]), mk_fin, list(range(max(0, 4 * T - 4), 4 * T + 4))))
                specs.append((1, r, h, VS, mk_slc(qs[r]), mk_fin, list(range(4 * T + 4))))
            for br, r, h, vt, es, mkf, cl in sorted(specs, key=lambda s: (-s[0], s[1])):
                o = obk[cnt["o"] % 2]
                cnt["o"] += 1
                fn = mkf(o, br, h, r, br == 1)
                for n, c in enumerate(cl):
                    if c >= 4 * T:
                        cols = (128 * (c - 4 * T), NT)
                    elif br == 2:
                        cols = (0, 128 * (c - (4 * T - 4) + 1))
                    else:
                        cols = (0, NT)
                    items.append(dict(emit_s=es, c=c, bc=h * 32 + (c - 4 * T + 28), o=o, v=vt, first=(n == 0),
                                      last=(n == len(cl) - 1), fin=fn, post=None, dep=None, slc_tile=(T if br == 1 else None),
                                      cols=cols))
            return items

        DEFER = 10
        load_gates(TORDER[0])
        seq = cmp_items(TORDER[0])
        pend = (len(seq) - 1, TORDER[0])
        for ti_, T in enumerate(TORDER):
            TN = TORDER[ti_ + 1] if ti_ + 1 < NTT else None
            sw = sw_items(T, None)
            if TN is not None:
                cm = cmp_items(TN)
                groups = []
                for it in cm:
                    if it["first"]:
                        groups.append([it])
                    else:
                        groups[-1].append(it)
                step = max(1, (len(sw) * 3) // (len(groups) * 5 + 1))
                pos = step
                merged = []
                gi = 0
                for n, it in enumerate(sw):
                    merged.append(it)
                    if gi < len(groups) and n + 1 == pos:
                        merged.extend(groups[gi])
                        gi += 1
                        pos += step
                while gi < len(groups):
                    merged.extend(groups[gi])
                    gi += 1
            else:
                cm = None
                merged = sw
            start = len(seq)
            seq += merged
            if TN is not None:
                seq[start].setdefault("pre", []).append((lambda TT: (lambda: load_gates(TT)))(TN))
            li, tt = pend
            hi_ = min(li + DEFER, len(seq) - 1)
            seq[hi_].setdefault("pre", []).append((lambda TT: (lambda: topk2(TT)))(tt))
            for it in seq[start:]:
                if it.get("slc_tile") == tt:
                    it["dep"] = seq[hi_]
            if cm is not None:
                pend = (max(i for i, it in enumerate(seq) if it is cm[-1]), TN)
        run_stream(seq)
    C.close()


def build_program(nphases=99, debug=False):
    nc = bass.Bass("TRN2", target_bir_lowering=False)
    P = Prog(nc)
    G = Common()
    G.d = {}
    G.dr = {}

    def dram(name, shape, dt, kind="Internal"):
        if debug and kind == "Internal":
            kind = "ExternalOutput"
        t = nc.dram_tensor(name, list(shape), dt, kind=kind)
        G.d[name] = t.ap()
        G.dr[name] = Buf(t, multi=True)

    dram("xT", [D, S], F32, "ExternalInput")
    dram("cstv", [128, 128], F32, "ExternalInput")
    dram("ropecs", [128, S], F32, "ExternalInput")
    dram("a_w_in", [D, 2624], F32, "ExternalInput")
    dram("a_w_q", [768, 1024], F32, "ExternalInput")
    dram("a_w_kv", [256, 1024], F32, "ExternalInput")
    dram("ycatT", [D, S], BF16)
    dram("qT", [8, 96, S], BF16)
    dram("kT", [8, 96, S], BF16)
    dram("vtok", [S, 512], BF16)
    dram("cmask", [4, 128, NT], BF16, "ExternalInput")
    dram("identv", [128, 128], BF16, "ExternalInput")
    dram("a_w_out", [D, D], F32, "ExternalInput")
    dram("w1_0", [D, 4096], F32, "ExternalInput")
    dram("w2_0", [4096, D], F32, "ExternalInput")
    dram("x1T", [D, S], F32)
    dram("h2T", [D, S], BF16)
    dram("x2T", [D, S], F32)
    dram("h3T", [D, S], BF16)
    dram("c_w_in", [D, 2608], F32, "ExternalInput")
    dram("qTn", [16, 64, S], BF16)
    dram("kvT", [4, 4, 64, S], BF16)
    dram("gT", [48, S], BF16)
    dram("vsw", [S, 512], BF16)
    for sfx in ("k", "v"):
        dram("cw1" + sfx, [64, 2048], F32, "ExternalInput")
        dram("cw2" + sfx, [64, 64], F32, "ExternalInput")
        dram("cpe" + sfx, [64, 32], F32, "ExternalInput")
    dram("qaug", [16, 1, S], BF16, "ExternalInput")
    dram("biastab", [128, 704], F32, "ExternalInput")
    dram("qterm", [16, 64, NT], F32, "ExternalInput")
    dram("wmask", [4, 128, NT], BF16, "ExternalInput")
    dram("cmpmask", [5, 128, NT], BF16, "ExternalInput")
    dram("expand", [64, S], BF16, "ExternalInput")
    dram("selm", [48, 48 * 64], BF16, "ExternalInput")
    dram("ov2", [128, 2, 65], BF16, "ExternalInput")
    dram("addtab", [128, 32, 64], F32, "ExternalInput")
    dram("ynsaT", [D, S], BF16)
    dram("c_w_out", [D, D], F32, "ExternalInput")
    dram("w1_1", [D, 4096], F32, "ExternalInput")
    dram("w2_1", [4096, D], F32, "ExternalInput")
    dram("x3T", [D, S], F32)
    dram("h4T", [D, S], BF16)
    dram("outT", [D, S], F32, "ExternalOutput")
    if debug:
        dram("dbgacc", [3, D, S], F32)
        dram("dbgimp", [4, 128, 32, 64], F32)
    G.co = {"gpre0": 0, "gpost0": 8, "gmpre0": 16, "gmpost0": 24, "gpre1": 32, "gpost1": 40, "gmpre1": 48,
            "gmpost1": 56, "qn": 64, "kvn": 70, "convw": 72}
    with ExitStack() as gs:
        def gsb(name, shape, dt):
            return Buf(gs.enter_context(nc.sbuf_tensor(name, list(shape), dt)))
        G.cst = gsb("cst", [128, 128], F32)
        G.ones = gsb("ones", [128, 128], BF16)
        G.eps = gsb("eps", [128, 1], F32)
        G.tiny = gsb("tiny", [128, 1], F32)
        G.banks = [Buf(gs.enter_context(nc.psum_tensor("bank%d" % i, [128, 512], F32)), psum=True) for i in range(8)]

        def bank():
            P.bank_i = (P.bank_i + 1) % 8
            return G.banks[P.bank_i]
        G.bank = bank
        P.dma("sp", G.cst[:, :], G.d["cstv"][:, :], [], [G.cst], chan="cst")
        P.op("pool", lambda e: e.memset(G.ones[:, :], 1.0), [], [G.ones])
        P.op("pool", lambda e: e.memset(G.eps[:, :], EPS), [], [G.eps])
        P.op("pool", lambda e: e.memset(G.tiny[0:64, :], 0.0), [], [G.tiny])
        P.op("pool", lambda e: e.memset(G.tiny[64:128, :], 1e-30), [], [G.tiny])
        G.ident = gsb("ident", [128, 128], BF16)
        P.dma("sp", G.ident[:, :], G.d["identv"][:, :], [], [G.ident], chan="cst")
        if nphases >= 1:
            phase_front_a(P, G)
        if nphases >= 2:
            phase_mla_attn(P, G)
        if nphases >= 3:
            Cw, W1 = prefetch_w1(P, G, "w1_0") if nphases >= 4 else (None, None)
            phase_outproj(P, G, "a_w_out", "ycatT", "xT", "x1T", "h2T", "gpost0", "gmpre0")
        if nphases >= 4:
            phase_mlp(P, G, "w1_0", "w2_0", "h2T", "x1T", "x2T", "h3T", "gmpost0", "gpre1", W1=W1)
            Cw.close()
        G.KC = [gsb("KC%d" % g, [128, 256], BF16) for g in range(4)]
        G.VC = [gsb("VC%d" % g, [128, 2, 128], BF16) for g in range(4)]
        if nphases >= 5:
            phase_front_c(P, G)
        if nphases >= 6:
            phase_compress(P, G)
        if nphases >= 7:
            phase_nsa(P, G, debug)
        if nphases >= 8:
            Cw, W1 = prefetch_w1(P, G, "w1_1") if nphases >= 9 else (None, None)
            phase_outproj(P, G, "c_w_out", "ynsaT", "x2T", "x3T", "h4T", "gpost1", "gmpre1")
        if nphases >= 9:
            phase_mlp(P, G, "w1_1", "w2_1", "h4T", "x3T", "outT", None, "gmpost1", None, W1=W1)
            Cw.close()
        P.barrier()
        P.final_wait()
    print("instructions:", P.ninst, "sems:", P.nsem)
    return nc


def pack_cols(v):
    return np.ascontiguousarray(np.asarray(v, np.float32).reshape(-1, 128).T)


def host_consts(inp):
    c = {}
    cst = np.zeros((128, 128), np.float32)
    cols = [inp["norm_mix_pre"][0], inp["norm_mix_post"][0], inp["norm_mlp_pre"][0], inp["norm_mlp_post"][0],
            inp["norm_mix_pre"][1], inp["norm_mix_post"][1], inp["norm_mlp_pre"][1], inp["norm_mlp_post"][1]]
    for i, v in enumerate(cols):
        cst[:, i * 8:(i + 1) * 8] = pack_cols(v)
    cst[:, 64:70] = pack_cols(inp["a_q_norm"][0])
    cst[:, 70:72] = pack_cols(inp["a_kv_norm"][0])
    cw = np.asarray(inp["a_conv_w"][0], np.float32)
    for cc in range(4):
        for k in range(3):
            cst[:, 72 + cc * 3 + k] = cw[k, cc * 128:(cc + 1) * 128]
    c["cstv"] = cst
    half = 16
    inv = (10000.0 ** (-np.arange(half, dtype=np.float32) / half)).astype(np.float32)
    ang = np.arange(S, dtype=np.float32)[None, :] * inv[:, None]
    cs, sn = np.cos(ang).astype(np.float32), np.sin(ang).astype(np.float32)
    Cm = np.concatenate([cs, cs], 0)
    Sm = np.concatenate([-sn, sn], 0)
    c["ropecs"] = np.ascontiguousarray(np.concatenate([Cm, Sm, Cm, Sm], 0))
    w = np.asarray(inp["a_w_in"][0], np.float32)
    c["a_w_in"] = np.ascontiguousarray(np.concatenate([w, w[:, 2576:2592], w[:, 2560:2576]], 1))
    wq = np.asarray(inp["a_w_q_up"][0], np.float32)
    cols = []
    for h in range(8):
        b = h * 96
        cols += [wq[:, b:b + 96], wq[:, b + 80:b + 96], wq[:, b + 64:b + 80]]
    c["a_w_q"] = np.ascontiguousarray(np.concatenate(cols, 1))
    wkv = np.asarray(inp["a_w_kv_up"][0], np.float32).reshape(256, 8, 128)
    c["a_w_kv"] = np.ascontiguousarray(np.concatenate([wkv[:, :, :64].reshape(256, 512), wkv[:, :, 64:].reshape(256, 512)], 1))
    NEGM = -30000.0
    cm = np.zeros((4, 128, NT), np.float32)
    ii = np.arange(128)[:, None]
    jj = np.arange(NT)[None, :]
    for off in range(4):
        cm[off] = np.where(ii + 128 * off <= jj, 0.0, NEGM)
    c["cmask"] = cm.astype(ml_dtypes.bfloat16)
    c["identv"] = np.eye(128, dtype=np.float32).astype(ml_dtypes.bfloat16)
    c["a_w_out"] = np.ascontiguousarray(inp["a_w_out"][0], np.float32)
    c["w1_0"] = np.ascontiguousarray(inp["mlp_w1"][0], np.float32)
    c["w2_0"] = np.ascontiguousarray(inp["mlp_w2"][0], np.float32)
    bf = ml_dtypes.bfloat16
    cw = np.asarray(inp["c_w_in"][0], np.float32)
    c["c_w_in"] = np.ascontiguousarray(np.concatenate(
        [cw[:, 0:1536], cw[:, 1536:1792], cw[:, 2048:2304], cw[:, 1792:2048], cw[:, 2304:2560], cw[:, 2560:2608]], 1))
    for sfx in ("k", "v"):
        w1 = np.asarray(inp["c_cmp_w1_" + sfx][0], np.float32)
        c["cw1" + sfx] = np.ascontiguousarray(w1.transpose(1, 0, 2).reshape(64, 2048))
        c["cw2" + sfx] = np.ascontiguousarray(inp["c_cmp_w2_" + sfx][0], np.float32)
        c["cpe" + sfx] = np.ascontiguousarray(np.asarray(inp["c_cmp_pe_" + sfx][0], np.float32).T)

    slopes = [float(np.float32(2.0 ** (-8.0 * (h + 1) / 16))) for h in range(16)]
    qaug = np.zeros((16, 1, S), np.float32)
    qterm = np.zeros((16, 64, NT), np.float32)
    btab = np.zeros((128, 704), np.float64)
    pp = np.arange(128, dtype=np.float64)
    for h in range(16):
        sp = 8.0 * slopes[h]
        qaug[h, 0] = -sp * (np.arange(S) % NT)
        qterm[h] = (-sp * np.arange(NT))[None, :]
        for dlt in range(-28, 4):
            btab[:, h * 32 + dlt + 28] = slopes[h] * (pp + 128.0 * dlt)
        for T in range(8):
            btab[:, 512 + h * 12 + T] = slopes[h] * (16.0 * pp + 15.5 - 512.0 * T)
        for T in range(4, 8):
            btab[:, 512 + h * 12 + 8 + (T - 4)] = slopes[h] * (16.0 * pp + 2048.0 + 15.5 - 512.0 * T)
    c["qaug"] = qaug.astype(bf)
    c["qterm"] = qterm
    c["biastab"] = btab.astype(np.float32)
    ii = np.arange(128)[:, None]
    jj2 = np.arange(NT)[None, :]
    wm = np.zeros((4, 128, NT), np.float32)
    cpm = np.zeros((5, 128, NT), np.float32)
    for o in range(4):
        wm[o] = np.where(ii > jj2 - 128 * o, 0.0, NEGM)
    for o in range(5):
        cpm[o] = np.where(16 * ii + 31 <= 512 * o + jj2, 0.0, NEGM)
    c["wmask"] = wm.astype(bf)
    c["cmpmask"] = cpm.astype(bf)
    c["expand"] = (np.arange(64)[:, None] == (np.arange(S)[None, :] // 64)).astype(np.float32).astype(bf)
    selm = np.zeros((48, 48, 64), np.float32)
    for k in range(48):
        selm[k, k, :] = 1.0
    c["selm"] = selm.reshape(48, 48 * 64).astype(bf)
    starts = np.arange(255) * 16
    ss = np.arange(64) * 64
    ov = np.clip(np.minimum(starts[:, None] + 32, ss[None, :] + 64) - np.maximum(starts[:, None], ss[None, :]), 0, None) / 32.0
    ov2 = np.zeros((256, 65), np.float32)
    ov2[:255, :64] = ov
    ov2[:255, 64] = 1.0
    c["ov2"] = np.ascontiguousarray(ov2.reshape(2, 128, 65).transpose(1, 0, 2)).astype(bf)
    t = np.arange(S)
    cur = t // 64
    jb = np.arange(64)[None, :]
    forced = (jb == 0) | (jb == cur[:, None]) | (jb == cur[:, None] - 1)
    add = np.where(jb > cur[:, None], -1.0e9, 1.0e4 * forced).astype(np.float32)
    c["addtab"] = np.ascontiguousarray(add.reshape(32, 128, 64).transpose(1, 0, 2))
    c["c_w_out"] = np.ascontiguousarray(inp["c_w_out"][0], np.float32)
    c["w1_1"] = np.ascontiguousarray(inp["mlp_w1"][1], np.float32)
    c["w2_1"] = np.ascontiguousarray(inp["mlp_w2"][1], np.float32)
    return c


def kernel(**inp):
    x = np.asarray(inp["x"], np.float32)
    c = host_consts(inp)
    nc = build_program()
    in_maps = []
    for b in range(8):
        m = dict(c)
        m["xT"] = np.ascontiguousarray(x[b].T)
        in_maps.append(m)
    res = run_bass_kernel_spmd(nc, in_maps, core_ids=list(range(8)))
    out = np.stack([np.ascontiguousarray(r["outT"].T) for r in res.results], 0)
    return out.astype(np.float32)
```

```python
import numpy as np
import ml_dtypes
from contextlib import ExitStack
import concourse.bass as bass
import concourse.mybir as mybir
from concourse.bass_utils import run_bass_kernel_spmd

F32 = mybir.dt.float32
BF16 = mybir.dt.bfloat16
ALU = mybir.AluOpType
AF = mybir.ActivationFunctionType

S = 4096
D = 1024
EPS = 1e-6
NT = 512
NTT = S // NT


class Trk:
    __slots__ = ("w", "r", "multi")

    def __init__(self, multi=False):
        self.w = {}
        self.r = {}
        self.multi = multi


class Buf:
    def __init__(self, t, multi=False, psum=False):
        self.t = t
        self.k = Trk(multi)
        self.psum = psum

    def __getitem__(self, idx):
        return self.t[idx]


class Prog:
    def __init__(self, nc):
        self.nc = nc
        self.e = {"pe": nc.tensor, "act": nc.scalar, "dve": nc.vector, "pool": nc.gpsimd, "sp": nc.sync}
        self.esem = {}
        self.dsem = {}
        self.seen = {k: {} for k in self.e}
        self.nsem = 0
        self.semtotal = {}
        self.ninst = 0
        self.bank_i = 0

    def _newsem(self, name):
        self.nsem += 1
        return (self.nsem, self.nc.alloc_semaphore("%s_%d" % (name, self.nsem)))

    def _wait(self, eng, deps):
        for key, (sem, val, src) in deps.items():
            if src == "pe" and eng == "pe":
                continue
            if src == "dma":
                val = max(val, self.semtotal[key])
            if self.seen[eng].get(key, 0) >= val:
                continue
            self.e[eng].wait_ge(sem, val)
            self.ninst += 1
            self.seen[eng][key] = val

    @staticmethod
    def _add(deps, d):
        for k, t in d.items():
            if k not in deps or deps[k][1] < t[1]:
                deps[k] = t

    def _deps(self, reads, writes, acc=False):
        deps = {}
        for b in reads:
            self._add(deps, b.k.w)
            if getattr(b, "psum", False):
                self._add(deps, b.k.r)
        for b in writes:
            self._add(deps, b.k.r)
            if not (b.k.multi or acc):
                self._add(deps, b.k.w)
        return deps

    def _commit(self, key, tok, reads, writes, acc=False):
        for b in reads:
            b.k.r[key] = tok
        for b in writes:
            if b.k.multi or acc:
                b.k.w[key] = tok
            else:
                b.k.w = {key: tok}
            b.k.r = {}

    def op(self, eng, fn, reads=(), writes=()):
        self._wait(eng, self._deps(reads, writes))
        ins = fn(self.e[eng])
        st = self.esem.get(eng)
        if st is None or st[2] >= 30000:
            k, sem = self._newsem("e" + eng)
            st = [k, sem, 0]
            self.esem[eng] = st
        st[2] += 1
        ins.then_inc(st[1], 1)
        self.ninst += 1
        self._commit(st[0], (st[1], st[2], eng), reads, writes)

    def dma(self, q, out, in_, reads, writes, chan, acc=False):
        self._wait(q, self._deps(reads, writes, acc))
        st = self.dsem.get(chan)
        if st is None or st[2] >= 30000:
            k, sem = self._newsem("d")
            st = [k, sem, 0]
            self.dsem[chan] = st
        ins = self.e[q].dma_start(out=out, in_=in_)
        st[2] += 16
        self.semtotal[st[0]] = st[2]
        ins.then_inc(st[1], 16)
        self.ninst += 1
        self._commit(st[0], (st[1], st[2], "dma"), reads, writes, acc)

    def barrier(self):
        toks = {}
        for eng, st in self.esem.items():
            toks[st[0]] = (st[1], st[2], eng)
        for ch, st in self.dsem.items():
            toks[st[0]] = (st[1], st[2], "dma")
        for eng in self.e:
            self._wait(eng, toks)

    def final_wait(self):
        toks = {}
        for ch, st in self.dsem.items():
            toks[st[0]] = (st[1], st[2], "dma")
        self._wait("sp", toks)


class Ctx:
    uid = 0

    def __init__(self, P):
        self.P = P
        self.nc = P.nc
        self.st = ExitStack()
        self.n = 0

    def sb(self, shape, dt, name="t"):
        Ctx.uid += 1
        t = self.st.enter_context(self.nc.sbuf_tensor("%s_%d" % (name, Ctx.uid), list(shape), dt))
        return Buf(t)

    def close(self):
        self.P.barrier()
        self.st.close()


def load_w_bf16(P, C, dram_ap, dst, kc, ncols, stage=None, engs=None):
    w3 = dram_ap.rearrange("(k p) c -> p k c", p=128)
    for k in range(kc):
        if stage is not None:
            P.dma("pool", dst[:, k, :], w3[:, k, :], [], [stage[k]], chan=("w", id(dst), k % 4))
        else:
            P.dma("pool", dst[:, k, :], w3[:, k, :], [], [dst], chan=("w", id(dst)), acc=True)


class Common:
    pass


def rstd_from_sumsq(P, G, bank, rstd, n_feat):
    P.op("act", lambda e: e.activation(out=rstd[:, :], in_=bank[:, :], func=AF.Sqrt, bias=G.eps[:, 0:1],
                                        scale=1.0 / n_feat), [bank, G.eps], [rstd])
    P.op("dve", lambda e: e.reciprocal(out=rstd[:, :], in_=rstd[:, :]), [rstd], [rstd])


def sumsq_bcast(P, G, src, sq, kc, bank, nparts=128):
    P.op("act", lambda e: e.activation(out=sq[:, 0:kc, :], in_=src[:, 0:kc, :], func=AF.Square), [src], [sq])
    for k in range(kc):
        P.op("pe", lambda e, k=k: e.matmul(bank[:, :], lhsT=G.ones[:, :], rhs=sq[:, k, :], start=(k == 0),
                                            stop=(k == kc - 1)), [sq, G.ones], [bank])


def phase_front_a(P, G):
    nc = P.nc
    C = Ctx(P)
    NIN = 2624
    win = C.sb([128, 8, NIN], BF16, "win")
    wq = C.sb([128, 6, 1024], BF16, "wq")
    wkv = C.sb([128, 2, 1024], BF16, "wkv")
    stage = None
    load_w_bf16(P, C, G.d["a_w_in"], win, 8, NIN, stage)
    load_w_bf16(P, C, G.d["a_w_q"], wq, 6, 1024, stage)
    load_w_bf16(P, C, G.d["a_w_kv"], wkv, 2, 1024, stage)
    xts = [C.sb([128, 8, NT], F32, "xt") for _ in range(2)]
    sqxs = [C.sb([128, 8, NT], BF16, "sqx") for _ in range(2)]
    hTs = [C.sb([128, 8, NT], BF16, "hT") for _ in range(2)]
    rstdxs = [C.sb([128, NT], F32, "rstdx") for _ in range(2)]
    sq = C.sb([128, 8, NT], BF16, "sq")
    rstd = C.sb([128, NT], F32, "rstd")
    rope = C.sb([128, NT], F32, "rope")
    u = [[C.sb([128, NT + 2], F32, "u") for _ in range(2)] for _ in range(4)]
    hvss = [C.sb([128, NT], F32, "hvs") for _ in range(2)]
    c1s = [C.sb([128, NT], F32, "c1") for _ in range(2)]
    yc = [C.sb([128, NT], BF16, "yc") for _ in range(2)]
    cq = C.sb([128, 6, NT], F32, "cq")
    cqn = C.sb([128, 6, NT], BF16, "cqn")
    ckv = C.sb([128, 2, NT], F32, "ckv")
    ckvn = C.sb([128, 2, NT], BF16, "ckvn")
    t1s = [C.sb([128, NT], F32, "t1") for _ in range(2)]
    t2s = [C.sb([128, NT], F32, "t2") for _ in range(2)]
    sqkv = C.sb([128, 2, NT], BF16, "sqkv")
    rstdkv = C.sb([128, NT], F32, "rstdkv")
    kr = C.sb([32, NT], BF16, "kr")
    qt = [C.sb([128, NT], BF16, "qt") for _ in range(2)]
    kn = [C.sb([128, NT], BF16, "kn") for _ in range(2)]
    vt = [C.sb([128, NT], BF16, "vt") for _ in range(2)]
    cst = G.cst
    xT3 = G.d["xT"].rearrange("(k p) t -> p k t", p=128)
    for cc in range(4):
        P.op("pool", lambda e, cc=cc: e.memset(u[cc][0][:, 0:2], 0.0), [], [u[cc][0]])
    ev = [0]

    def evac(out_ap, bank, in_ap, reads, writes):
        ev[0] += 1
        if ev[0] % 2:
            P.op("act", lambda e: e.activation(out=out_ap, in_=in_ap, func=AF.Copy), reads, writes)
        else:
            P.op("dve", lambda e: e.tensor_copy(out=out_ap, in_=in_ap), reads, writes)

    cur = {}

    def mm_chunk(bank, col0, m, rows=slice(0, 128)):
        hT = cur["hT"]
        for k in range(8):
            P.op("pe", lambda e, k=k: e.matmul(bank[0:m, :], lhsT=win[:, k, col0:col0 + m], rhs=hT[:, k, :],
                                                start=(k == 0), stop=(k == 7)), [win, hT], [bank])

    def load_x(ti):
        ts = slice(ti * NT, (ti + 1) * NT)
        xt = xts[ti % 2]
        P.dma("sp", xt[:, :, :], xT3[:, :, ts], [G.dr["xT"]], [xt], chan=("xt", ti % 2))

    def prenorm(ti):
        xt, sqx, hT, rstdx = xts[ti % 2], sqxs[ti % 2], hTs[ti % 2], rstdxs[ti % 2]
        bA = G.bank()
        sumsq_bcast(P, G, xt, sqx, 8, bA)
        rstd_from_sumsq(P, G, bA, rstdx, 1024)
        for k in range(8):
            P.op("dve", lambda e, k=k: e.scalar_tensor_tensor(out=hT[:, k, :], in0=xt[:, k, :],
                                                               scalar=cst[:, G.co["gpre0"] + k:G.co["gpre0"] + k + 1],
                                                               in1=rstdx[:, :], op0=ALU.mult, op1=ALU.mult),
                 [xt, rstdx, cst], [hT])

    load_x(0)
    prenorm(0)
    for ti in range(NTT):
        ts = slice(ti * NT, (ti + 1) * NT)
        cur["hT"] = hTs[ti % 2]
        if ti + 1 < NTT:
            load_x(ti + 1)
        P.dma("sp", rope[:, :], G.d["ropecs"][:, ts], [], [rope], chan="rope")
        for cc in range(4):
            b0, b1, b2 = G.bank(), G.bank(), G.bank()
            mm_chunk(b0, cc * 128, 128)
            mm_chunk(b1, 512 + cc * 128, 128)
            mm_chunk(b2, 1024 + cc * 128, 128)
            uc, un = u[cc][ti % 2], u[cc][(ti + 1) % 2]
            hvs, c1 = hvss[cc % 2], c1s[cc % 2]
            P.op("act", lambda e: e.activation(out=hvs[:, :], in_=b2[:, :], func=AF.Copy), [b2], [hvs])
            P.op("dve", lambda e: e.tensor_tensor(out=uc[:, 2:NT + 2], in0=b1[:, :], in1=hvs[:, :], op=ALU.mult),
                 [b1, hvs], [uc])
            cw = G.co["convw"] + cc * 3
            P.op("dve", lambda e: e.tensor_scalar(out=c1[:, :], in0=uc[:, 0:NT], scalar1=cst[:, cw:cw + 1],
                                                  scalar2=None, op0=ALU.mult), [uc, cst], [c1])
            P.op("dve", lambda e: e.scalar_tensor_tensor(out=c1[:, :], in0=uc[:, 1:NT + 1], scalar=cst[:, cw + 1:cw + 2],
                                                         in1=c1[:, :], op0=ALU.mult, op1=ALU.add), [uc, cst, c1], [c1])
            P.op("dve", lambda e: e.scalar_tensor_tensor(out=c1[:, :], in0=uc[:, 2:NT + 2], scalar=cst[:, cw + 2:cw + 3],
                                                         in1=c1[:, :], op0=ALU.mult, op1=ALU.add), [uc, cst, c1], [c1])
            y = yc[cc % 2]
            P.op("dve", lambda e: e.tensor_tensor(out=y[:, :], in0=b0[:, :], in1=c1[:, :], op=ALU.mult), [b0, c1], [y])
            P.op("pool", lambda e: e.tensor_copy(out=un[:, 0:2], in_=uc[:, NT:NT + 2]), [uc], [un])
            P.dma("pool", G.d["ycatT"][cc * 128:(cc + 1) * 128, ts], y[:, :], [y], [G.dr["ycatT"]], chan=("yc", cc % 2))
        if ti + 1 < NTT:
            prenorm(ti + 1)
        for j in range(2):
            b = G.bank()
            mm_chunk(b, 2304 + j * 128, 128)
            evac(ckv[:, j, :], b, b[:, :], [b], [ckv])
        for j in range(6):
            b = G.bank()
            mm_chunk(b, 1536 + j * 128, 128)
            evac(cq[:, j, :], b, b[:, :], [b], [cq])
        bk = G.bank()
        sumsq_bcast(P, G, ckv, sqkv, 2, bk)
        rstd_from_sumsq(P, G, bk, rstdkv, 256)
        for j in range(2):
            P.op("dve", lambda e, j=j: e.scalar_tensor_tensor(out=ckvn[:, j, :], in0=ckv[:, j, :],
                                                               scalar=cst[:, G.co["kvn"] + j:G.co["kvn"] + j + 1],
                                                               in1=rstdkv[:, :], op0=ALU.mult, op1=ALU.mult),
                 [ckv, rstdkv, cst], [ckvn])
        bq = G.bank()
        sumsq_bcast(P, G, cq, sq, 6, bq)
        rstd_from_sumsq(P, G, bq, rstd, 768)
        for j in range(6):
            P.op("dve", lambda e, j=j: e.scalar_tensor_tensor(out=cqn[:, j, :], in0=cq[:, j, :],
                                                               scalar=cst[:, G.co["qn"] + j:G.co["qn"] + j + 1],
                                                               in1=rstd[:, :], op0=ALU.mult, op1=ALU.mult),
                 [cq, rstd, cst], [cqn])
        t1, t2 = t1s[0], t2s[0]
        b = G.bank()
        mm_chunk(b, 2560, 64)
        P.op("dve", lambda e: e.tensor_tensor(out=t1[0:32, :], in0=b[0:32, :], in1=rope[0:32, :], op=ALU.mult),
             [b, rope], [t1])
        P.op("dve", lambda e: e.tensor_tensor(out=t2[0:32, :], in0=b[32:64, :], in1=rope[32:64, :], op=ALU.mult),
             [b, rope], [t2])
        P.op("dve", lambda e: e.tensor_tensor(out=kr[0:32, :], in0=t1[0:32, :], in1=t2[0:32, :], op=ALU.add),
             [t1, t2], [kr])
        for h in range(8):
            P.dma("pool", G.d["kT"][h, 64:96, ts], kr[0:32, :], [kr], [G.dr["kT"]], chan="kr")
        for hp in range(4):
            b = G.bank()
            for j in range(2):
                P.op("pe", lambda e, j=j: e.matmul(b[:, :], lhsT=wkv[:, j, hp * 128:(hp + 1) * 128], rhs=ckvn[:, j, :],
                                                    start=(j == 0), stop=(j == 1)), [wkv, ckvn], [b])
            kk = kn[hp % 2]
            evac(kk[:, :], b, b[:, :], [b], [kk])
            P.dma("pool", G.d["kT"][2 * hp, 0:64, ts], kk[0:64, :], [kk], [G.dr["kT"]], chan=("kn", hp % 2))
            P.dma("pool", G.d["kT"][2 * hp + 1, 0:64, ts], kk[64:128, :], [kk], [G.dr["kT"]], chan=("kn", hp % 2))
        for tb in range(4):
            b = G.bank()
            for j in range(2):
                P.op("pe", lambda e, j=j: e.matmul(b[:, :], lhsT=ckvn[:, j, tb * 128:(tb + 1) * 128],
                                                    rhs=wkv[:, j, 512:1024], start=(j == 0), stop=(j == 1)),
                     [wkv, ckvn], [b])
            vv = vt[tb % 2]
            evac(vv[:, :], b, b[:, :], [b], [vv])
            r0 = ti * NT + tb * 128
            P.dma("pool", G.d["vtok"][r0:r0 + 128, :], vv[:, :], [vv], [G.dr["vtok"]], chan=("vt", tb % 2))
        for h in range(8):
            b = G.bank()
            for j in range(6):
                P.op("pe", lambda e, j=j: e.matmul(b[:, :], lhsT=wq[:, j, h * 128:(h + 1) * 128], rhs=cqn[:, j, :],
                                                    start=(j == 0), stop=(j == 5)), [wq, cqn], [b])
            q = qt[h % 2]
            t1, t2 = t1s[h % 2], t2s[h % 2]
            P.op("act", lambda e: e.activation(out=q[0:64, :], in_=b[0:64, :], func=AF.Copy), [b], [q])
            P.op("dve", lambda e: e.tensor_tensor(out=t1[64:96, :], in0=b[64:96, :], in1=rope[64:96, :], op=ALU.mult),
                 [b, rope], [t1])
            P.op("dve", lambda e: e.tensor_tensor(out=t2[64:96, :], in0=b[96:128, :], in1=rope[96:128, :], op=ALU.mult),
                 [b, rope], [t2])
            P.op("dve", lambda e: e.tensor_tensor(out=q[64:96, :], in0=t1[64:96, :], in1=t2[64:96, :], op=ALU.add),
                 [t1, t2], [q])
            P.dma("pool", G.d["qT"][h, :, ts], q[0:96, :], [q], [G.dr["qT"]], chan=("qt", h % 2))
    C.close()


def phase_mla_attn(P, G):
    C = Ctx(P)
    scale = 96 ** -0.5
    QT = [C.sb([96, S], BF16, "QT") for _ in range(2)]
    KT = [C.sb([96, S], BF16, "KT") for _ in range(2)]
    V = [C.sb([128, 32, 128], BF16, "V") for _ in range(2)]
    E = [C.sb([128, NT], BF16, "E") for _ in range(3)]
    cm = C.sb([128, 4, NT], BF16, "cm")
    rz = C.sb([64, NT], F32, "rz")
    yo = [C.sb([64, NT], BF16, "yo") for _ in range(2)]
    P.dma("sp", cm[:, :, :], G.d["cmask"].rearrange("o p n -> p o n"), [], [cm], chan="cm")
    for i in range(2):
        P.op("pool", lambda e: e.memset(V[i][:, :, 64:128], 1.0), [], [V[i]])
    sb = G.banks[0:3]
    ob = G.banks[4:6]
    vsrc = G.d["vtok"].rearrange("(c p) f -> p c f", p=128)
    its = [(T, c) for T in range(NTT) for c in range(4 * T + 4)]
    n_o = [0]

    def load(h):
        P.dma("sp", QT[h % 2][:, :], G.d["qT"][h, :, :], [G.dr["qT"]], [QT[h % 2]], chan=("QT", h % 2))
        P.dma("sp", KT[h % 2][:, :], G.d["kT"][h, :, :], [G.dr["kT"]], [KT[h % 2]], chan=("KT", h % 2))
        P.dma("sp", V[h % 2][:, :, 0:64], vsrc[:, :, h * 64:(h + 1) * 64], [G.dr["vtok"]], [V[h % 2]],
              chan=("V", h % 2))

    load(0)
    for h in range(8):
        if h + 1 < 8:
            load(h + 1)
        q, k, v = QT[h % 2], KT[h % 2], V[h % 2]

        def emit_s(n):
            T, c = its[n]
            b = sb[n % 3]
            diag = c >= 4 * T
            j0 = 128 * (c - 4 * T) if diag else 0
            P.op("pe", lambda e: e.matmul(b[:, j0:NT], lhsT=k[0:96, c * 128:(c + 1) * 128],
                                          rhs=q[0:96, T * NT + j0:(T + 1) * NT], start=True, stop=not diag), [k, q], [b])
            if diag:
                P.op("pe", lambda e: e.matmul(b[:, j0:NT], lhsT=G.ident[:, :], rhs=cm[:, c - 4 * T, j0:NT], start=False,
                                              stop=True), [G.ident, cm], [b])

        emit_s(0)
        emit_s(1)
        for n in range(len(its)):
            T, c = its[n]
            if n + 2 < len(its):
                emit_s(n + 2)
            b = sb[n % 3]
            ee = E[n % 3]
            j0 = 128 * (c - 4 * T) if c >= 4 * T else 0
            P.op("act", lambda e: e.activation(out=ee[:, j0:NT], in_=b[:, j0:NT], func=AF.Exp, scale=scale), [b], [ee])
            o = ob[T % 2]
            last = (c == 4 * T + 3)
            P.op("pe", lambda e: e.matmul(o[:, j0:NT], lhsT=v[:, c, :], rhs=ee[:, j0:NT], start=(c == 0), stop=last,
                                          skip_group_check=True), [v, ee], [o])
            if last:
                y = yo[n_o[0] % 2]
                n_o[0] += 1
                P.op("dve", lambda e: e.reciprocal(out=rz[0:64, :], in_=o[64:128, :]), [o], [rz])
                P.op("dve", lambda e: e.tensor_tensor(out=y[0:64, :], in0=o[0:64, :], in1=rz[0:64, :], op=ALU.mult),
                     [o, rz], [y])
                P.dma("pool", G.d["ycatT"][512 + h * 64:512 + (h + 1) * 64, T * NT:(T + 1) * NT], y[0:64, :], [y],
                      [G.dr["ycatT"]], chan=("yo", (n_o[0] - 1) % 2))
    C.close()


def phase_outproj(P, G, wname, yname, xin, xout, hout, gpost, gmpre):
    C = Ctx(P)
    wo = C.sb([128, 8, 1024], BF16, "wo")
    load_w_bf16(P, C, G.d[wname], wo, 8, 1024)
    yt = [C.sb([128, 8, NT], BF16, "yt") for _ in range(2)]
    xts = [C.sb([128, 8, NT], F32, "xt") for _ in range(2)]
    ms = [C.sb([128, 8, NT], F32, "m") for _ in range(2)]
    sqs = [C.sb([128, 8, NT], BF16, "sq")] * 2
    sq2s = [C.sb([128, 8, NT], BF16, "sq2")] * 2
    rstds = [C.sb([128, NT], F32, "rstd") for _ in range(2)]
    rstd2s = [C.sb([128, NT], F32, "rstd2") for _ in range(2)]
    h2s = [C.sb([128, 8, NT], BF16, "h2")] * 2
    def chunked(bufs):
        first = [Buf(bufs[0].t) for _ in range(8)]
        return [first, first if bufs[1] is bufs[0] else [Buf(bufs[1].t) for _ in range(8)]]
    xtc, mc, sqc, sq2c, h2c = chunked(xts), chunked(ms), chunked(sqs), chunked(sq2s), chunked(h2s)
    cst = G.cst
    y3 = G.d[yname].rearrange("(k p) t -> p k t", p=128)
    x3 = G.d[xin].rearrange("(k p) t -> p k t", p=128)
    xo3 = G.d[xout].rearrange("(k p) t -> p k t", p=128)
    ho3 = G.d[hout].rearrange("(k p) t -> p k t", p=128)
    mb = G.banks[0:6]
    bAs = G.banks[6]
    bBs = G.banks[7]
    nb = [0]

    def ones_mm(bank, sq, sqk, k):
        P.op("pe", lambda e: e.matmul(bank[:, :], lhsT=G.ones[:, :], rhs=sq[:, k, :], start=(k == 0), stop=(k == 7)),
             [sqk[k], G.ones], [bank])

    def rstd_ops(bank, rstd):
        P.op("act", lambda e: e.activation(out=rstd[:, :], in_=bank[:, :], func=AF.Sqrt, bias=G.eps[:, 0:1], scale=1.0 / 1024),
             [bank, G.eps], [rstd])
        P.op("dve", lambda e: e.reciprocal(out=rstd[:, :], in_=rstd[:, :]), [rstd], [rstd])

    def epi1_step(tp, k):
        xt, m, sq2, rstd = xts[tp % 2], ms[tp % 2], sq2s[tp % 2], rstds[tp % 2]
        mk, xk, s2k = mc[tp % 2][k], xtc[tp % 2][k], sq2c[tp % 2][k]
        P.op("dve", lambda e: e.scalar_tensor_tensor(out=m[:, k, :], in0=m[:, k, :],
                                                     scalar=cst[:, G.co[gpost] + k:G.co[gpost] + k + 1],
                                                     in1=rstd[:, :], op0=ALU.mult, op1=ALU.mult), [mk, rstd, cst], [mk])
        P.op("pool", lambda e: e.tensor_tensor(out=xt[:, k, :], in0=xt[:, k, :], in1=m[:, k, :], op=ALU.add), [xk, mk], [xk])
        P.op("act", lambda e: e.activation(out=sq2[:, k, :], in_=xt[:, k, :], func=AF.Square), [xk], [s2k])

    def epi2(tp):
        ts = slice(tp * NT, (tp + 1) * NT)
        xt, h2, rstd2 = xts[tp % 2], h2s[tp % 2], rstd2s[tp % 2]
        P.dma("pool", xo3[:, :, ts], xt[:, :, :], xtc[tp % 2], [G.dr[xout]], chan=("opxo", tp % 2))
        rstd_ops(bBs, rstd2)
        for k in range(8):
            P.op("dve", lambda e: e.scalar_tensor_tensor(out=h2[:, k, :], in0=xt[:, k, :],
                                                         scalar=cst[:, G.co[gmpre] + k:G.co[gmpre] + k + 1],
                                                         in1=rstd2[:, :], op0=ALU.mult, op1=ALU.mult),
                 [xtc[tp % 2][k], rstd2, cst], [h2c[tp % 2][k]])
        P.dma("pool", ho3[:, :, ts], h2[:, :, :], h2c[tp % 2], [G.dr[hout]], chan=("opho", tp % 2))

    def main(ti, prev):
        if ti is not None:
            ts = slice(ti * NT, (ti + 1) * NT)
            y, xt, m, sq = yt[ti % 2], xts[ti % 2], ms[ti % 2], sqs[ti % 2]
            P.dma("sp", y[:, :, :], y3[:, :, ts], [G.dr[yname]], [y], chan=("opy", ti % 2))
            P.dma("sp", xt[:, :, :], x3[:, :, ts], [G.dr[xin]], xtc[ti % 2], chan=("opx", ti % 2))
        if prev is not None:
            ones_mm(bAs, sqs[prev % 2], sqc[prev % 2], 7)
            rstd_ops(bAs, rstds[prev % 2])
        for oc in range(8):
            if ti is not None:
                b = mb[nb[0] % 6]
                nb[0] += 1
                for k in range(8):
                    P.op("pe", lambda e: e.matmul(b[:, :], lhsT=wo[:, k, oc * 128:(oc + 1) * 128], rhs=y[:, k, :],
                                                  start=(k == 0), stop=(k == 7)), [wo, y], [b])
                P.op("dve", lambda e: e.tensor_copy(out=m[:, oc, :], in_=b[:, :]), [b], [mc[ti % 2][oc]])
                P.op("act", lambda e: e.activation(out=sq[:, oc, :], in_=m[:, oc, :], func=AF.Square), [mc[ti % 2][oc]],
                     [sqc[ti % 2][oc]])
                if oc > 0:
                    ones_mm(bAs, sq, sqc[ti % 2], oc - 1)
            if prev is not None:
                epi1_step(prev, oc)
                if oc > 1:
                    ones_mm(bBs, sq2s[prev % 2], sq2c[prev % 2], oc - 2)
        if prev is not None:
            ones_mm(bBs, sq2s[prev % 2], sq2c[prev % 2], 6)
            ones_mm(bBs, sq2s[prev % 2], sq2c[prev % 2], 7)
            epi2(prev)

    for ti in range(NTT):
        main(ti, ti - 1 if ti > 0 else None)
    main(None, NTT - 1)
    C.close()


def prefetch_w1(P, G, w1name):
    Cw = Ctx(P)
    W1 = Cw.sb([128, 8, 4096], BF16, "W1")
    load_w_bf16(P, Cw, G.d[w1name], W1, 8, 4096)
    return Cw, W1


def phase_mlp(P, G, w1name, w2name, hin, xin, xout, hout, gpost, gnext, W1=None):
    C = Ctx(P)
    MT = 256
    if W1 is None:
        W1 = C.sb([128, 8, 4096], BF16, "W1")
        load_w_bf16(P, C, G.d[w1name], W1, 8, 4096)
    W2 = C.sb([128, 32, 1024], BF16, "W2")
    W2c = [Buf(W2.t) for _ in range(32)]
    load_w_bf16(P, C, G.d[w2name], W2, 32, 1024, stage=W2c)
    ht = [C.sb([128, 8, MT], BF16, "ht") for _ in range(2)]
    xts = [C.sb([128, 8, MT], F32, "xt") for _ in range(2)]
    mos = [C.sb([128, 8, MT], F32, "mo") for _ in range(2)]
    sqs = [C.sb([128, 8, MT], BF16, "sq") for _ in range(2)]
    rstds = [C.sb([128, MT], F32, "rstd") for _ in range(2)]
    hns = [C.sb([128, 8, MT], BF16, "hn") for _ in range(2)]
    rl = [C.sb([128, MT], F32, "rl") for _ in range(2)]
    hid = [C.sb([128, MT], BF16, "hid") for _ in range(3)]
    cst = G.cst
    h3 = G.d[hin].rearrange("(k p) t -> p k t", p=128)
    x3 = G.d[xin].rearrange("(k p) t -> p k t", p=128)
    xo3 = G.d[xout].rearrange("(k p) t -> p k t", p=128)
    hb = G.banks[0:3]
    eb = G.banks[3]
    obk = G.banks[4:8]
    ntile = S // MT

    def epilogue_sq(ti):
        mo, sq = mos[ti % 2], sqs[ti % 2]
        P.op("act", lambda e: e.activation(out=sq[:, :, :], in_=mo[:, :, :], func=AF.Square), [mo], [sq])

    def epilogue(ti):
        ts = slice(ti * MT, (ti + 1) * MT)
        xt, mo, sq, rstd, hn = xts[ti % 2], mos[ti % 2], sqs[ti % 2], rstds[ti % 2], hns[ti % 2]
        for k in range(8):
            P.op("pe", lambda e: e.matmul(eb[:, 0:MT], lhsT=G.ones[:, :], rhs=sq[:, k, :], start=(k == 0), stop=(k == 7)),
                 [sq, G.ones], [eb])
        P.op("act", lambda e: e.activation(out=rstd[:, :], in_=eb[:, 0:MT], func=AF.Sqrt, bias=G.eps[:, 0:1], scale=1.0 / 1024),
             [eb, G.eps], [rstd])
        P.op("dve", lambda e: e.reciprocal(out=rstd[:, :], in_=rstd[:, :]), [rstd], [rstd])
        for k in range(8):
            P.op("dve", lambda e: e.scalar_tensor_tensor(out=mo[:, k, :], in0=mo[:, k, :],
                                                         scalar=cst[:, G.co[gpost] + k:G.co[gpost] + k + 1],
                                                         in1=rstd[:, :], op0=ALU.mult, op1=ALU.mult), [mo, rstd, cst], [mo])
        P.op("pool", lambda e: e.tensor_tensor(out=xt[:, :, :], in0=xt[:, :, :], in1=mo[:, :, :], op=ALU.add), [xt, mo], [xt])
        P.dma("pool", xo3[:, :, ts], xt[:, :, :], [xt], [G.dr[xout]], chan=("mlpxo", ti % 2))

    def epilogue_b(ti):
        ts = slice(ti * MT, (ti + 1) * MT)
        xt, mo, sq, rstd, hn = xts[ti % 2], mos[ti % 2], sqs[ti % 2], rstds[ti % 2], hns[ti % 2]
        if hout is not None:
            ho3 = G.d[hout].rearrange("(k p) t -> p k t", p=128)
            P.op("act", lambda e: e.activation(out=sq[:, :, :], in_=xt[:, :, :], func=AF.Square), [xt], [sq])
            for k in range(8):
                P.op("pe", lambda e: e.matmul(eb[:, 0:MT], lhsT=G.ones[:, :], rhs=sq[:, k, :], start=(k == 0), stop=(k == 7)),
                     [sq, G.ones], [eb])
            P.op("act", lambda e: e.activation(out=rstd[:, :], in_=eb[:, 0:MT], func=AF.Sqrt, bias=G.eps[:, 0:1],
                                               scale=1.0 / 1024), [eb, G.eps], [rstd])
            P.op("dve", lambda e: e.reciprocal(out=rstd[:, :], in_=rstd[:, :]), [rstd], [rstd])
            for k in range(8):
                P.op("dve", lambda e: e.scalar_tensor_tensor(out=hn[:, k, :], in0=xt[:, k, :],
                                                             scalar=cst[:, G.co[gnext] + k:G.co[gnext] + k + 1],
                                                             in1=rstd[:, :], op0=ALU.mult, op1=ALU.mult), [xt, rstd, cst], [hn])
            P.dma("pool", ho3[:, :, ts], hn[:, :, :], [hn], [G.dr[hout]], chan=("mlpho", ti % 2))

    for ti in range(ntile):
        ts = slice(ti * MT, (ti + 1) * MT)
        h = ht[ti % 2]
        xt = xts[ti % 2]
        mo = mos[ti % 2]
        P.dma("sp", h[:, :, :], h3[:, :, ts], [G.dr[hin]], [h], chan=("mlph", ti % 2))
        P.dma("sp", xt[:, :, :], x3[:, :, ts], [G.dr[xin]], [xt], chan=("mlpx", ti % 2))

        def emit_h(f):
            b = hb[f % 3]
            for k in range(8):
                P.op("pe", lambda e: e.matmul(b[:, 0:MT], lhsT=W1[:, k, f * 128:(f + 1) * 128], rhs=h[:, k, :],
                                              start=(k == 0), stop=(k == 7)), [W1, h], [b])
        emit_h(0)
        emit_h(1)
        for f in range(32):
            if f + 2 < 32:
                emit_h(f + 2)
            b = hb[f % 3]
            r = rl[f % 2]
            hd = hid[f % 3]
            P.op("act", lambda e: e.activation(out=r[:, :], in_=b[:, 0:MT], func=AF.Relu), [b], [r])
            P.op("dve", lambda e: e.tensor_tensor(out=hd[:, :], in0=r[:, :], in1=r[:, :], op=ALU.mult), [r], [hd])
            for oc in range(8):
                o = obk[oc // 2]
                P.op("pe", lambda e: e.matmul(o[:, (oc % 2) * MT:(oc % 2 + 1) * MT], lhsT=W2[:, f, oc * 128:(oc + 1) * 128],
                                              rhs=hd[:, :], start=(f == 0 and oc % 2 == 0), stop=(f == 31),
                                              skip_group_check=True), [W2c[f], hd], [o])
            if f == 4 and ti > 0:
                epilogue_sq(ti - 1)
            if f == 7 and ti > 0:
                epilogue(ti - 1)
            if f == 18 and ti > 0:
                epilogue_b(ti - 1)
        for ob_i in range(4):
            o = obk[ob_i]
            if ob_i % 2:
                P.op("act", lambda e: e.activation(out=mo[:, 2 * ob_i:2 * ob_i + 2, :], in_=o[:, :].rearrange("p (a n) -> p a n", a=2),
                                                   func=AF.Copy), [o], [mo])
            else:
                P.op("dve", lambda e: e.tensor_copy(out=mo[:, 2 * ob_i:2 * ob_i + 2, :], in_=o[:, :].rearrange("p (a n) -> p a n", a=2)),
                     [o], [mo])
    epilogue_sq(ntile - 1)
    epilogue(ntile - 1)
    epilogue_b(ntile - 1)
    C.close()


def phase_front_c(P, G):
    C = Ctx(P)
    NIN = 2608
    win = C.sb([128, 8, NIN], BF16, "cwin")
    stage = None
    load_w_bf16(P, C, G.d["c_w_in"], win, 8, NIN, stage)
    ht = [C.sb([128, 8, NT], BF16, "ht") for _ in range(2)]
    ob = [C.sb([128, NT], BF16, "ob") for _ in range(3)]
    gt = C.sb([48, NT], BF16, "gt")
    h3 = G.d["h3T"].rearrange("(k p) t -> p k t", p=128)
    n = 0
    for ti in range(NTT):
        ts = slice(ti * NT, (ti + 1) * NT)
        h = ht[ti % 2]
        P.dma("sp", h[:, :, :], h3[:, :, ts], [G.dr["h3T"]], [h], chan=("fch", ti % 2))
        for c in range(16):
            b = G.bank()
            for k in range(8):
                P.op("pe", lambda e: e.matmul(b[:, :], lhsT=win[:, k, c * 128:(c + 1) * 128], rhs=h[:, k, :],
                                              start=(k == 0), stop=(k == 7)), [win, h], [b])
            o = ob[n % 3]
            if n % 2:
                P.op("act", lambda e: e.activation(out=o[:, :], in_=b[:, :], func=AF.Copy), [b], [o])
            else:
                P.op("dve", lambda e: e.tensor_copy(out=o[:, :], in_=b[:, :]), [b], [o])
            if c < 8:
                d0, d1, nm = G.d["qTn"][2 * c, :, ts], G.d["qTn"][2 * c + 1, :, ts], "qTn"
            else:
                wh, gp = (c - 8) // 2, (c - 8) % 2
                d0, d1, nm = G.d["kvT"][wh, 2 * gp, :, ts], G.d["kvT"][wh, 2 * gp + 1, :, ts], "kvT"
            P.dma("pool", d0, o[0:64, :], [o], [G.dr[nm]], chan=("fco", n % 3))
            P.dma("pool", d1, o[64:128, :], [o], [G.dr[nm]], chan=("fco", n % 3))
            n += 1
        b = G.bank()
        for k in range(8):
            P.op("pe", lambda e: e.matmul(b[0:48, :], lhsT=win[:, k, 2560:2608], rhs=h[:, k, :], start=(k == 0),
                                          stop=(k == 7)), [win, h], [b])
        P.op("act", lambda e: e.activation(out=gt[0:48, :], in_=b[0:48, :], func=AF.Sigmoid), [b], [gt])
        P.dma("pool", G.d["gT"][:, ts], gt[0:48, :], [gt], [G.dr["gT"]], chan="fcg")
        for tb in range(4):
            b = G.bank()
            for k in range(8):
                P.op("pe", lambda e: e.matmul(b[:, :], lhsT=h[:, k, tb * 128:(tb + 1) * 128], rhs=win[:, k, 2048:2560],
                                              start=(k == 0), stop=(k == 7)), [win, h], [b])
            o = ob[n % 3]
            if n % 2:
                P.op("act", lambda e: e.activation(out=o[:, :], in_=b[:, :], func=AF.Copy), [b], [o])
            else:
                P.op("dve", lambda e: e.tensor_copy(out=o[:, :], in_=b[:, :]), [b], [o])
            r0 = ti * NT + tb * 128
            P.dma("pool", G.d["vsw"][r0:r0 + 128, :], o[:, :], [o], [G.dr["vsw"]], chan=("fco", n % 3))
            n += 1
    C.close()


def phase_compress(P, G):
    C = Ctx(P)
    stage = C.sb([64, 2048], F32, "cstg")
    w1 = [C.sb([64, 2048], BF16, "cw1") for _ in range(2)]
    w2 = [C.sb([64, 64], BF16, "cw2") for _ in range(2)]
    peT = [C.sb([64, 32], BF16, "cpe") for _ in range(2)]
    bias = [C.sb([64, 1], F32, "cbias") for _ in range(2)]
    for kv, sfx in enumerate(("k", "v")):
        P.dma("sp", stage[:, :], G.d["cw1" + sfx][:, :], [], [stage], chan="cstg")
        P.op("dve", lambda e: e.tensor_copy(out=w1[kv][:, :], in_=stage[:, :]), [stage], [w1[kv]])
        P.dma("sp", stage[:, 0:64], G.d["cw2" + sfx][:, :], [], [stage], chan="cstg")
        P.op("dve", lambda e: e.tensor_copy(out=w2[kv][:, :], in_=stage[:, 0:64]), [stage], [w2[kv]])
        P.dma("sp", stage[:, 0:32], G.d["cpe" + sfx][:, :], [], [stage], chan="cstg")
        P.op("dve", lambda e: e.tensor_copy(out=peT[kv][:, :], in_=stage[:, 0:32]), [stage], [peT[kv]])
        b = G.bank()
        for l in range(32):
            P.op("pe", lambda e: e.matmul(b[0:64, 0:1], lhsT=w1[kv][:, l * 64:(l + 1) * 64], rhs=peT[kv][:, l:l + 1],
                                          start=(l == 0), stop=(l == 31)), [w1[kv], peT[kv]], [b])
        P.op("dve", lambda e: e.tensor_copy(out=bias[kv][:, :], in_=b[0:64, 0:1]), [b], [bias[kv]])
    zt = [C.sb([64, S], BF16, "zt") for _ in range(2)]
    xb = C.sb([64, 256], F32, "xb")
    tt = C.sb([64, 256], F32, "tt")
    sg = C.sb([64, 256], F32, "sg")
    hdn = C.sb([64, 256], BF16, "hdn")
    P.op("pool", lambda e: e.memset(hdn[:, :], 0.0), [], [hdn])
    n = 0
    for g in range(4):
        KC, VC = G.KC[g], G.VC[g]
        P.op("pool", lambda e: e.memset(KC[:, :], 0.0), [], [KC])
        P.op("pool", lambda e: e.memset(KC[64:65, :], 1.0), [], [KC])
        P.op("pool", lambda e: e.memset(VC[:, :, 0:64], 0.0), [], [VC])
        P.op("pool", lambda e: e.memset(VC[:, :, 64:128], 1.0), [], [VC])
        for kv in range(2):
            z = zt[n % 2]
            n += 1
            P.dma("sp", z[:, :], G.d["kvT"][kv, g, :, :], [G.dr["kvT"]], [z], chan=("zt", n % 2))
            b = G.bank()
            for l in range(32):
                P.op("pe", lambda e: e.matmul(b[0:64, 0:255], lhsT=w1[kv][:, l * 64:(l + 1) * 64],
                                              rhs=z[:, l:l + 16 * 254 + 1:16], start=(l == 0), stop=(l == 31)),
                     [w1[kv], z], [b])
            P.op("dve", lambda e: e.tensor_scalar(out=xb[:, 0:255], in0=b[0:64, 0:255], scalar1=bias[kv][:, 0:1],
                                                  scalar2=None, op0=ALU.add), [b, bias[kv]], [xb])
            P.op("dve", lambda e: e.tensor_tensor(out=tt[:, 0:255], in0=xb[:, 0:255], in1=xb[:, 0:255], op=ALU.mult),
                 [xb], [tt])
            P.op("dve", lambda e: e.tensor_scalar(out=tt[:, 0:255], in0=tt[:, 0:255], scalar1=0.044715, scalar2=1.0,
                                                  op0=ALU.mult, op1=ALU.add), [tt], [tt])
            P.op("dve", lambda e: e.tensor_tensor(out=tt[:, 0:255], in0=tt[:, 0:255], in1=xb[:, 0:255], op=ALU.mult),
                 [tt, xb], [tt])
            P.op("act", lambda e: e.activation(out=sg[:, 0:255], in_=tt[:, 0:255], func=AF.Sigmoid,
                                               scale=2.0 * 0.7978845608028654), [tt], [sg])
            P.op("dve", lambda e: e.tensor_tensor(out=hdn[:, 0:255], in0=xb[:, 0:255], in1=sg[:, 0:255], op=ALU.mult),
                 [xb, sg], [hdn])
            b2 = G.bank()
            if kv == 0:
                P.op("pe", lambda e: e.matmul(b2[0:64, 0:255], lhsT=w2[0][:, :], rhs=hdn[:, 0:255], start=True, stop=True),
                     [w2[0], hdn], [b2])
                P.op("act", lambda e: e.activation(out=KC[0:64, 0:255], in_=b2[0:64, 0:255], func=AF.Copy), [b2], [KC])
            else:
                for c in range(2):
                    ncol = 128 if c == 0 else 127
                    b3 = G.bank()
                    P.op("pe", lambda e: e.matmul(b3[0:ncol, 0:64], lhsT=hdn[:, c * 128:c * 128 + ncol], rhs=w2[1][:, :],
                                                  start=True, stop=True), [w2[1], hdn], [b3])
                    P.op("act", lambda e: e.activation(out=VC[0:ncol, c, 0:64], in_=b3[0:ncol, 0:64], func=AF.Copy),
                         [b3], [VC])
    C.close()


def phase_nsa(P, G, debug=False):
    C = Ctx(P)
    scale = 0.125
    Q = [C.sb([128, S], BF16, "Q") for _ in range(4)]
    Qs = [[C.sb([128, NT], BF16, "Qs") for _ in range(4)] for _ in range(2)]
    KS = C.sb([128, S], BF16, "KS")
    KW = C.sb([128, S], BF16, "KW")
    VS = C.sb([128, 32, 128], BF16, "VS")
    VW = C.sb([128, 32, 128], BF16, "VW")
    gT = C.sb([48, S], BF16, "gT")
    E = [C.sb([128, NT], BF16, "E") for _ in range(3)]
    masks = C.sb([128, 13, NT], BF16, "masks")
    selm = C.sb([48, 48 * 64], BF16, "selm")
    ov2 = C.sb([128, 2, 65], BF16, "ov2")
    btab = C.sb([128, 704], F32, "btab")
    qterm = C.sb([64, 4, NT], F32, "qterm")
    addt = C.sb([128, 4, 64], F32, "addt")
    acc = [[C.sb([64, NT], F32, "acc") for _ in range(4)] for _ in range(2)]
    imp = C.sb([128, 4, 64], F32, "imp")
    itmp = C.sb([128, 4, 64], F32, "itmp")
    zr = C.sb([128, 4, 1], F32, "zr")
    rz = C.sb([64, NT], F32, "rz")
    coefs = [C.sb([64, NT], F32, "coef") for _ in range(2)]
    ftmp = C.sb([64, NT], F32, "ftmp")
    m8 = C.sb([128, 16], F32, "m8")
    wk = C.sb([128, 64], F32, "wk")
    vv = C.sb([128, 64], F32, "vv")
    ngf = C.sb([128, 64], F32, "ngf")
    ng = C.sb([128, 64], BF16, "ng")
    yo = [C.sb([64, NT], BF16, "yo") for _ in range(2)]
    sbk = G.banks[0:3]
    obk = G.banks[3:5]
    obc = G.banks[5]
    ub = G.banks[6]
    gb = G.banks[7]
    tb = G.banks[7]
    P.dma("sp", masks[:, 0:4, :], G.d["cmask"].rearrange("o p n -> p o n"), [], [masks], chan="nsac")
    P.dma("sp", masks[:, 4:8, :], G.d["wmask"].rearrange("o p n -> p o n"), [], [masks], chan="nsac")
    P.dma("sp", masks[:, 8:13, :], G.d["cmpmask"].rearrange("o p n -> p o n"), [], [masks], chan="nsac")
    P.dma("sp", selm[:, :], G.d["selm"][:, :], [], [selm], chan="nsac")
    P.dma("sp", ov2[:, :, :], G.d["ov2"][:, :, :], [], [ov2], chan="nsac")
    P.dma("sp", btab[:, :], G.d["biastab"][:, :], [], [btab], chan="nsac")
    P.dma("sp", gT[:, :], G.d["gT"][:, :], [G.dr["gT"]], [gT], chan="nsac")
    P.dma("sp", KS[64:128, :], G.d["expand"][:, :], [], [KS], chan="nsak")
    P.op("pool", lambda e: e.memset(KW[64:128, :], 0.0), [], [KW])
    P.op("pool", lambda e: e.memset(KW[64:65, :], 1.0), [], [KW])
    for r in range(4):
        P.op("pool", lambda e: e.memset(Q[r][64:128, :], 0.0), [], [Q[r]])
    P.op("pool", lambda e: e.memset(VS[:, :, 64:128], 1.0), [], [VS])
    P.op("pool", lambda e: e.memset(VW[:, :, 64:128], 1.0), [], [VW])
    vsrc = G.d["vsw"].rearrange("(c p) f -> p c f", p=128)
    cnt = {"s": 0, "o": 0, "y": 0, "f": 0}
    NOSB = 8
    osb = [C.sb([128, NT], F32, "osb") for _ in range(NOSB)]
    TORDER = [0, 7, 1, 6, 2, 5, 3, 4]
    PAR = {T: i % 2 for i, T in enumerate(TORDER)}
    gbs = [C.sb([64, 12, NT], BF16, "gbs") for _ in range(2)]
    ngs = [C.sb([128, 64], BF16, "ngs") for _ in range(4)]

    def finalize(o, br, h, T, r, first):
        ts = slice(T * NT, (T + 1) * NT)
        col = (br * 16 + h) * 64
        os_ = osb[cnt["f"] % NOSB]
        cnt["f"] += 1
        if T >= 4:
            P.op("dve", lambda e: e.tensor_scalar(out=os_[:, :], in0=o[:, :], scalar1=G.tiny[:, 0:1], scalar2=None, op0=ALU.add),
                 [o, G.tiny], [os_])
        else:
            P.op("act", lambda e: e.activation(out=os_[:, :], in_=o[:, :], func=AF.Identity, bias=G.tiny[:, 0:1], scale=1.0),
                 [o, G.tiny], [os_])
        gsb = gbs[PAR[T]]
        gi = br * 4 + r
        coef = coefs[cnt["f"] % 2]
        P.op("dve", lambda e: e.reciprocal(out=coef[0:64, :], in_=os_[64:128, :]), [os_], [coef])
        P.op("dve", lambda e: e.tensor_tensor(out=coef[0:64, :], in0=gsb[0:64, gi, :], in1=coef[0:64, :], op=ALU.mult),
             [gsb, coef], [coef])
        a = acc[PAR[T]][r]
        if first:
            P.op("pool", lambda e: e.tensor_tensor(out=a[0:64, :], in0=os_[0:64, :], in1=coef[0:64, :], op=ALU.mult),
                 [os_, coef], [a])
        else:
            P.op("pool", lambda e: e.tensor_tensor(out=ftmp[0:64, :], in0=os_[0:64, :], in1=coef[0:64, :], op=ALU.mult),
                 [os_, coef], [ftmp])
            P.op("pool", lambda e: e.tensor_tensor(out=a[0:64, :], in0=a[0:64, :], in1=ftmp[0:64, :], op=ALU.add),
                 [a, ftmp], [a])
        if debug:
            P.dma("pool", G.d["dbgacc"][br, h * 64:(h + 1) * 64, ts], a[0:64, :], [a], [G.dr["dbgacc"]], chan="dbg")

    def run_stream(items):
        base = cnt["s"]
        cnt["s"] += len(items)
        index = {id(it): i for i, it in enumerate(items)}
        depi = [index[id(it["dep"])] if it.get("dep") is not None else -1 for it in items]
        st = {"emitted": 0}

        def emit_upto(limit, done):
            while st["emitted"] <= limit and st["emitted"] < len(items) and depi[st["emitted"]] <= done:
                m = st["emitted"]
                items[m]["emit_s"](items[m]["c"], sbk[(base + m) % 3], items[m].get("cols", (0, NT)))
                st["emitted"] += 1

        for n, it in enumerate(items):
            for hook in it.get("pre", ()):
                hook()
            emit_upto(n + 2, n - 1)
            assert st["emitted"] > n
            b = sbk[(base + n) % 3]
            ee = E[(base + n) % 3]
            bc = it["bc"]
            o = it["o"]
            vtile = it["v"]
            c = it["c"]
            j0, j1 = it.get("cols", (0, NT))
            P.op("act", lambda e: e.activation(out=ee[:, j0:j1], in_=b[:, j0:j1], func=AF.Exp, bias=btab[:, bc:bc + 1],
                                               scale=scale), [b, btab], [ee])
            P.op("pe", lambda e: e.matmul(o[:, j0:j1], lhsT=vtile[:, c, :], rhs=ee[:, j0:j1], start=it["first"],
                                          stop=it["last"], skip_group_check=True), [vtile, ee], [o])
            if it.get("post"):
                it["post"](ee)
            if it["last"]:
                it["fin"]()

    for g in range(4):
        KC, VC = G.KC[g], G.VC[g]
        P.dma("sp", KS[0:64, :], G.d["kvT"][2, g, :, :], [G.dr["kvT"]], [KS], chan="nsak")
        P.dma("sp", KW[0:64, :], G.d["kvT"][3, g, :, :], [G.dr["kvT"]], [KW], chan="nsak")
        P.dma("sp", VS[:, :, 0:64], vsrc[:, :, g * 64:(g + 1) * 64], [G.dr["vsw"]], [VS], chan="nsav")
        P.dma("sp", VW[:, :, 0:64], vsrc[:, :, 256 + g * 64:256 + (g + 1) * 64], [G.dr["vsw"]], [VW], chan="nsav")
        P.dma("sp", qterm[:, :, :], G.d["qterm"][4 * g:4 * g + 4, :, :].rearrange("r p n -> p r n"), [], [qterm], chan="nsaqt")
        for r in range(4):
            P.dma("sp", Q[r][0:64, :], G.d["qTn"][4 * g + r, :, :], [G.dr["qTn"]], [Q[r]], chan=("nsaq", r))
            P.dma("sp", Q[r][64:65, :], G.d["qaug"][4 * g + r, :, :], [], [Q[r]], chan=("nsaq", r))
        def cmp_items(T):
            ts = slice(T * NT, (T + 1) * NT)
            out = []
            for r in range(4):
                h = 4 * g + r
                q = Q[r]
                chunks = [0] if T <= 3 else [0, 1]
                o = obc

                def mk_emit(q):
                    def emit_cmp(c, b, cols=None):
                        dl = T - 4 * c
                        P.op("pe", lambda e: e.matmul(b[:, :], lhsT=KC[:, c * 128:(c + 1) * 128], rhs=q[:, ts], start=True,
                                                      stop=(dl > 4)), [KC, q], [b])
                        if dl <= 4:
                            P.op("pe", lambda e: e.matmul(b[:, :], lhsT=G.ident[:, :], rhs=masks[:, 8 + dl, :], start=False,
                                                          stop=True), [G.ident, masks], [b])
                    return emit_cmp

                def mk_post(c, nlast):
                    def post(ee):
                        for s in range(4):
                            P.op("pe", lambda e: e.matmul(ub[:, s * 65:(s + 1) * 65], lhsT=ee[:, s * 128:(s + 1) * 128],
                                                          rhs=ov2[:, c, :], start=(c == 0 and s == 0), stop=nlast,
                                                          skip_group_check=True), [ov2, ee], [ub])
                    return post

                def mk_fin(o, h, r):
                    def fin():
                        if r == 0:
                            P.dma("sp", addt[:, :, :], G.d["addtab"][:, 4 * T:4 * T + 4, :], [], [addt], chan="addt")
                        finalize(o, 0, h, T, r, True)
                        u3 = ub[:, 0:260].rearrange("p (a n) -> p a n", a=4)
                        P.op("dve", lambda e: e.tensor_scalar(out=zr[:, :, :], in0=u3[:, :, 64:65], scalar1=1e-30, scalar2=None,
                                                              op0=ALU.max), [ub], [zr])
                        P.op("dve", lambda e: e.reciprocal(out=zr[:, :, :], in_=zr[:, :, :]), [zr], [zr])
                        if r == 0:
                            P.op("dve", lambda e: e.tensor_tensor(out=imp[:, :, :], in0=u3[:, :, 0:64],
                                                                  in1=zr[:, :, 0:1].to_broadcast([128, 4, 64]), op=ALU.mult),
                                 [ub, zr], [imp])
                        else:
                            P.op("dve", lambda e: e.tensor_tensor(out=itmp[:, :, :], in0=u3[:, :, 0:64],
                                                                  in1=zr[:, :, 0:1].to_broadcast([128, 4, 64]), op=ALU.mult),
                                 [ub, zr], [itmp])
                            P.op("pool", lambda e: e.tensor_tensor(out=imp[:, :, :], in0=imp[:, :, :], in1=itmp[:, :, :],
                                                                   op=ALU.add), [imp, itmp], [imp])
                        if r == 3:
                            topk(T)
                    return fin

                es = mk_emit(q)
                fn = mk_fin(o, h, r)
                for n, c in enumerate(chunks):
                    out.append(dict(emit_s=es, c=c, bc=512 + h * 12 + (T if c == 0 else 8 + (T - 4)), o=o, v=VC,
                                    first=(n == 0), last=(n == len(chunks) - 1), fin=fn, post=mk_post(c, n == len(chunks) - 1)))
            return out

        def load_gates(T):
            ts = slice(T * NT, (T + 1) * NT)
            gsb = gbs[PAR[T]]
            for br in range(3):
                for r in range(4):
                    row = br * 16 + 4 * g + r
                    P.dma("sp", gsb[0:64, br * 4 + r, :], G.d["gT"][row:row + 1, ts].partition_broadcast(64),
                          [G.dr["gT"]], [gsb], chan=("gbs", PAR[T]), acc=True)

        def topk(T):
            ts = slice(T * NT, (T + 1) * NT)
            if debug:
                P.dma("pool", G.d["dbgimp"][g, :, 4 * T:4 * T + 4, :], imp[:, :, :], [imp], [G.dr["dbgimp"]], chan="dbg")
            for s in range(4):
                P.op("dve", lambda e: e.tensor_tensor(out=vv[:, :], in0=imp[:, s, :], in1=addt[:, s, :], op=ALU.add),
                     [imp, addt], [vv])
                P.op("dve", lambda e: e.max(out=m8[:, 0:8], in_=vv[:, :]), [vv], [m8])
                P.op("dve", lambda e: e.match_replace(out=wk[:, :], in_to_replace=m8[:, 0:8], in_values=vv[:, :],
                                                      imm_value=-3.0e38), [m8, vv], [wk])
                P.op("dve", lambda e: e.max(out=m8[:, 8:16], in_=wk[:, :]), [wk], [m8])
                P.op("dve", lambda e: e.tensor_scalar(out=ngf[:, :], in0=vv[:, :], scalar1=m8[:, 15:16], scalar2=30000.0,
                                                      op0=ALU.is_ge, op1=ALU.mult), [vv, m8], [ngf])
                ngx = ngs[s]
                P.op("dve", lambda e: e.tensor_scalar(out=ngx[:, :], in0=ngf[:, :], scalar1=-30000.0, scalar2=None,
                                                      op0=ALU.add), [ngf], [ngx])

        def topk2(T):
            ts = slice(T * NT, (T + 1) * NT)
            for s in range(4):
                ngx = ngs[s]
                P.op("pe", lambda e: e.matmul(tb[0:64, s * 128:(s + 1) * 128], lhsT=ngx[:, :], rhs=G.ident[:, :], start=True,
                                              stop=True), [ngx, G.ident], [tb])
            qs = Qs[PAR[T]]
            for r in range(4):
                P.op("dve", lambda e: e.tensor_tensor(out=qs[r][64:128, :], in0=tb[0:64, :], in1=qterm[0:64, r, :], op=ALU.add),
                     [tb, qterm], [qs[r]])
                P.op("pool", lambda e: e.tensor_copy(out=qs[r][0:64, :], in_=Q[r][0:64, ts]), [Q[r]], [qs[r]])

        def sw_items(T, dep):
            ts = slice(T * NT, (T + 1) * NT)
            qs = Qs[PAR[T]]
            items = []
            specs = []
            for r in range(4):
                h = 4 * g + r

                def mk_slc(qq):
                    def emit_slc(c, b, cols):
                        diag = c >= 4 * T
                        j0, j1 = cols
                        P.op("pe", lambda e: e.matmul(b[:, j0:j1], lhsT=KS[:, c * 128:(c + 1) * 128], rhs=qq[:, j0:j1], start=True,
                                                      stop=not diag), [KS, qq], [b])
                        if diag:
                            P.op("pe", lambda e: e.matmul(b[:, j0:j1], lhsT=G.ident[:, :], rhs=masks[:, c - 4 * T, j0:j1],
                                                          start=False, stop=True), [G.ident, masks], [b])
                    return emit_slc

                def mk_win(q):
                    def emit_win(c, b, cols):
                        mi = (c - 4 * T) if c >= 4 * T else 4 + (c - (4 * T - 4))
                        j0, j1 = cols
                        P.op("pe", lambda e: e.matmul(b[:, j0:j1], lhsT=KW[:, c * 128:(c + 1) * 128],
                                                      rhs=q[:, T * NT + j0:T * NT + j1], start=True, stop=False), [KW, q], [b])
                        P.op("pe", lambda e: e.matmul(b[:, j0:j1], lhsT=G.ident[:, :], rhs=masks[:, mi, j0:j1], start=False,
                                                      stop=True), [G.ident, masks], [b])
                    return emit_win

                def mk_fin(o, br, h, r, store):
                    def fin():
                        finalize(o, br, h, T, r, False)
                        if store:
                            y = yo[cnt["y"] % 2]
                            a = acc[PAR[T]][r]
                            P.op("pool", lambda e: e.tensor_copy(out=y[0:64, :], in_=a[0:64, :]), [a], [y])
                            P.dma("pool", G.d["ynsaT"][h * 64:(h + 1) * 64, ts], y[0:64, :], [y], [G.dr["ynsaT"]],
                                  chan=("nsay", cnt["y"] % 2))
                            cnt["y"] += 1
                    return fin

                specs.append((2, r, h, VW, mk_win(Q[r]), mk_fin, list(range(max(0, 4 * T - 4), 4 * T + 4))))
                specs.append((1, r, h, VS, mk_slc(qs[r]), mk_fin, list(range(4 * T + 4))))
            for br, r, h, vt, es, mkf, cl in sorted(specs, key=lambda s: (-s[0], s[1])):
                o = obk[cnt["o"] % 2]
                cnt["o"] += 1
                fn = mkf(o, br, h, r, br == 1)
                for n, c in enumerate(cl):
                    if c >= 4 * T:
                        cols = (128 * (c - 4 * T), NT)
                    elif br == 2:
                        cols = (0, 128 * (c - (4 * T - 4) + 1))
                    else:
                        cols = (0, NT)
                    items.append(dict(emit_s=es, c=c, bc=h * 32 + (c - 4 * T + 28), o=o, v=vt, first=(n == 0),
                                      last=(n == len(cl) - 1), fin=fn, post=None, dep=None, slc_tile=(T if br == 1 else None),
                                      cols=cols))
            return items

        DEFER = 10
        load_gates(TORDER[0])
        seq = cmp_items(TORDER[0])
        pend = (len(seq) - 1, TORDER[0])
        for ti_, T in enumerate(TORDER):
            TN = TORDER[ti_ + 1] if ti_ + 1 < NTT else None
            sw = sw_items(T, None)
            if TN is not None:
                cm = cmp_items(TN)
                groups = []
                for it in cm:
                    if it["first"]:
                        groups.append([it])
                    else:
                        groups[-1].append(it)
                step = max(1, (len(sw) * 3) // (len(groups) * 5 + 1))
                pos = step
                merged = []
                gi = 0
                for n, it in enumerate(sw):
                    merged.append(it)
                    if gi < len(groups) and n + 1 == pos:
                        merged.extend(groups[gi])
                        gi += 1
                        pos += step
                while gi < len(groups):
                    merged.extend(groups[gi])
                    gi += 1
            else:
                cm = None
                merged = sw
            start = len(seq)
            seq += merged
            if TN is not None:
                seq[start].setdefault("pre", []).append((lambda TT: (lambda: load_gates(TT)))(TN))
            li, tt = pend
            hi_ = min(li + DEFER, len(seq) - 1)
            seq[hi_].setdefault("pre", []).append((lambda TT: (lambda: topk2(TT)))(tt))
            for it in seq[start:]:
                if it.get("slc_tile") == tt:
                    it["dep"] = seq[hi_]
            if cm is not None:
                pend = (max(i for i, it in enumerate(seq) if it is cm[-1]), TN)
        run_stream(seq)
    C.close()


def build_program(nphases=99, debug=False):
    nc = bass.Bass("TRN2", target_bir_lowering=False)
    P = Prog(nc)
    G = Common()
    G.d = {}
    G.dr = {}

    def dram(name, shape, dt, kind="Internal"):
        if debug and kind == "Internal":
            kind = "ExternalOutput"
        t = nc.dram_tensor(name, list(shape), dt, kind=kind)
        G.d[name] = t.ap()
        G.dr[name] = Buf(t, multi=True)

    dram("xT", [D, S], F32, "ExternalInput")
    dram("cstv", [128, 128], F32, "ExternalInput")
    dram("ropecs", [128, S], F32, "ExternalInput")
    dram("a_w_in", [D, 2624], F32, "ExternalInput")
    dram("a_w_q", [768, 1024], F32, "ExternalInput")
    dram("a_w_kv", [256, 1024], F32, "ExternalInput")
    dram("ycatT", [D, S], BF16)
    dram("qT", [8, 96, S], BF16)
    dram("kT", [8, 96, S], BF16)
    dram("vtok", [S, 512], BF16)
    dram("cmask", [4, 128, NT], BF16, "ExternalInput")
    dram("identv", [128, 128], BF16, "ExternalInput")
    dram("a_w_out", [D, D], F32, "ExternalInput")
    dram("w1_0", [D, 4096], F32, "ExternalInput")
    dram("w2_0", [4096, D], F32, "ExternalInput")
    dram("x1T", [D, S], F32)
    dram("h2T", [D, S], BF16)
    dram("x2T", [D, S], F32)
    dram("h3T", [D, S], BF16)
    dram("c_w_in", [D, 2608], F32, "ExternalInput")
    dram("qTn", [16, 64, S], BF16)
    dram("kvT", [4, 4, 64, S], BF16)
    dram("gT", [48, S], BF16)
    dram("vsw", [S, 512], BF16)
    for sfx in ("k", "v"):
        dram("cw1" + sfx, [64, 2048], F32, "ExternalInput")
        dram("cw2" + sfx, [64, 64], F32, "ExternalInput")
        dram("cpe" + sfx, [64, 32], F32, "ExternalInput")
    dram("qaug", [16, 1, S], BF16, "ExternalInput")
    dram("biastab", [128, 704], F32, "ExternalInput")
    dram("qterm", [16, 64, NT], F32, "ExternalInput")
    dram("wmask", [4, 128, NT], BF16, "ExternalInput")
    dram("cmpmask", [5, 128, NT], BF16, "ExternalInput")
    dram("expand", [64, S], BF16, "ExternalInput")
    dram("selm", [48, 48 * 64], BF16, "ExternalInput")
    dram("ov2", [128, 2, 65], BF16, "ExternalInput")
    dram("addtab", [128, 32, 64], F32, "ExternalInput")
    dram("ynsaT", [D, S], BF16)
    dram("c_w_out", [D, D], F32, "ExternalInput")
    dram("w1_1", [D, 4096], F32, "ExternalInput")
    dram("w2_1", [4096, D], F32, "ExternalInput")
    dram("x3T", [D, S], F32)
    dram("h4T", [D, S], BF16)
    dram("outT", [D, S], F32, "ExternalOutput")
    if debug:
        dram("dbgacc", [3, D, S], F32)
        dram("dbgimp", [4, 128, 32, 64], F32)
    G.co = {"gpre0": 0, "gpost0": 8, "gmpre0": 16, "gmpost0": 24, "gpre1": 32, "gpost1": 40, "gmpre1": 48,
            "gmpost1": 56, "qn": 64, "kvn": 70, "convw": 72}
    with ExitStack() as gs:
        def gsb(name, shape, dt):
            return Buf(gs.enter_context(nc.sbuf_tensor(name, list(shape), dt)))
        G.cst = gsb("cst", [128, 128], F32)
        G.ones = gsb("ones", [128, 128], BF16)
        G.eps = gsb("eps", [128, 1], F32)
        G.tiny = gsb("tiny", [128, 1], F32)
        G.banks = [Buf(gs.enter_context(nc.psum_tensor("bank%d" % i, [128, 512], F32)), psum=True) for i in range(8)]

        def bank():
            P.bank_i = (P.bank_i + 1) % 8
            return G.banks[P.bank_i]
        G.bank = bank
        P.dma("sp", G.cst[:, :], G.d["cstv"][:, :], [], [G.cst], chan="cst")
        P.op("pool", lambda e: e.memset(G.ones[:, :], 1.0), [], [G.ones])
        P.op("pool", lambda e: e.memset(G.eps[:, :], EPS), [], [G.eps])
        P.op("pool", lambda e: e.memset(G.tiny[0:64, :], 0.0), [], [G.tiny])
        P.op("pool", lambda e: e.memset(G.tiny[64:128, :], 1e-30), [], [G.tiny])
        G.ident = gsb("ident", [128, 128], BF16)
        P.dma("sp", G.ident[:, :], G.d["identv"][:, :], [], [G.ident], chan="cst")
        if nphases >= 1:
            phase_front_a(P, G)
        if nphases >= 2:
            phase_mla_attn(P, G)
        if nphases >= 3:
            Cw, W1 = prefetch_w1(P, G, "w1_0") if nphases >= 4 else (None, None)
            phase_outproj(P, G, "a_w_out", "ycatT", "xT", "x1T", "h2T", "gpost0", "gmpre0")
        if nphases >= 4:
            phase_mlp(P, G, "w1_0", "w2_0", "h2T", "x1T", "x2T", "h3T", "gmpost0", "gpre1", W1=W1)
            Cw.close()
        G.KC = [gsb("KC%d" % g, [128, 256], BF16) for g in range(4)]
        G.VC = [gsb("VC%d" % g, [128, 2, 128], BF16) for g in range(4)]
        if nphases >= 5:
            phase_front_c(P, G)
        if nphases >= 6:
            phase_compress(P, G)
        if nphases >= 7:
            phase_nsa(P, G, debug)
        if nphases >= 8:
            Cw, W1 = prefetch_w1(P, G, "w1_1") if nphases >= 9 else (None, None)
            phase_outproj(P, G, "c_w_out", "ynsaT", "x2T", "x3T", "h4T", "gpost1", "gmpre1")
        if nphases >= 9:
            phase_mlp(P, G, "w1_1", "w2_1", "h4T", "x3T", "outT", None, "gmpost1", None, W1=W1)
            Cw.close()
        P.barrier()
        P.final_wait()
    print("instructions:", P.ninst, "sems:", P.nsem)
    return nc


def pack_cols(v):
    return np.ascontiguousarray(np.asarray(v, np.float32).reshape(-1, 128).T)


def host_consts(inp):
    c = {}
    cst = np.zeros((128, 128), np.float32)
    cols = [inp["norm_mix_pre"][0], inp["norm_mix_post"][0], inp["norm_mlp_pre"][0], inp["norm_mlp_post"][0],
            inp["norm_mix_pre"][1], inp["norm_mix_post"][1], inp["norm_mlp_pre"][1], inp["norm_mlp_post"][1]]
    for i, v in enumerate(cols):
        cst[:, i * 8:(i + 1) * 8] = pack_cols(v)
    cst[:, 64:70] = pack_cols(inp["a_q_norm"][0])
    cst[:, 70:72] = pack_cols(inp["a_kv_norm"][0])
    cw = np.asarray(inp["a_conv_w"][0], np.float32)
    for cc in range(4):
        for k in range(3):
            cst[:, 72 + cc * 3 + k] = cw[k, cc * 128:(cc + 1) * 128]
    c["cstv"] = cst
    half = 16
    inv = (10000.0 ** (-np.arange(half, dtype=np.float32) / half)).astype(np.float32)
    ang = np.arange(S, dtype=np.float32)[None, :] * inv[:, None]
    cs, sn = np.cos(ang).astype(np.float32), np.sin(ang).astype(np.float32)
    Cm = np.concatenate([cs, cs], 0)
    Sm = np.concatenate([-sn, sn], 0)
    c["ropecs"] = np.ascontiguousarray(np.concatenate([Cm, Sm, Cm, Sm], 0))
    w = np.asarray(inp["a_w_in"][0], np.float32)
    c["a_w_in"] = np.ascontiguousarray(np.concatenate([w, w[:, 2576:2592], w[:, 2560:2576]], 1))
    wq = np.asarray(inp["a_w_q_up"][0], np.float32)
    cols = []
    for h in range(8):
        b = h * 96
        cols += [wq[:, b:b + 96], wq[:, b + 80:b + 96], wq[:, b + 64:b + 80]]
    c["a_w_q"] = np.ascontiguousarray(np.concatenate(cols, 1))
    wkv = np.asarray(inp["a_w_kv_up"][0], np.float32).reshape(256, 8, 128)
    c["a_w_kv"] = np.ascontiguousarray(np.concatenate([wkv[:, :, :64].reshape(256, 512), wkv[:, :, 64:].reshape(256, 512)], 1))
    NEGM = -30000.0
    cm = np.zeros((4, 128, NT), np.float32)
    ii = np.arange(128)[:, None]
    jj = np.arange(NT)[None, :]
    for off in range(4):
        cm[off] = np.where(ii + 128 * off <= jj, 0.0, NEGM)
    c["cmask"] = cm.astype(ml_dtypes.bfloat16)
    c["identv"] = np.eye(128, dtype=np.float32).astype(ml_dtypes.bfloat16)
    c["a_w_out"] = np.ascontiguousarray(inp["a_w_out"][0], np.float32)
    c["w1_0"] = np.ascontiguousarray(inp["mlp_w1"][0], np.float32)
    c["w2_0"] = np.ascontiguousarray(inp["mlp_w2"][0], np.float32)
    bf = ml_dtypes.bfloat16
    cw = np.asarray(inp["c_w_in"][0], np.float32)
    c["c_w_in"] = np.ascontiguousarray(np.concatenate(
        [cw[:, 0:1536], cw[:, 1536:1792], cw[:, 2048:2304], cw[:, 1792:2048], cw[:, 2304:2560], cw[:, 2560:2608]], 1))
    for sfx in ("k", "v"):
        w1 = np.asarray(inp["c_cmp_w1_" + sfx][0], np.float32)
        c["cw1" + sfx] = np.ascontiguousarray(w1.transpose(1, 0, 2).reshape(64, 2048))
        c["cw2" + sfx] = np.ascontiguousarray(inp["c_cmp_w2_" + sfx][0], np.float32)
        c["cpe" + sfx] = np.ascontiguousarray(np.asarray(inp["c_cmp_pe_" + sfx][0], np.float32).T)

    slopes = [float(np.float32(2.0 ** (-8.0 * (h + 1) / 16))) for h in range(16)]
    qaug = np.zeros((16, 1, S), np.float32)
    qterm = np.zeros((16, 64, NT), np.float32)
    btab = np.zeros((128, 704), np.float64)
    pp = np.arange(128, dtype=np.float64)
    for h in range(16):
        sp = 8.0 * slopes[h]
        qaug[h, 0] = -sp * (np.arange(S) % NT)
        qterm[h] = (-sp * np.arange(NT))[None, :]
        for dlt in range(-28, 4):
            btab[:, h * 32 + dlt + 28] = slopes[h] * (pp + 128.0 * dlt)
        for T in range(8):
            btab[:, 512 + h * 12 + T] = slopes[h] * (16.0 * pp + 15.5 - 512.0 * T)
        for T in range(4, 8):
            btab[:, 512 + h * 12 + 8 + (T - 4)] = slopes[h] * (16.0 * pp + 2048.0 + 15.5 - 512.0 * T)
    c["qaug"] = qaug.astype(bf)
    c["qterm"] = qterm
    c["biastab"] = btab.astype(np.float32)
    ii = np.arange(128)[:, None]
    jj2 = np.arange(NT)[None, :]
    wm = np.zeros((4, 128, NT), np.float32)
    cpm = np.zeros((5, 128, NT), np.float32)
    for o in range(4):
        wm[o] = np.where(ii > jj2 - 128 * o, 0.0, NEGM)
    for o in range(5):
        cpm[o] = np.where(16 * ii + 31 <= 512 * o + jj2, 0.0, NEGM)
    c["wmask"] = wm.astype(bf)
    c["cmpmask"] = cpm.astype(bf)
    c["expand"] = (np.arange(64)[:, None] == (np.arange(S)[None, :] // 64)).astype(np.float32).astype(bf)
    selm = np.zeros((48, 48, 64), np.float32)
    for k in range(48):
        selm[k, k, :] = 1.0
    c["selm"] = selm.reshape(48, 48 * 64).astype(bf)
    starts = np.arange(255) * 16
    ss = np.arange(64) * 64
    ov = np.clip(np.minimum(starts[:, None] + 32, ss[None, :] + 64) - np.maximum(starts[:, None], ss[None, :]), 0, None) / 32.0
    ov2 = np.zeros((256, 65), np.float32)
    ov2[:255, :64] = ov
    ov2[:255, 64] = 1.0
    c["ov2"] = np.ascontiguousarray(ov2.reshape(2, 128, 65).transpose(1, 0, 2)).astype(bf)
    t = np.arange(S)
    cur = t // 64
    jb = np.arange(64)[None, :]
    forced = (jb == 0) | (jb == cur[:, None]) | (jb == cur[:, None] - 1)
    add = np.where(jb > cur[:, None], -1.0e9, 1.0e4 * forced).astype(np.float32)
    c["addtab"] = np.ascontiguousarray(add.reshape(32, 128, 64).transpose(1, 0, 2))
    c["c_w_out"] = np.ascontiguousarray(inp["c_w_out"][0], np.float32)
    c["w1_1"] = np.ascontiguousarray(inp["mlp_w1"][1], np.float32)
    c["w2_1"] = np.ascontiguousarray(inp["mlp_w2"][1], np.float32)
    return c


def kernel(**inp):
    x = np.asarray(inp["x"], np.float32)
    c = host_consts(inp)
    nc = build_program()
    in_maps = []
    for b in range(8):
        m = dict(c)
        m["xT"] = np.ascontiguousarray(x[b].T)
        in_maps.append(m)
    res = run_bass_kernel_spmd(nc, in_maps, core_ids=list(range(8)))
    out = np.stack([np.ascontiguousarray(r["outT"].T) for r in res.results], 0)
    return out.astype(np.float32)
```

```python
import numpy as np
import ml_dtypes
from contextlib import ExitStack
import concourse.bass as bass
import concourse.mybir as mybir
from concourse.bass_utils import run_bass_kernel_spmd

F32 = mybir.dt.float32
BF16 = mybir.dt.bfloat16
ALU = mybir.AluOpType
AF = mybir.ActivationFunctionType

S = 4096
D = 1024
EPS = 1e-6
NT = 512
NTT = S // NT


class Trk:
    __slots__ = ("w", "r", "multi")

    def __init__(self, multi=False):
        self.w = {}
        self.r = {}
        self.multi = multi


class Buf:
    def __init__(self, t, multi=False, psum=False):
        self.t = t
        self.k = Trk(multi)
        self.psum = psum

    def __getitem__(self, idx):
        return self.t[idx]


class Prog:
    def __init__(self, nc):
        self.nc = nc
        self.e = {"pe": nc.tensor, "act": nc.scalar, "dve": nc.vector, "pool": nc.gpsimd, "sp": nc.sync}
        self.esem = {}
        self.dsem = {}
        self.seen = {k: {} for k in self.e}
        self.nsem = 0
        self.semtotal = {}
        self.ninst = 0
        self.bank_i = 0

    def _newsem(self, name):
        self.nsem += 1
        return (self.nsem, self.nc.alloc_semaphore("%s_%d" % (name, self.nsem)))

    def _wait(self, eng, deps):
        for key, (sem, val, src) in deps.items():
            if src == "pe" and eng == "pe":
                continue
            if src == "dma":
                val = max(val, self.semtotal[key])
            if self.seen[eng].get(key, 0) >= val:
                continue
            self.e[eng].wait_ge(sem, val)
            self.ninst += 1
            self.seen[eng][key] = val

    @staticmethod
    def _add(deps, d):
        for k, t in d.items():
            if k not in deps or deps[k][1] < t[1]:
                deps[k] = t

    def _deps(self, reads, writes, acc=False):
        deps = {}
        for b in reads:
            self._add(deps, b.k.w)
            if getattr(b, "psum", False):
                self._add(deps, b.k.r)
        for b in writes:
            self._add(deps, b.k.r)
            if not (b.k.multi or acc):
                self._add(deps, b.k.w)
        return deps

    def _commit(self, key, tok, reads, writes, acc=False):
        for b in reads:
            b.k.r[key] = tok
        for b in writes:
            if b.k.multi or acc:
                b.k.w[key] = tok
            else:
                b.k.w = {key: tok}
            b.k.r = {}

    def op(self, eng, fn, reads=(), writes=()):
        self._wait(eng, self._deps(reads, writes))
        ins = fn(self.e[eng])
        st = self.esem.get(eng)
        if st is None or st[2] >= 30000:
            k, sem = self._newsem("e" + eng)
            st = [k, sem, 0]
            self.esem[eng] = st
        st[2] += 1
        ins.then_inc(st[1], 1)
        self.ninst += 1
        self._commit(st[0], (st[1], st[2], eng), reads, writes)

    def dma(self, q, out, in_, reads, writes, chan, acc=False):
        self._wait(q, self._deps(reads, writes, acc))
        st = self.dsem.get(chan)
        if st is None or st[2] >= 30000:
            k, sem = self._newsem("d")
            st = [k, sem, 0]
            self.dsem[chan] = st
        ins = self.e[q].dma_start(out=out, in_=in_)
        st[2] += 16
        self.semtotal[st[0]] = st[2]
        ins.then_inc(st[1], 16)
        self.ninst += 1
        self._commit(st[0], (st[1], st[2], "dma"), reads, writes, acc)

    def barrier(self):
        toks = {}
        for eng, st in self.esem.items():
            toks[st[0]] = (st[1], st[2], eng)
        for ch, st in self.dsem.items():
            toks[st[0]] = (st[1], st[2], "dma")
        for eng in self.e:
            self._wait(eng, toks)

    def final_wait(self):
        toks = {}
        for ch, st in self.dsem.items():
            toks[st[0]] = (st[1], st[2], "dma")
        self._wait("sp", toks)


class Ctx:
    uid = 0

    def __init__(self, P):
        self.P = P
        self.nc = P.nc
        self.st = ExitStack()
        self.n = 0

    def sb(self, shape, dt, name="t"):
        Ctx.uid += 1
        t = self.st.enter_context(self.nc.sbuf_tensor("%s_%d" % (name, Ctx.uid), list(shape), dt))
        return Buf(t)

    def close(self):
        self.P.barrier()
        self.st.close()


def load_w_bf16(P, C, dram_ap, dst, kc, ncols, stage=None, engs=None):
    w3 = dram_ap.rearrange("(k p) c -> p k c", p=128)
    for k in range(kc):
        if stage is not None:
            P.dma("pool", dst[:, k, :], w3[:, k, :], [], [stage[k]], chan=("w", id(dst), k % 4))
        else:
            P.dma("pool", dst[:, k, :], w3[:, k, :], [], [dst], chan=("w", id(dst)), acc=True)


class Common:
    pass


def rstd_from_sumsq(P, G, bank, rstd, n_feat):
    P.op("act", lambda e: e.activation(out=rstd[:, :], in_=bank[:, :], func=AF.Sqrt, bias=G.eps[:, 0:1],
                                        scale=1.0 / n_feat), [bank, G.eps], [rstd])
    P.op("dve", lambda e: e.reciprocal(out=rstd[:, :], in_=rstd[:, :]), [rstd], [rstd])


def sumsq_bcast(P, G, src, sq, kc, bank, nparts=128):
    P.op("act", lambda e: e.activation(out=sq[:, 0:kc, :], in_=src[:, 0:kc, :], func=AF.Square), [src], [sq])
    for k in range(kc):
        P.op("pe", lambda e, k=k: e.matmul(bank[:, :], lhsT=G.ones[:, :], rhs=sq[:, k, :], start=(k == 0),
                                            stop=(k == kc - 1)), [sq, G.ones], [bank])


def phase_front_a(P, G):
    nc = P.nc
    C = Ctx(P)
    NIN = 2624
    win = C.sb([128, 8, NIN], BF16, "win")
    wq = C.sb([128, 6, 1024], BF16, "wq")
    wkv = C.sb([128, 2, 1024], BF16, "wkv")
    stage = None
    load_w_bf16(P, C, G.d["a_w_in"], win, 8, NIN, stage)
    load_w_bf16(P, C, G.d["a_w_q"], wq, 6, 1024, stage)
    load_w_bf16(P, C, G.d["a_w_kv"], wkv, 2, 1024, stage)
    xts = [C.sb([128, 8, NT], F32, "xt") for _ in range(2)]
    sqxs = [C.sb([128, 8, NT], BF16, "sqx") for _ in range(2)]
    hTs = [C.sb([128, 8, NT], BF16, "hT") for _ in range(2)]
    rstdxs = [C.sb([128, NT], F32, "rstdx") for _ in range(2)]
    sq = C.sb([128, 8, NT], BF16, "sq")
    rstd = C.sb([128, NT], F32, "rstd")
    rope = C.sb([128, NT], F32, "rope")
    u = [[C.sb([128, NT + 2], F32, "u") for _ in range(2)] for _ in range(4)]
    hvss = [C.sb([128, NT], F32, "hvs") for _ in range(2)]
    c1s = [C.sb([128, NT], F32, "c1") for _ in range(2)]
    yc = [C.sb([128, NT], BF16, "yc") for _ in range(2)]
    cq = C.sb([128, 6, NT], F32, "cq")
    cqn = C.sb([128, 6, NT], BF16, "cqn")
    ckv = C.sb([128, 2, NT], F32, "ckv")
    ckvn = C.sb([128, 2, NT], BF16, "ckvn")
    t1s = [C.sb([128, NT], F32, "t1") for _ in range(2)]
    t2s = [C.sb([128, NT], F32, "t2") for _ in range(2)]
    sqkv = C.sb([128, 2, NT], BF16, "sqkv")
    rstdkv = C.sb([128, NT], F32, "rstdkv")
    kr = C.sb([32, NT], BF16, "kr")
    qt = [C.sb([128, NT], BF16, "qt") for _ in range(2)]
    kn = [C.sb([128, NT], BF16, "kn") for _ in range(2)]
    vt = [C.sb([128, NT], BF16, "vt") for _ in range(2)]
    cst = G.cst
    xT3 = G.d["xT"].rearrange("(k p) t -> p k t", p=128)
    for cc in range(4):
        P.op("pool", lambda e, cc=cc: e.memset(u[cc][0][:, 0:2], 0.0), [], [u[cc][0]])
    ev = [0]

    def evac(out_ap, bank, in_ap, reads, writes):
        ev[0] += 1
        if ev[0] % 2:
            P.op("act", lambda e: e.activation(out=out_ap, in_=in_ap, func=AF.Copy), reads, writes)
        else:
            P.op("dve", lambda e: e.tensor_copy(out=out_ap, in_=in_ap), reads, writes)

    cur = {}

    def mm_chunk(bank, col0, m, rows=slice(0, 128)):
        hT = cur["hT"]
        for k in range(8):
            P.op("pe", lambda e, k=k: e.matmul(bank[0:m, :], lhsT=win[:, k, col0:col0 + m], rhs=hT[:, k, :],
                                                start=(k == 0), stop=(k == 7)), [win, hT], [bank])

    def load_x(ti):
        ts = slice(ti * NT, (ti + 1) * NT)
        xt = xts[ti % 2]
        P.dma("sp", xt[:, :, :], xT3[:, :, ts], [G.dr["xT"]], [xt], chan=("xt", ti % 2))

    def prenorm(ti):
        xt, sqx, hT, rstdx = xts[ti % 2], sqxs[ti % 2], hTs[ti % 2], rstdxs[ti % 2]
        bA = G.bank()
        sumsq_bcast(P, G, xt, sqx, 8, bA)
        rstd_from_sumsq(P, G, bA, rstdx, 1024)
        for k in range(8):
            P.op("dve", lambda e, k=k: e.scalar_tensor_tensor(out=hT[:, k, :], in0=xt[:, k, :],
                                                               scalar=cst[:, G.co["gpre0"] + k:G.co["gpre0"] + k + 1],
                                                               in1=rstdx[:, :], op0=ALU.mult, op1=ALU.mult),
                 [xt, rstdx, cst], [hT])

    load_x(0)
    prenorm(0)
    for ti in range(NTT):
        ts = slice(ti * NT, (ti + 1) * NT)
        cur["hT"] = hTs[ti % 2]
        if ti + 1 < NTT:
            load_x(ti + 1)
        P.dma("sp", rope[:, :], G.d["ropecs"][:, ts], [], [rope], chan="rope")
        for cc in range(4):
            b0, b1, b2 = G.bank(), G.bank(), G.bank()
            mm_chunk(b0, cc * 128, 128)
            mm_chunk(b1, 512 + cc * 128, 128)
            mm_chunk(b2, 1024 + cc * 128, 128)
            uc, un = u[cc][ti % 2], u[cc][(ti + 1) % 2]
            hvs, c1 = hvss[cc % 2], c1s[cc % 2]
            P.op("act", lambda e: e.activation(out=hvs[:, :], in_=b2[:, :], func=AF.Copy), [b2], [hvs])
            P.op("dve", lambda e: e.tensor_tensor(out=uc[:, 2:NT + 2], in0=b1[:, :], in1=hvs[:, :], op=ALU.mult),
                 [b1, hvs], [uc])
            cw = G.co["convw"] + cc * 3
            P.op("dve", lambda e: e.tensor_scalar(out=c1[:, :], in0=uc[:, 0:NT], scalar1=cst[:, cw:cw + 1],
                                                  scalar2=None, op0=ALU.mult), [uc, cst], [c1])
            P.op("dve", lambda e: e.scalar_tensor_tensor(out=c1[:, :], in0=uc[:, 1:NT + 1], scalar=cst[:, cw + 1:cw + 2],
                                                         in1=c1[:, :], op0=ALU.mult, op1=ALU.add), [uc, cst, c1], [c1])
            P.op("dve", lambda e: e.scalar_tensor_tensor(out=c1[:, :], in0=uc[:, 2:NT + 2], scalar=cst[:, cw + 2:cw + 3],
                                                         in1=c1[:, :], op0=ALU.mult, op1=ALU.add), [uc, cst, c1], [c1])
            y = yc[cc % 2]
            P.op("dve", lambda e: e.tensor_tensor(out=y[:, :], in0=b0[:, :], in1=c1[:, :], op=ALU.mult), [b0, c1], [y])
            P.op("pool", lambda e: e.tensor_copy(out=un[:, 0:2], in_=uc[:, NT:NT + 2]), [uc], [un])
            P.dma("pool", G.d["ycatT"][cc * 128:(cc + 1) * 128, ts], y[:, :], [y], [G.dr["ycatT"]], chan=("yc", cc % 2))
        if ti + 1 < NTT:
            prenorm(ti + 1)
        for j in range(2):
            b = G.bank()
            mm_chunk(b, 2304 + j * 128, 128)
            evac(ckv[:, j, :], b, b[:, :], [b], [ckv])
        for j in range(6):
            b = G.bank()
            mm_chunk(b, 1536 + j * 128, 128)
            evac(cq[:, j, :], b, b[:, :], [b], [cq])
        bk = G.bank()
        sumsq_bcast(P, G, ckv, sqkv, 2, bk)
        rstd_from_sumsq(P, G, bk, rstdkv, 256)
        for j in range(2):
            P.op("dve", lambda e, j=j: e.scalar_tensor_tensor(out=ckvn[:, j, :], in0=ckv[:, j, :],
                                                               scalar=cst[:, G.co["kvn"] + j:G.co["kvn"] + j + 1],
                                                               in1=rstdkv[:, :], op0=ALU.mult, op1=ALU.mult),
                 [ckv, rstdkv, cst], [ckvn])
        bq = G.bank()
        sumsq_bcast(P, G, cq, sq, 6, bq)
        rstd_from_sumsq(P, G, bq, rstd, 768)
        for j in range(6):
            P.op("dve", lambda e, j=j: e.scalar_tensor_tensor(out=cqn[:, j, :], in0=cq[:, j, :],
                                                               scalar=cst[:, G.co["qn"] + j:G.co["qn"] + j + 1],
                                                               in1=rstd[:, :], op0=ALU.mult, op1=ALU.mult),
                 [cq, rstd, cst], [cqn])
        t1, t2 = t1s[0], t2s[0]
        b = G.bank()
        mm_chunk(b, 2560, 64)
        P.op("dve", lambda e: e.tensor_tensor(out=t1[0:32, :], in0=b[0:32, :], in1=rope[0:32, :], op=ALU.mult),
             [b, rope], [t1])
        P.op("dve", lambda e: e.tensor_tensor(out=t2[0:32, :], in0=b[32:64, :], in1=rope[32:64, :], op=ALU.mult),
             [b, rope], [t2])
        P.op("dve", lambda e: e.tensor_tensor(out=kr[0:32, :], in0=t1[0:32, :], in1=t2[0:32, :], op=ALU.add),
             [t1, t2], [kr])
        for h in range(8):
            P.dma("pool", G.d["kT"][h, 64:96, ts], kr[0:32, :], [kr], [G.dr["kT"]], chan="kr")
        for hp in range(4):
            b = G.bank()
            for j in range(2):
                P.op("pe", lambda e, j=j: e.matmul(b[:, :], lhsT=wkv[:, j, hp * 128:(hp + 1) * 128], rhs=ckvn[:, j, :],
                                                    start=(j == 0), stop=(j == 1)), [wkv, ckvn], [b])
            kk = kn[hp % 2]
            evac(kk[:, :], b, b[:, :], [b], [kk])
            P.dma("pool", G.d["kT"][2 * hp, 0:64, ts], kk[0:64, :], [kk], [G.dr["kT"]], chan=("kn", hp % 2))
            P.dma("pool", G.d["kT"][2 * hp + 1, 0:64, ts], kk[64:128, :], [kk], [G.dr["kT"]], chan=("kn", hp % 2))
        for tb in range(4):
            b = G.bank()
            for j in range(2):
                P.op("pe", lambda e, j=j: e.matmul(b[:, :], lhsT=ckvn[:, j, tb * 128:(tb + 1) * 128],
                                                    rhs=wkv[:, j, 512:1024], start=(j == 0), stop=(j == 1)),
                     [wkv, ckvn], [b])
            vv = vt[tb % 2]
            evac(vv[:, :], b, b[:, :], [b], [vv])
            r0 = ti * NT + tb * 128
            P.dma("pool", G.d["vtok"][r0:r0 + 128, :], vv[:, :], [vv], [G.dr["vtok"]], chan=("vt", tb % 2))
        for h in range(8):
            b = G.bank()
            for j in range(6):
                P.op("pe", lambda e, j=j: e.matmul(b[:, :], lhsT=wq[:, j, h * 128:(h + 1) * 128], rhs=cqn[:, j, :],
                                                    start=(j == 0), stop=(j == 5)), [wq, cqn], [b])
            q = qt[h % 2]
            t1, t2 = t1s[h % 2], t2s[h % 2]
            P.op("act", lambda e: e.activation(out=q[0:64, :], in_=b[0:64, :], func=AF.Copy), [b], [q])
            P.op("dve", lambda e: e.tensor_tensor(out=t1[64:96, :], in0=b[64:96, :], in1=rope[64:96, :], op=ALU.mult),
                 [b, rope], [t1])
            P.op("dve", lambda e: e.tensor_tensor(out=t2[64:96, :], in0=b[96:128, :], in1=rope[96:128, :], op=ALU.mult),
                 [b, rope], [t2])
            P.op("dve", lambda e: e.tensor_tensor(out=q[64:96, :], in0=t1[64:96, :], in1=t2[64:96, :], op=ALU.add),
                 [t1, t2], [q])
            P.dma("pool", G.d["qT"][h, :, ts], q[0:96, :], [q], [G.dr["qT"]], chan=("qt", h % 2))
    C.close()


def phase_mla_attn(P, G):
    C = Ctx(P)
    scale = 96 ** -0.5
    QT = [C.sb([96, S], BF16, "QT") for _ in range(2)]
    KT = [C.sb([96, S], BF16, "KT") for _ in range(2)]
    V = [C.sb([128, 32, 128], BF16, "V") for _ in range(2)]
    E = [C.sb([128, NT], BF16, "E") for _ in range(3)]
    cm = C.sb([128, 4, NT], BF16, "cm")
    rz = C.sb([64, NT], F32, "rz")
    yo = [C.sb([64, NT], BF16, "yo") for _ in range(2)]
    P.dma("sp", cm[:, :, :], G.d["cmask"].rearrange("o p n -> p o n"), [], [cm], chan="cm")
    for i in range(2):
        P.op("pool", lambda e: e.memset(V[i][:, :, 64:128], 1.0), [], [V[i]])
    sb = G.banks[0:3]
    ob = G.banks[4:6]
    vsrc = G.d["vtok"].rearrange("(c p) f -> p c f", p=128)
    its = [(T, c) for T in range(NTT) for c in range(4 * T + 4)]
    n_o = [0]

    def load(h):
        P.dma("sp", QT[h % 2][:, :], G.d["qT"][h, :, :], [G.dr["qT"]], [QT[h % 2]], chan=("QT", h % 2))
        P.dma("sp", KT[h % 2][:, :], G.d["kT"][h, :, :], [G.dr["kT"]], [KT[h % 2]], chan=("KT", h % 2))
        P.dma("sp", V[h % 2][:, :, 0:64], vsrc[:, :, h * 64:(h + 1) * 64], [G.dr["vtok"]], [V[h % 2]],
              chan=("V", h % 2))

    load(0)
    for h in range(8):
        if h + 1 < 8:
            load(h + 1)
        q, k, v = QT[h % 2], KT[h % 2], V[h % 2]

        def emit_s(n):
            T, c = its[n]
            b = sb[n % 3]
            diag = c >= 4 * T
            j0 = 128 * (c - 4 * T) if diag else 0
            P.op("pe", lambda e: e.matmul(b[:, j0:NT], lhsT=k[0:96, c * 128:(c + 1) * 128],
                                          rhs=q[0:96, T * NT + j0:(T + 1) * NT], start=True, stop=not diag), [k, q], [b])
            if diag:
                P.op("pe", lambda e: e.matmul(b[:, j0:NT], lhsT=G.ident[:, :], rhs=cm[:, c - 4 * T, j0:NT], start=False,
                                              stop=True), [G.ident, cm], [b])

        emit_s(0)
        emit_s(1)
        for n in range(len(its)):
            T, c = its[n]
            if n + 2 < len(its):
                emit_s(n + 2)
            b = sb[n % 3]
            ee = E[n % 3]
            j0 = 128 * (c - 4 * T) if c >= 4 * T else 0
            P.op("act", lambda e: e.activation(out=ee[:, j0:NT], in_=b[:, j0:NT], func=AF.Exp, scale=scale), [b], [ee])
            o = ob[T % 2]
            last = (c == 4 * T + 3)
            P.op("pe", lambda e: e.matmul(o[:, j0:NT], lhsT=v[:, c, :], rhs=ee[:, j0:NT], start=(c == 0), stop=last,
                                          skip_group_check=True), [v, ee], [o])
            if last:
                y = yo[n_o[0] % 2]
                n_o[0] += 1
                P.op("dve", lambda e: e.reciprocal(out=rz[0:64, :], in_=o[64:128, :]), [o], [rz])
                P.op("dve", lambda e: e.tensor_tensor(out=y[0:64, :], in0=o[0:64, :], in1=rz[0:64, :], op=ALU.mult),
                     [o, rz], [y])
                P.dma("pool", G.d["ycatT"][512 + h * 64:512 + (h + 1) * 64, T * NT:(T + 1) * NT], y[0:64, :], [y],
                      [G.dr["ycatT"]], chan=("yo", (n_o[0] - 1) % 2))
    C.close()


def phase_outproj(P, G, wname, yname, xin, xout, hout, gpost, gmpre, after_first=None):
    C = Ctx(P)
    wo = C.sb([128, 8, 1024], BF16, "wo")
    load_w_bf16(P, C, G.d[wname], wo, 8, 1024)
    yt = [C.sb([128, 8, NT], BF16, "yt") for _ in range(2)]
    xts = [C.sb([128, 8, NT], F32, "xt") for _ in range(2)]
    ms = [C.sb([128, 8, NT], F32, "m") for _ in range(2)]
    sqs = [C.sb([128, 8, NT], BF16, "sq")] * 2
    sq2s = [C.sb([128, 8, NT], BF16, "sq2")] * 2
    rstds = [C.sb([128, NT], F32, "rstd") for _ in range(2)]
    rstd2s = [C.sb([128, NT], F32, "rstd2") for _ in range(2)]
    h2s = [C.sb([128, 8, NT], BF16, "h2")] * 2
    def chunked(bufs):
        first = [Buf(bufs[0].t) for _ in range(8)]
        return [first, first if bufs[1] is bufs[0] else [Buf(bufs[1].t) for _ in range(8)]]
    xtc, mc, sqc, sq2c, h2c = chunked(xts), chunked(ms), chunked(sqs), chunked(sq2s), chunked(h2s)
    cst = G.cst
    y3 = G.d[yname].rearrange("(k p) t -> p k t", p=128)
    x3 = G.d[xin].rearrange("(k p) t -> p k t", p=128)
    xo3 = G.d[xout].rearrange("(k p) t -> p k t", p=128)
    ho3 = G.d[hout].rearrange("(k p) t -> p k t", p=128)
    mb = G.banks[0:6]
    bAs = G.banks[6]
    bBs = G.banks[7]
    nb = [0]

    def ones_mm(bank, sq, sqk, k):
        P.op("pe", lambda e: e.matmul(bank[:, :], lhsT=G.ones[:, :], rhs=sq[:, k, :], start=(k == 0), stop=(k == 7)),
             [sqk[k], G.ones], [bank])

    def rstd_ops(bank, rstd):
        P.op("act", lambda e: e.activation(out=rstd[:, :], in_=bank[:, :], func=AF.Sqrt, bias=G.eps[:, 0:1], scale=1.0 / 1024),
             [bank, G.eps], [rstd])
        P.op("dve", lambda e: e.reciprocal(out=rstd[:, :], in_=rstd[:, :]), [rstd], [rstd])

    def epi1_step(tp, k):
        xt, m, sq2, rstd = xts[tp % 2], ms[tp % 2], sq2s[tp % 2], rstds[tp % 2]
        mk, xk, s2k = mc[tp % 2][k], xtc[tp % 2][k], sq2c[tp % 2][k]
        P.op("dve", lambda e: e.scalar_tensor_tensor(out=m[:, k, :], in0=m[:, k, :],
                                                     scalar=cst[:, G.co[gpost] + k:G.co[gpost] + k + 1],
                                                     in1=rstd[:, :], op0=ALU.mult, op1=ALU.mult), [mk, rstd, cst], [mk])
        P.op("pool", lambda e: e.tensor_tensor(out=xt[:, k, :], in0=xt[:, k, :], in1=m[:, k, :], op=ALU.add), [xk, mk], [xk])
        P.op("act", lambda e: e.activation(out=sq2[:, k, :], in_=xt[:, k, :], func=AF.Square), [xk], [s2k])

    def epi2(tp):
        ts = slice(tp * NT, (tp + 1) * NT)
        xt, h2, rstd2 = xts[tp % 2], h2s[tp % 2], rstd2s[tp % 2]
        P.dma("pool", xo3[:, :, ts], xt[:, :, :], xtc[tp % 2], [G.dr[xout]], chan=("opxo", tp % 2))
        rstd_ops(bBs, rstd2)
        for k in range(8):
            P.op("dve", lambda e: e.scalar_tensor_tensor(out=h2[:, k, :], in0=xt[:, k, :],
                                                         scalar=cst[:, G.co[gmpre] + k:G.co[gmpre] + k + 1],
                                                         in1=rstd2[:, :], op0=ALU.mult, op1=ALU.mult),
                 [xtc[tp % 2][k], rstd2, cst], [h2c[tp % 2][k]])
        P.dma("pool", ho3[:, :, ts], h2[:, :, :], h2c[tp % 2], [G.dr[hout]], chan=("opho", tp % 2))

    def main(ti, prev):
        if ti is not None:
            ts = slice(ti * NT, (ti + 1) * NT)
            y, xt, m, sq = yt[ti % 2], xts[ti % 2], ms[ti % 2], sqs[ti % 2]
            P.dma("sp", y[:, :, :], y3[:, :, ts], [G.dr[yname]], [y], chan=("opy", ti % 2))
            P.dma("sp", xt[:, :, :], x3[:, :, ts], [G.dr[xin]], xtc[ti % 2], chan=("opx", ti % 2))
        if prev is not None:
            ones_mm(bAs, sqs[prev % 2], sqc[prev % 2], 7)
            rstd_ops(bAs, rstds[prev % 2])
        for oc in range(8):
            if ti is not None:
                b = mb[nb[0] % 6]
                nb[0] += 1
                for k in range(8):
                    P.op("pe", lambda e: e.matmul(b[:, :], lhsT=wo[:, k, oc * 128:(oc + 1) * 128], rhs=y[:, k, :],
                                                  start=(k == 0), stop=(k == 7)), [wo, y], [b])
                P.op("dve", lambda e: e.tensor_copy(out=m[:, oc, :], in_=b[:, :]), [b], [mc[ti % 2][oc]])
                P.op("act", lambda e: e.activation(out=sq[:, oc, :], in_=m[:, oc, :], func=AF.Square), [mc[ti % 2][oc]],
                     [sqc[ti % 2][oc]])
                if oc > 0:
                    ones_mm(bAs, sq, sqc[ti % 2], oc - 1)
            if prev is not None:
                epi1_step(prev, oc)
                if oc > 1:
                    ones_mm(bBs, sq2s[prev % 2], sq2c[prev % 2], oc - 2)
        if prev is not None:
            ones_mm(bBs, sq2s[prev % 2], sq2c[prev % 2], 6)
            ones_mm(bBs, sq2s[prev % 2], sq2c[prev % 2], 7)
            epi2(prev)

    for ti in range(NTT):
        main(ti, ti - 1 if ti > 0 else None)
        if ti == 1 and after_first is not None:
            after_first()
    main(None, NTT - 1)
    C.close()


def prefetch_w1(P, G, w1name):
    Cw = Ctx(P)
    W1 = Cw.sb([128, 8, 4096], BF16, "W1")
    Cw.deferred_load = lambda: load_w_bf16(P, Cw, G.d[w1name], W1, 8, 4096)
    return Cw, W1


def phase_mlp(P, G, w1name, w2name, hin, xin, xout, hout, gpost, gnext, W1=None):
    C = Ctx(P)
    MT = 256
    if W1 is None:
        W1 = C.sb([128, 8, 4096], BF16, "W1")
        load_w_bf16(P, C, G.d[w1name], W1, 8, 4096)
    W2 = C.sb([128, 32, 1024], BF16, "W2")
    W2c = [Buf(W2.t) for _ in range(32)]
    ht = [C.sb([128, 8, MT], BF16, "ht") for _ in range(2)]
    xts = [C.sb([128, 8, MT], F32, "xt") for _ in range(2)]
    mos = [C.sb([128, 8, MT], F32, "mo") for _ in range(2)]
    sqs = [C.sb([128, 8, MT], BF16, "sq") for _ in range(2)]
    rstds = [C.sb([128, MT], F32, "rstd") for _ in range(2)]
    hns = [C.sb([128, 8, MT], BF16, "hn") for _ in range(2)]
    rl = [C.sb([128, MT], F32, "rl") for _ in range(2)]
    hid = [C.sb([128, MT], BF16, "hid") for _ in range(3)]
    cst = G.cst
    h3 = G.d[hin].rearrange("(k p) t -> p k t", p=128)
    x3 = G.d[xin].rearrange("(k p) t -> p k t", p=128)
    xo3 = G.d[xout].rearrange("(k p) t -> p k t", p=128)
    hb = G.banks[0:3]
    eb = G.banks[3]
    obk = G.banks[4:8]
    ntile = S // MT

    def epilogue(ti):
        ts = slice(ti * MT, (ti + 1) * MT)
        xt, mo, sq, rstd, hn = xts[ti % 2], mos[ti % 2], sqs[ti % 2], rstds[ti % 2], hns[ti % 2]
        P.op("act", lambda e: e.activation(out=sq[:, :, :], in_=mo[:, :, :], func=AF.Square), [mo], [sq])
        for k in range(8):
            P.op("pe", lambda e: e.matmul(eb[:, 0:MT], lhsT=G.ones[:, :], rhs=sq[:, k, :], start=(k == 0), stop=(k == 7)),
                 [sq, G.ones], [eb])
        P.op("act", lambda e: e.activation(out=rstd[:, :], in_=eb[:, 0:MT], func=AF.Sqrt, bias=G.eps[:, 0:1], scale=1.0 / 1024),
             [eb, G.eps], [rstd])
        P.op("dve", lambda e: e.reciprocal(out=rstd[:, :], in_=rstd[:, :]), [rstd], [rstd])
        for k in range(8):
            P.op("dve", lambda e: e.scalar_tensor_tensor(out=mo[:, k, :], in0=mo[:, k, :],
                                                         scalar=cst[:, G.co[gpost] + k:G.co[gpost] + k + 1],
                                                         in1=rstd[:, :], op0=ALU.mult, op1=ALU.mult), [mo, rstd, cst], [mo])
        P.op("pool", lambda e: e.tensor_tensor(out=xt[:, :, :], in0=xt[:, :, :], in1=mo[:, :, :], op=ALU.add), [xt, mo], [xt])
        P.dma("pool", xo3[:, :, ts], xt[:, :, :], [xt], [G.dr[xout]], chan=("mlpxo", ti % 2))

    def epilogue_b(ti):
        ts = slice(ti * MT, (ti + 1) * MT)
        xt, mo, sq, rstd, hn = xts[ti % 2], mos[ti % 2], sqs[ti % 2], rstds[ti % 2], hns[ti % 2]
        if hout is not None:
            ho3 = G.d[hout].rearrange("(k p) t -> p k t", p=128)
            P.op("act", lambda e: e.activation(out=sq[:, :, :], in_=xt[:, :, :], func=AF.Square), [xt], [sq])
            for k in range(8):
                P.op("pe", lambda e: e.matmul(eb[:, 0:MT], lhsT=G.ones[:, :], rhs=sq[:, k, :], start=(k == 0), stop=(k == 7)),
                     [sq, G.ones], [eb])
            P.op("act", lambda e: e.activation(out=rstd[:, :], in_=eb[:, 0:MT], func=AF.Sqrt, bias=G.eps[:, 0:1],
                                               scale=1.0 / 1024), [eb, G.eps], [rstd])
            P.op("dve", lambda e: e.reciprocal(out=rstd[:, :], in_=rstd[:, :]), [rstd], [rstd])
            for k in range(8):
                P.op("dve", lambda e: e.scalar_tensor_tensor(out=hn[:, k, :], in0=xt[:, k, :],
                                                             scalar=cst[:, G.co[gnext] + k:G.co[gnext] + k + 1],
                                                             in1=rstd[:, :], op0=ALU.mult, op1=ALU.mult), [xt, rstd, cst], [hn])
            P.dma("pool", ho3[:, :, ts], hn[:, :, :], [hn], [G.dr[hout]], chan=("mlpho", ti % 2))

    for ti in range(ntile):
        ts = slice(ti * MT, (ti + 1) * MT)
        h = ht[ti % 2]
        xt = xts[ti % 2]
        mo = mos[ti % 2]
        P.dma("sp", h[:, :, :], h3[:, :, ts], [G.dr[hin]], [h], chan=("mlph", ti % 2))
        P.dma("sp", xt[:, :, :], x3[:, :, ts], [G.dr[xin]], [xt], chan=("mlpx", ti % 2))
        if ti == 0:
            load_w_bf16(P, C, G.d[w2name], W2, 32, 1024, stage=W2c)

        def emit_h(f):
            b = hb[f % 3]
            for k in range(8):
                P.op("pe", lambda e: e.matmul(b[:, 0:MT], lhsT=W1[:, k, f * 128:(f + 1) * 128], rhs=h[:, k, :],
                                              start=(k == 0), stop=(k == 7)), [W1, h], [b])
        emit_h(0)
        emit_h(1)
        for f in range(32):
            if f + 2 < 32:
                emit_h(f + 2)
            b = hb[f % 3]
            r = rl[f % 2]
            hd = hid[f % 3]
            P.op("act", lambda e: e.activation(out=r[:, :], in_=b[:, 0:MT], func=AF.Relu), [b], [r])
            P.op("dve", lambda e: e.tensor_tensor(out=hd[:, :], in0=r[:, :], in1=r[:, :], op=ALU.mult), [r], [hd])
            for oc in range(8):
                o = obk[oc // 2]
                P.op("pe", lambda e: e.matmul(o[:, (oc % 2) * MT:(oc % 2 + 1) * MT], lhsT=W2[:, f, oc * 128:(oc + 1) * 128],
                                              rhs=hd[:, :], start=(f == 0 and oc % 2 == 0), stop=(f == 31),
                                              skip_group_check=True), [W2c[f], hd], [o])
            if f == 4 and ti > 0:
                epilogue(ti - 1)
            if f == 18 and ti > 0:
                epilogue_b(ti - 1)
        for ob_i in range(4):
            o = obk[ob_i]
            if ob_i % 2:
                P.op("act", lambda e: e.activation(out=mo[:, 2 * ob_i:2 * ob_i + 2, :], in_=o[:, :].rearrange("p (a n) -> p a n", a=2),
                                                   func=AF.Copy), [o], [mo])
            else:
                P.op("dve", lambda e: e.tensor_copy(out=mo[:, 2 * ob_i:2 * ob_i + 2, :], in_=o[:, :].rearrange("p (a n) -> p a n", a=2)),
                     [o], [mo])
    epilogue(ntile - 1)
    epilogue_b(ntile - 1)
    C.close()


def phase_front_c(P, G):
    C = Ctx(P)
    NIN = 2608
    win = C.sb([128, 8, NIN], BF16, "cwin")
    stage = None
    load_w_bf16(P, C, G.d["c_w_in"], win, 8, NIN, stage)
    ht = [C.sb([128, 8, NT], BF16, "ht") for _ in range(2)]
    ob = [C.sb([128, NT], BF16, "ob") for _ in range(3)]
    gt = C.sb([48, NT], BF16, "gt")
    h3 = G.d["h3T"].rearrange("(k p) t -> p k t", p=128)
    n = 0
    for ti in range(NTT):
        ts = slice(ti * NT, (ti + 1) * NT)
        h = ht[ti % 2]
        P.dma("sp", h[:, :, :], h3[:, :, ts], [G.dr["h3T"]], [h], chan=("fch", ti % 2))
        for c in range(16):
            b = G.bank()
            for k in range(8):
                P.op("pe", lambda e: e.matmul(b[:, :], lhsT=win[:, k, c * 128:(c + 1) * 128], rhs=h[:, k, :],
                                              start=(k == 0), stop=(k == 7)), [win, h], [b])
            o = ob[n % 3]
            if n % 2:
                P.op("act", lambda e: e.activation(out=o[:, :], in_=b[:, :], func=AF.Copy), [b], [o])
            else:
                P.op("dve", lambda e: e.tensor_copy(out=o[:, :], in_=b[:, :]), [b], [o])
            if c < 8:
                d0, d1, nm = G.d["qTn"][2 * c, :, ts], G.d["qTn"][2 * c + 1, :, ts], "qTn"
            else:
                wh, gp = (c - 8) // 2, (c - 8) % 2
                d0, d1, nm = G.d["kvT"][wh, 2 * gp, :, ts], G.d["kvT"][wh, 2 * gp + 1, :, ts], "kvT"
            P.dma("pool", d0, o[0:64, :], [o], [G.dr[nm]], chan=("fco", n % 3))
            P.dma("pool", d1, o[64:128, :], [o], [G.dr[nm]], chan=("fco", n % 3))
            n += 1
        b = G.bank()
        for k in range(8):
            P.op("pe", lambda e: e.matmul(b[0:48, :], lhsT=win[:, k, 2560:2608], rhs=h[:, k, :], start=(k == 0),
                                          stop=(k == 7)), [win, h], [b])
        P.op("act", lambda e: e.activation(out=gt[0:48, :], in_=b[0:48, :], func=AF.Sigmoid), [b], [gt])
        P.dma("pool", G.d["gT"][:, ts], gt[0:48, :], [gt], [G.dr["gT"]], chan="fcg")
        for tb in range(4):
            b = G.bank()
            for k in range(8):
                P.op("pe", lambda e: e.matmul(b[:, :], lhsT=h[:, k, tb * 128:(tb + 1) * 128], rhs=win[:, k, 2048:2560],
                                              start=(k == 0), stop=(k == 7)), [win, h], [b])
            o = ob[n % 3]
            if n % 2:
                P.op("act", lambda e: e.activation(out=o[:, :], in_=b[:, :], func=AF.Copy), [b], [o])
            else:
                P.op("dve", lambda e: e.tensor_copy(out=o[:, :], in_=b[:, :]), [b], [o])
            r0 = ti * NT + tb * 128
            P.dma("pool", G.d["vsw"][r0:r0 + 128, :], o[:, :], [o], [G.dr["vsw"]], chan=("fco", n % 3))
            n += 1
    C.close()


def phase_compress(P, G):
    C = Ctx(P)
    stage = C.sb([64, 2048], F32, "cstg")
    w1 = [C.sb([64, 2048], BF16, "cw1") for _ in range(2)]
    w2 = [C.sb([64, 64], BF16, "cw2") for _ in range(2)]
    peT = [C.sb([64, 32], BF16, "cpe") for _ in range(2)]
    bias = [C.sb([64, 1], F32, "cbias") for _ in range(2)]
    for kv, sfx in enumerate(("k", "v")):
        P.dma("sp", stage[:, :], G.d["cw1" + sfx][:, :], [], [stage], chan="cstg")
        P.op("dve", lambda e: e.tensor_copy(out=w1[kv][:, :], in_=stage[:, :]), [stage], [w1[kv]])
        P.dma("sp", stage[:, 0:64], G.d["cw2" + sfx][:, :], [], [stage], chan="cstg")
        P.op("dve", lambda e: e.tensor_copy(out=w2[kv][:, :], in_=stage[:, 0:64]), [stage], [w2[kv]])
        P.dma("sp", stage[:, 0:32], G.d["cpe" + sfx][:, :], [], [stage], chan="cstg")
        P.op("dve", lambda e: e.tensor_copy(out=peT[kv][:, :], in_=stage[:, 0:32]), [stage], [peT[kv]])
        b = G.bank()
        for l in range(32):
            P.op("pe", lambda e: e.matmul(b[0:64, 0:1], lhsT=w1[kv][:, l * 64:(l + 1) * 64], rhs=peT[kv][:, l:l + 1],
                                          start=(l == 0), stop=(l == 31)), [w1[kv], peT[kv]], [b])
        P.op("dve", lambda e: e.tensor_copy(out=bias[kv][:, :], in_=b[0:64, 0:1]), [b], [bias[kv]])
    zt = [C.sb([64, S], BF16, "zt") for _ in range(2)]
    xb = C.sb([64, 256], F32, "xb")
    tt = C.sb([64, 256], F32, "tt")
    sg = C.sb([64, 256], F32, "sg")
    hdn = C.sb([64, 256], BF16, "hdn")
    P.op("pool", lambda e: e.memset(hdn[:, :], 0.0), [], [hdn])
    n = 0
    for g in range(4):
        KC, VC = G.KC[g], G.VC[g]
        P.op("pool", lambda e: e.memset(KC[:, :], 0.0), [], [KC])
        P.op("pool", lambda e: e.memset(KC[64:65, :], 1.0), [], [KC])
        P.op("pool", lambda e: e.memset(VC[:, :, 0:64], 0.0), [], [VC])
        P.op("pool", lambda e: e.memset(VC[:, :, 64:128], 1.0), [], [VC])
        for kv in range(2):
            z = zt[n % 2]
            n += 1
            P.dma("sp", z[:, :], G.d["kvT"][kv, g, :, :], [G.dr["kvT"]], [z], chan=("zt", n % 2))
            b = G.bank()
            for l in range(32):
                P.op("pe", lambda e: e.matmul(b[0:64, 0:255], lhsT=w1[kv][:, l * 64:(l + 1) * 64],
                                              rhs=z[:, l:l + 16 * 254 + 1:16], start=(l == 0), stop=(l == 31)),
                     [w1[kv], z], [b])
            P.op("dve", lambda e: e.tensor_scalar(out=xb[:, 0:255], in0=b[0:64, 0:255], scalar1=bias[kv][:, 0:1],
                                                  scalar2=None, op0=ALU.add), [b, bias[kv]], [xb])
            P.op("dve", lambda e: e.tensor_tensor(out=tt[:, 0:255], in0=xb[:, 0:255], in1=xb[:, 0:255], op=ALU.mult),
                 [xb], [tt])
            P.op("dve", lambda e: e.tensor_scalar(out=tt[:, 0:255], in0=tt[:, 0:255], scalar1=0.044715, scalar2=1.0,
                                                  op0=ALU.mult, op1=ALU.add), [tt], [tt])
            P.op("dve", lambda e: e.tensor_tensor(out=tt[:, 0:255], in0=tt[:, 0:255], in1=xb[:, 0:255], op=ALU.mult),
                 [tt, xb], [tt])
            P.op("act", lambda e: e.activation(out=sg[:, 0:255], in_=tt[:, 0:255], func=AF.Sigmoid,
                                               scale=2.0 * 0.7978845608028654), [tt], [sg])
            P.op("dve", lambda e: e.tensor_tensor(out=hdn[:, 0:255], in0=xb[:, 0:255], in1=sg[:, 0:255], op=ALU.mult),
                 [xb, sg], [hdn])
            b2 = G.bank()
            if kv == 0:
                P.op("pe", lambda e: e.matmul(b2[0:64, 0:255], lhsT=w2[0][:, :], rhs=hdn[:, 0:255], start=True, stop=True),
                     [w2[0], hdn], [b2])
                P.op("act", lambda e: e.activation(out=KC[0:64, 0:255], in_=b2[0:64, 0:255], func=AF.Copy), [b2], [KC])
            else:
                for c in range(2):
                    ncol = 128 if c == 0 else 127
                    b3 = G.bank()
                    P.op("pe", lambda e: e.matmul(b3[0:ncol, 0:64], lhsT=hdn[:, c * 128:c * 128 + ncol], rhs=w2[1][:, :],
                                                  start=True, stop=True), [w2[1], hdn], [b3])
                    P.op("act", lambda e: e.activation(out=VC[0:ncol, c, 0:64], in_=b3[0:ncol, 0:64], func=AF.Copy),
                         [b3], [VC])
    C.close()


def phase_nsa(P, G, debug=False):
    C = Ctx(P)
    scale = 0.125
    Q = [C.sb([128, S], BF16, "Q") for _ in range(4)]
    Qs = [[C.sb([128, NT], BF16, "Qs") for _ in range(4)] for _ in range(2)]
    KS = C.sb([128, S], BF16, "KS")
    KW = C.sb([128, S], BF16, "KW")
    VS = C.sb([128, 32, 128], BF16, "VS")
    VW = C.sb([128, 32, 128], BF16, "VW")
    gT = C.sb([48, S], BF16, "gT")
    E = [C.sb([128, NT], BF16, "E") for _ in range(3)]
    masks = C.sb([128, 13, NT], BF16, "masks")
    selm = C.sb([48, 48 * 64], BF16, "selm")
    ov2 = C.sb([128, 2, 65], BF16, "ov2")
    btab = C.sb([128, 704], F32, "btab")
    qterm = C.sb([64, 4, NT], F32, "qterm")
    addt = C.sb([128, 4, 64], F32, "addt")
    acc = [[C.sb([64, NT], F32, "acc") for _ in range(4)] for _ in range(2)]
    imp = C.sb([128, 4, 64], F32, "imp")
    itmp = C.sb([128, 4, 64], F32, "itmp")
    zr = C.sb([128, 4, 1], F32, "zr")
    rz = C.sb([64, NT], F32, "rz")
    coefs = [C.sb([64, NT], F32, "coef") for _ in range(2)]
    ftmp = C.sb([64, NT], F32, "ftmp")
    m8 = C.sb([128, 16], F32, "m8")
    wk = C.sb([128, 64], F32, "wk")
    vv = C.sb([128, 64], F32, "vv")
    ngf = C.sb([128, 64], F32, "ngf")
    ng = C.sb([128, 64], BF16, "ng")
    yo = [C.sb([64, NT], BF16, "yo") for _ in range(2)]
    sbk = G.banks[0:3]
    obk = G.banks[3:5]
    obc = G.banks[5]
    ub = G.banks[6]
    gb = G.banks[7]
    tb = G.banks[7]
    P.dma("sp", masks[:, 0:4, :], G.d["cmask"].rearrange("o p n -> p o n"), [], [masks], chan="nsac")
    P.dma("sp", masks[:, 4:8, :], G.d["wmask"].rearrange("o p n -> p o n"), [], [masks], chan="nsac")
    P.dma("sp", masks[:, 8:13, :], G.d["cmpmask"].rearrange("o p n -> p o n"), [], [masks], chan="nsac")
    P.dma("sp", selm[:, :], G.d["selm"][:, :], [], [selm], chan="nsac")
    P.dma("sp", ov2[:, :, :], G.d["ov2"][:, :, :], [], [ov2], chan="nsac")
    P.dma("sp", btab[:, :], G.d["biastab"][:, :], [], [btab], chan="nsac")
    P.dma("sp", gT[:, :], G.d["gT"][:, :], [G.dr["gT"]], [gT], chan="nsac")
    P.dma("sp", KS[64:128, :], G.d["expand"][:, :], [], [KS], chan="nsak")
    P.op("pool", lambda e: e.memset(KW[64:128, :], 0.0), [], [KW])
    P.op("pool", lambda e: e.memset(KW[64:65, :], 1.0), [], [KW])
    for r in range(4):
        P.op("pool", lambda e: e.memset(Q[r][64:128, :], 0.0), [], [Q[r]])
    P.op("pool", lambda e: e.memset(VS[:, :, 64:128], 1.0), [], [VS])
    P.op("pool", lambda e: e.memset(VW[:, :, 64:128], 1.0), [], [VW])
    vsrc = G.d["vsw"].rearrange("(c p) f -> p c f", p=128)
    cnt = {"s": 0, "o": 0, "y": 0, "f": 0}
    NOSB = 8
    osb = [C.sb([128, NT], F32, "osb") for _ in range(NOSB)]
    TORDER = [0, 7, 1, 6, 2, 5, 3, 4]
    PAR = {T: i % 2 for i, T in enumerate(TORDER)}
    gbs = [C.sb([64, 12, NT], BF16, "gbs") for _ in range(2)]
    ngs = [C.sb([128, 64], BF16, "ngs") for _ in range(4)]

    def finalize(o, br, h, T, r, first):
        ts = slice(T * NT, (T + 1) * NT)
        col = (br * 16 + h) * 64
        os_ = osb[cnt["f"] % NOSB]
        cnt["f"] += 1
        if T >= 4:
            P.op("dve", lambda e: e.tensor_scalar(out=os_[:, :], in0=o[:, :], scalar1=G.tiny[:, 0:1], scalar2=None, op0=ALU.add),
                 [o, G.tiny], [os_])
        else:
            P.op("act", lambda e: e.activation(out=os_[:, :], in_=o[:, :], func=AF.Identity, bias=G.tiny[:, 0:1], scale=1.0),
                 [o, G.tiny], [os_])
        gsb = gbs[PAR[T]]
        gi = br * 4 + r
        coef = coefs[cnt["f"] % 2]
        P.op("dve", lambda e: e.reciprocal(out=coef[0:64, :], in_=os_[64:128, :]), [os_], [coef])
        P.op("dve", lambda e: e.tensor_tensor(out=coef[0:64, :], in0=gsb[0:64, gi, :], in1=coef[0:64, :], op=ALU.mult),
             [gsb, coef], [coef])
        a = acc[PAR[T]][r]
        if first:
            P.op("pool", lambda e: e.tensor_tensor(out=a[0:64, :], in0=os_[0:64, :], in1=coef[0:64, :], op=ALU.mult),
                 [os_, coef], [a])
        else:
            P.op("pool", lambda e: e.tensor_tensor(out=ftmp[0:64, :], in0=os_[0:64, :], in1=coef[0:64, :], op=ALU.mult),
                 [os_, coef], [ftmp])
            P.op("pool", lambda e: e.tensor_tensor(out=a[0:64, :], in0=a[0:64, :], in1=ftmp[0:64, :], op=ALU.add),
                 [a, ftmp], [a])
        if debug:
            P.dma("pool", G.d["dbgacc"][br, h * 64:(h + 1) * 64, ts], a[0:64, :], [a], [G.dr["dbgacc"]], chan="dbg")

    def run_stream(items):
        base = cnt["s"]
        cnt["s"] += len(items)
        index = {id(it): i for i, it in enumerate(items)}
        depi = [index[id(it["dep"])] if it.get("dep") is not None else -1 for it in items]
        st = {"emitted": 0}

        def emit_upto(limit, done):
            while st["emitted"] <= limit and st["emitted"] < len(items) and depi[st["emitted"]] <= done:
                m = st["emitted"]
                items[m]["emit_s"](items[m]["c"], sbk[(base + m) % 3], items[m].get("cols", (0, NT)))
                st["emitted"] += 1

        for n, it in enumerate(items):
            for hook in it.get("pre", ()):
                hook()
            emit_upto(n + 2, n - 1)
            assert st["emitted"] > n
            b = sbk[(base + n) % 3]
            ee = E[(base + n) % 3]
            bc = it["bc"]
            o = it["o"]
            vtile = it["v"]
            c = it["c"]
            j0, j1 = it.get("cols", (0, NT))
            P.op("act", lambda e: e.activation(out=ee[:, j0:j1], in_=b[:, j0:j1], func=AF.Exp, bias=btab[:, bc:bc + 1],
                                               scale=scale), [b, btab], [ee])
            P.op("pe", lambda e: e.matmul(o[:, j0:j1], lhsT=vtile[:, c, :], rhs=ee[:, j0:j1], start=it["first"],
                                          stop=it["last"], skip_group_check=True), [vtile, ee], [o])
            if it.get("post"):
                it["post"](ee)
            if it["last"]:
                it["fin"]()

    for g in range(4):
        KC, VC = G.KC[g], G.VC[g]
        P.dma("sp", KS[0:64, :], G.d["kvT"][2, g, :, :], [G.dr["kvT"]], [KS], chan="nsak")
        P.dma("sp", KW[0:64, :], G.d["kvT"][3, g, :, :], [G.dr["kvT"]], [KW], chan="nsak")
        P.dma("sp", VS[:, :, 0:64], vsrc[:, :, g * 64:(g + 1) * 64], [G.dr["vsw"]], [VS], chan="nsav")
        P.dma("sp", VW[:, :, 0:64], vsrc[:, :, 256 + g * 64:256 + (g + 1) * 64], [G.dr["vsw"]], [VW], chan="nsav")
        P.dma("sp", qterm[:, :, :], G.d["qterm"][4 * g:4 * g + 4, :, :].rearrange("r p n -> p r n"), [], [qterm], chan="nsaqt")
        for r in range(4):
            P.dma("sp", Q[r][0:64, :], G.d["qTn"][4 * g + r, :, :], [G.dr["qTn"]], [Q[r]], chan=("nsaq", r))
            P.dma("sp", Q[r][64:65, :], G.d["qaug"][4 * g + r, :, :], [], [Q[r]], chan=("nsaq", r))
        def cmp_items(T):
            ts = slice(T * NT, (T + 1) * NT)
            out = []
            for r in range(4):
                h = 4 * g + r
                q = Q[r]
                chunks = [0] if T <= 3 else [0, 1]
                o = obc

                def mk_emit(q):
                    def emit_cmp(c, b, cols=None):
                        dl = T - 4 * c
                        P.op("pe", lambda e: e.matmul(b[:, :], lhsT=KC[:, c * 128:(c + 1) * 128], rhs=q[:, ts], start=True,
                                                      stop=(dl > 4)), [KC, q], [b])
                        if dl <= 4:
                            P.op("pe", lambda e: e.matmul(b[:, :], lhsT=G.ident[:, :], rhs=masks[:, 8 + dl, :], start=False,
                                                          stop=True), [G.ident, masks], [b])
                    return emit_cmp

                def mk_post(c, nlast):
                    def post(ee):
                        for s in range(4):
                            P.op("pe", lambda e: e.matmul(ub[:, s * 65:(s + 1) * 65], lhsT=ee[:, s * 128:(s + 1) * 128],
                                                          rhs=ov2[:, c, :], start=(c == 0 and s == 0), stop=nlast,
                                                          skip_group_check=True), [ov2, ee], [ub])
                    return post

                def mk_fin(o, h, r):
                    def fin():
                        if r == 0:
                            P.dma("sp", addt[:, :, :], G.d["addtab"][:, 4 * T:4 * T + 4, :], [], [addt], chan="addt")
                        finalize(o, 0, h, T, r, True)
                        u3 = ub[:, 0:260].rearrange("p (a n) -> p a n", a=4)
                        P.op("dve", lambda e: e.tensor_scalar(out=zr[:, :, :], in0=u3[:, :, 64:65], scalar1=1e-30, scalar2=None,
                                                              op0=ALU.max), [ub], [zr])
                        P.op("dve", lambda e: e.reciprocal(out=zr[:, :, :], in_=zr[:, :, :]), [zr], [zr])
                        if r == 0:
                            P.op("dve", lambda e: e.tensor_tensor(out=imp[:, :, :], in0=u3[:, :, 0:64],
                                                                  in1=zr[:, :, 0:1].to_broadcast([128, 4, 64]), op=ALU.mult),
                                 [ub, zr], [imp])
                        else:
                            P.op("dve", lambda e: e.tensor_tensor(out=itmp[:, :, :], in0=u3[:, :, 0:64],
                                                                  in1=zr[:, :, 0:1].to_broadcast([128, 4, 64]), op=ALU.mult),
                                 [ub, zr], [itmp])
                            P.op("pool", lambda e: e.tensor_tensor(out=imp[:, :, :], in0=imp[:, :, :], in1=itmp[:, :, :],
                                                                   op=ALU.add), [imp, itmp], [imp])
                        if r == 3:
                            topk(T)
                    return fin

                es = mk_emit(q)
                fn = mk_fin(o, h, r)
                for n, c in enumerate(chunks):
                    out.append(dict(emit_s=es, c=c, bc=512 + h * 12 + (T if c == 0 else 8 + (T - 4)), o=o, v=VC,
                                    first=(n == 0), last=(n == len(chunks) - 1), fin=fn, post=mk_post(c, n == len(chunks) - 1)))
            return out

        def load_gates(T):
            ts = slice(T * NT, (T + 1) * NT)
            gsb = gbs[PAR[T]]
            for br in range(3):
                for r in range(4):
                    row = br * 16 + 4 * g + r
                    P.dma("sp", gsb[0:64, br * 4 + r, :], G.d["gT"][row:row + 1, ts].partition_broadcast(64),
                          [G.dr["gT"]], [gsb], chan=("gbs", PAR[T]), acc=True)

        def topk(T):
            ts = slice(T * NT, (T + 1) * NT)
            if debug:
                P.dma("pool", G.d["dbgimp"][g, :, 4 * T:4 * T + 4, :], imp[:, :, :], [imp], [G.dr["dbgimp"]], chan="dbg")
            for s in range(4):
                P.op("dve", lambda e: e.tensor_tensor(out=vv[:, :], in0=imp[:, s, :], in1=addt[:, s, :], op=ALU.add),
                     [imp, addt], [vv])
                P.op("dve", lambda e: e.max(out=m8[:, 0:8], in_=vv[:, :]), [vv], [m8])
                P.op("dve", lambda e: e.match_replace(out=wk[:, :], in_to_replace=m8[:, 0:8], in_values=vv[:, :],
                                                      imm_value=-3.0e38), [m8, vv], [wk])
                P.op("dve", lambda e: e.max(out=m8[:, 8:16], in_=wk[:, :]), [wk], [m8])
                P.op("dve", lambda e: e.tensor_scalar(out=ngf[:, :], in0=vv[:, :], scalar1=m8[:, 15:16], scalar2=30000.0,
                                                      op0=ALU.is_ge, op1=ALU.mult), [vv, m8], [ngf])
                ngx = ngs[s]
                P.op("dve", lambda e: e.tensor_scalar(out=ngx[:, :], in0=ngf[:, :], scalar1=-30000.0, scalar2=None,
                                                      op0=ALU.add), [ngf], [ngx])

        def topk2(T):
            ts = slice(T * NT, (T + 1) * NT)
            for s in range(4):
                ngx = ngs[s]
                P.op("pe", lambda e: e.matmul(tb[0:64, s * 128:(s + 1) * 128], lhsT=ngx[:, :], rhs=G.ident[:, :], start=True,
                                              stop=True), [ngx, G.ident], [tb])
            qs = Qs[PAR[T]]
            for r in range(4):
                P.op("dve", lambda e: e.tensor_tensor(out=qs[r][64:128, :], in0=tb[0:64, :], in1=qterm[0:64, r, :], op=ALU.add),
                     [tb, qterm], [qs[r]])
                P.op("pool", lambda e: e.tensor_copy(out=qs[r][0:64, :], in_=Q[r][0:64, ts]), [Q[r]], [qs[r]])

        def sw_items(T, dep):
            ts = slice(T * NT, (T + 1) * NT)
            qs = Qs[PAR[T]]
            items = []
            specs = []
            for r in range(4):
                h = 4 * g + r

                def mk_slc(qq):
                    def emit_slc(c, b, cols):
                        diag = c >= 4 * T
                        j0, j1 = cols
                        P.op("pe", lambda e: e.matmul(b[:, j0:j1], lhsT=KS[:, c * 128:(c + 1) * 128], rhs=qq[:, j0:j1], start=True,
                                                      stop=not diag), [KS, qq], [b])
                        if diag:
                            P.op("pe", lambda e: e.matmul(b[:, j0:j1], lhsT=G.ident[:, :], rhs=masks[:, c - 4 * T, j0:j1],
                                                          start=False, stop=True), [G.ident, masks], [b])
                    return emit_slc

                def mk_win(q):
                    def emit_win(c, b, cols):
                        mi = (c - 4 * T) if c >= 4 * T else 4 + (c - (4 * T - 4))
                        j0, j1 = cols
                        P.op("pe", lambda e: e.matmul(b[:, j0:j1], lhsT=KW[:, c * 128:(c + 1) * 128],
                                                      rhs=q[:, T * NT + j0:T * NT + j1], start=True, stop=False), [KW, q], [b])
                        P.op("pe", lambda e: e.matmul(b[:, j0:j1], lhsT=G.ident[:, :], rhs=masks[:, mi, j0:j1], start=False,
                                                      stop=True), [G.ident, masks], [b])
                    return emit_win

                def mk_fin(o, br, h, r, store):
                    def fin():
                        finalize(o, br, h, T, r, False)
                        if store:
                            y = yo[cnt["y"] % 2]
                            a = acc[PAR[T]][r]
                            P.op("pool", lambda e: e.tensor_copy(out=y[0:64, :], in_=a[0:64, :]), [a], [y])
                            P.dma("pool", G.d["ynsaT"][h * 64:(h + 1) * 64, ts], y[0:64, :], [y], [G.dr["ynsaT"]],
                                  chan=("nsay", cnt["y"] % 2))
                            cnt["y"] += 1
                    return fin

                specs.append((2, r, h, VW, mk_win(Q[r]), mk_fin, list(range(max(0, 4 * T - 4), 4 * T + 4))))
                specs.append((1, r, h, VS, mk_slc(qs[r]), mk_fin, list(range(4 * T + 4))))
            for br, r, h, vt, es, mkf, cl in sorted(specs, key=lambda s: (-s[0], s[1])):
                o = obk[cnt["o"] % 2]
                cnt["o"] += 1
                fn = mkf(o, br, h, r, br == 1)
                for n, c in enumerate(cl):
                    if c >= 4 * T:
                        cols = (128 * (c - 4 * T), NT)
                    elif br == 2:
                        cols = (0, 128 * (c - (4 * T - 4) + 1))
                    else:
                        cols = (0, NT)
                    items.append(dict(emit_s=es, c=c, bc=h * 32 + (c - 4 * T + 28), o=o, v=vt, first=(n == 0),
                                      last=(n == len(cl) - 1), fin=fn, post=None, dep=None, slc_tile=(T if br == 1 else None),
                                      cols=cols))
            return items

        DEFER = 10
        load_gates(TORDER[0])
        seq = cmp_items(TORDER[0])
        pend = (len(seq) - 1, TORDER[0])
        for ti_, T in enumerate(TORDER):
            TN = TORDER[ti_ + 1] if ti_ + 1 < NTT else None
            sw = sw_items(T, None)
            if TN is not None:
                cm = cmp_items(TN)
                groups = []
                for it in cm:
                    if it["first"]:
                        groups.append([it])
                    else:
                        groups[-1].append(it)
                step = max(1, (len(sw) * 3) // (len(groups) * 5 + 1))
                pos = step
                merged = []
                gi = 0
                for n, it in enumerate(sw):
                    merged.append(it)
                    if gi < len(groups) and n + 1 == pos:
                        merged.extend(groups[gi])
                        gi += 1
                        pos += step
                while gi < len(groups):
                    merged.extend(groups[gi])
                    gi += 1
            else:
                cm = None
                merged = sw
            start = len(seq)
            seq += merged
            if TN is not None:
                seq[start].setdefault("pre", []).append((lambda TT: (lambda: load_gates(TT)))(TN))
            li, tt = pend
            hi_ = min(li + DEFER, len(seq) - 1)
            seq[hi_].setdefault("pre", []).append((lambda TT: (lambda: topk2(TT)))(tt))
            for it in seq[start:]:
                if it.get("slc_tile") == tt:
                    it["dep"] = seq[hi_]
            if cm is not None:
                pend = (max(i for i, it in enumerate(seq) if it is cm[-1]), TN)
        run_stream(seq)
    C.close()


def build_program(nphases=99, debug=False):
    nc = bass.Bass("TRN2", target_bir_lowering=False)
    P = Prog(nc)
    G = Common()
    G.d = {}
    G.dr = {}

    def dram(name, shape, dt, kind="Internal"):
        if debug and kind == "Internal":
            kind = "ExternalOutput"
        t = nc.dram_tensor(name, list(shape), dt, kind=kind)
        G.d[name] = t.ap()
        G.dr[name] = Buf(t, multi=True)

    dram("xT", [D, S], F32, "ExternalInput")
    dram("cstv", [128, 128], F32, "ExternalInput")
    dram("ropecs", [128, S], F32, "ExternalInput")
    dram("a_w_in", [D, 2624], F32, "ExternalInput")
    dram("a_w_q", [768, 1024], F32, "ExternalInput")
    dram("a_w_kv", [256, 1024], F32, "ExternalInput")
    dram("ycatT", [D, S], BF16)
    dram("qT", [8, 96, S], BF16)
    dram("kT", [8, 96, S], BF16)
    dram("vtok", [S, 512], BF16)
    dram("cmask", [4, 128, NT], BF16, "ExternalInput")
    dram("identv", [128, 128], BF16, "ExternalInput")
    dram("a_w_out", [D, D], F32, "ExternalInput")
    dram("w1_0", [D, 4096], F32, "ExternalInput")
    dram("w2_0", [4096, D], F32, "ExternalInput")
    dram("x1T", [D, S], F32)
    dram("h2T", [D, S], BF16)
    dram("x2T", [D, S], F32)
    dram("h3T", [D, S], BF16)
    dram("c_w_in", [D, 2608], F32, "ExternalInput")
    dram("qTn", [16, 64, S], BF16)
    dram("kvT", [4, 4, 64, S], BF16)
    dram("gT", [48, S], BF16)
    dram("vsw", [S, 512], BF16)
    for sfx in ("k", "v"):
        dram("cw1" + sfx, [64, 2048], F32, "ExternalInput")
        dram("cw2" + sfx, [64, 64], F32, "ExternalInput")
        dram("cpe" + sfx, [64, 32], F32, "ExternalInput")
    dram("qaug", [16, 1, S], BF16, "ExternalInput")
    dram("biastab", [128, 704], F32, "ExternalInput")
    dram("qterm", [16, 64, NT], F32, "ExternalInput")
    dram("wmask", [4, 128, NT], BF16, "ExternalInput")
    dram("cmpmask", [5, 128, NT], BF16, "ExternalInput")
    dram("expand", [64, S], BF16, "ExternalInput")
    dram("selm", [48, 48 * 64], BF16, "ExternalInput")
    dram("ov2", [128, 2, 65], BF16, "ExternalInput")
    dram("addtab", [128, 32, 64], F32, "ExternalInput")
    dram("ynsaT", [D, S], BF16)
    dram("c_w_out", [D, D], F32, "ExternalInput")
    dram("w1_1", [D, 4096], F32, "ExternalInput")
    dram("w2_1", [4096, D], F32, "ExternalInput")
    dram("x3T", [D, S], F32)
    dram("h4T", [D, S], BF16)
    dram("outT", [D, S], F32, "ExternalOutput")
    if debug:
        dram("dbgacc", [3, D, S], F32)
        dram("dbgimp", [4, 128, 32, 64], F32)
    G.co = {"gpre0": 0, "gpost0": 8, "gmpre0": 16, "gmpost0": 24, "gpre1": 32, "gpost1": 40, "gmpre1": 48,
            "gmpost1": 56, "qn": 64, "kvn": 70, "convw": 72}
    with ExitStack() as gs:
        def gsb(name, shape, dt):
            return Buf(gs.enter_context(nc.sbuf_tensor(name, list(shape), dt)))
        G.cst = gsb("cst", [128, 128], F32)
        G.ones = gsb("ones", [128, 128], BF16)
        G.eps = gsb("eps", [128, 1], F32)
        G.tiny = gsb("tiny", [128, 1], F32)
        G.banks = [Buf(gs.enter_context(nc.psum_tensor("bank%d" % i, [128, 512], F32)), psum=True) for i in range(8)]

        def bank():
            P.bank_i = (P.bank_i + 1) % 8
            return G.banks[P.bank_i]
        G.bank = bank
        P.dma("sp", G.cst[:, :], G.d["cstv"][:, :], [], [G.cst], chan="cst")
        P.op("pool", lambda e: e.memset(G.ones[:, :], 1.0), [], [G.ones])
        P.op("pool", lambda e: e.memset(G.eps[:, :], EPS), [], [G.eps])
        P.op("pool", lambda e: e.memset(G.tiny[0:64, :], 0.0), [], [G.tiny])
        P.op("pool", lambda e: e.memset(G.tiny[64:128, :], 1e-30), [], [G.tiny])
        G.ident = gsb("ident", [128, 128], BF16)
        P.dma("sp", G.ident[:, :], G.d["identv"][:, :], [], [G.ident], chan="cst")
        if nphases >= 1:
            phase_front_a(P, G)
        if nphases >= 2:
            phase_mla_attn(P, G)
        if nphases >= 3:
            Cw, W1 = prefetch_w1(P, G, "w1_0") if nphases >= 4 else (None, None)
            phase_outproj(P, G, "a_w_out", "ycatT", "xT", "x1T", "h2T", "gpost0", "gmpre0",
                          after_first=(Cw.deferred_load if Cw is not None else None))
        if nphases >= 4:
            phase_mlp(P, G, "w1_0", "w2_0", "h2T", "x1T", "x2T", "h3T", "gmpost0", "gpre1", W1=W1)
            Cw.close()
        G.KC = [gsb("KC%d" % g, [128, 256], BF16) for g in range(4)]
        G.VC = [gsb("VC%d" % g, [128, 2, 128], BF16) for g in range(4)]
        if nphases >= 5:
            phase_front_c(P, G)
        if nphases >= 6:
            phase_compress(P, G)
        if nphases >= 7:
            phase_nsa(P, G, debug)
        if nphases >= 8:
            Cw, W1 = prefetch_w1(P, G, "w1_1") if nphases >= 9 else (None, None)
            phase_outproj(P, G, "c_w_out", "ynsaT", "x2T", "x3T", "h4T", "gpost1", "gmpre1",
                          after_first=(Cw.deferred_load if Cw is not None else None))
        if nphases >= 9:
            phase_mlp(P, G, "w1_1", "w2_1", "h4T", "x3T", "outT", None, "gmpost1", None, W1=W1)
            Cw.close()
        P.barrier()
        P.final_wait()
    print("instructions:", P.ninst, "sems:", P.nsem)
    return nc


def pack_cols(v):
    return np.ascontiguousarray(np.asarray(v, np.float32).reshape(-1, 128).T)


def host_consts(inp):
    c = {}
    cst = np.zeros((128, 128), np.float32)
    cols = [inp["norm_mix_pre"][0], inp["norm_mix_post"][0], inp["norm_mlp_pre"][0], inp["norm_mlp_post"][0],
            inp["norm_mix_pre"][1], inp["norm_mix_post"][1], inp["norm_mlp_pre"][1], inp["norm_mlp_post"][1]]
    for i, v in enumerate(cols):
        cst[:, i * 8:(i + 1) * 8] = pack_cols(v)
    cst[:, 64:70] = pack_cols(inp["a_q_norm"][0])
    cst[:, 70:72] = pack_cols(inp["a_kv_norm"][0])
    cw = np.asarray(inp["a_conv_w"][0], np.float32)
    for cc in range(4):
        for k in range(3):
            cst[:, 72 + cc * 3 + k] = cw[k, cc * 128:(cc + 1) * 128]
    c["cstv"] = cst
    half = 16
    inv = (10000.0 ** (-np.arange(half, dtype=np.float32) / half)).astype(np.float32)
    ang = np.arange(S, dtype=np.float32)[None, :] * inv[:, None]
    cs, sn = np.cos(ang).astype(np.float32), np.sin(ang).astype(np.float32)
    Cm = np.concatenate([cs, cs], 0)
    Sm = np.concatenate([-sn, sn], 0)
    c["ropecs"] = np.ascontiguousarray(np.concatenate([Cm, Sm, Cm, Sm], 0))
    w = np.asarray(inp["a_w_in"][0], np.float32)
    c["a_w_in"] = np.ascontiguousarray(np.concatenate([w, w[:, 2576:2592], w[:, 2560:2576]], 1))
    wq = np.asarray(inp["a_w_q_up"][0], np.float32)
    cols = []
    for h in range(8):
        b = h * 96
        cols += [wq[:, b:b + 96], wq[:, b + 80:b + 96], wq[:, b + 64:b + 80]]
    c["a_w_q"] = np.ascontiguousarray(np.concatenate(cols, 1))
    wkv = np.asarray(inp["a_w_kv_up"][0], np.float32).reshape(256, 8, 128)
    c["a_w_kv"] = np.ascontiguousarray(np.concatenate([wkv[:, :, :64].reshape(256, 512), wkv[:, :, 64:].reshape(256, 512)], 1))
    NEGM = -30000.0
    cm = np.zeros((4, 128, NT), np.float32)
    ii = np.arange(128)[:, None]
    jj = np.arange(NT)[None, :]
    for off in range(4):
        cm[off] = np.where(ii + 128 * off <= jj, 0.0, NEGM)
    c["cmask"] = cm.astype(ml_dtypes.bfloat16)
    c["identv"] = np.eye(128, dtype=np.float32).astype(ml_dtypes.bfloat16)
    c["a_w_out"] = np.ascontiguousarray(inp["a_w_out"][0], np.float32)
    c["w1_0"] = np.ascontiguousarray(inp["mlp_w1"][0], np.float32)
    c["w2_0"] = np.ascontiguousarray(inp["mlp_w2"][0], np.float32)
    bf = ml_dtypes.bfloat16
    cw = np.asarray(inp["c_w_in"][0], np.float32)
    c["c_w_in"] = np.ascontiguousarray(np.concatenate(
        [cw[:, 0:1536], cw[:, 1536:1792], cw[:, 2048:2304], cw[:, 1792:2048], cw[:, 2304:2560], cw[:, 2560:2608]], 1))
    for sfx in ("k", "v"):
        w1 = np.asarray(inp["c_cmp_w1_" + sfx][0], np.float32)
        c["cw1" + sfx] = np.ascontiguousarray(w1.transpose(1, 0, 2).reshape(64, 2048))
        c["cw2" + sfx] = np.ascontiguousarray(inp["c_cmp_w2_" + sfx][0], np.float32)
        c["cpe" + sfx] = np.ascontiguousarray(np.asarray(inp["c_cmp_pe_" + sfx][0], np.float32).T)

    slopes = [float(np.float32(2.0 ** (-8.0 * (h + 1) / 16))) for h in range(16)]
    qaug = np.zeros((16, 1, S), np.float32)
    qterm = np.zeros((16, 64, NT), np.float32)
    btab = np.zeros((128, 704), np.float64)
    pp = np.arange(128, dtype=np.float64)
    for h in range(16):
        sp = 8.0 * slopes[h]
        qaug[h, 0] = -sp * (np.arange(S) % NT)
        qterm[h] = (-sp * np.arange(NT))[None, :]
        for dlt in range(-28, 4):
            btab[:, h * 32 + dlt + 28] = slopes[h] * (pp + 128.0 * dlt)
        for T in range(8):
            btab[:, 512 + h * 12 + T] = slopes[h] * (16.0 * pp + 15.5 - 512.0 * T)
        for T in range(4, 8):
            btab[:, 512 + h * 12 + 8 + (T - 4)] = slopes[h] * (16.0 * pp + 2048.0 + 15.5 - 512.0 * T)
    c["qaug"] = qaug.astype(bf)
    c["qterm"] = qterm
    c["biastab"] = btab.astype(np.float32)
    ii = np.arange(128)[:, None]
    jj2 = np.arange(NT)[None, :]
    wm = np.zeros((4, 128, NT), np.float32)
    cpm = np.zeros((5, 128, NT), np.float32)
    for o in range(4):
        wm[o] = np.where(ii > jj2 - 128 * o, 0.0, NEGM)
    for o in range(5):
        cpm[o] = np.where(16 * ii + 31 <= 512 * o + jj2, 0.0, NEGM)
    c["wmask"] = wm.astype(bf)
    c["cmpmask"] = cpm.astype(bf)
    c["expand"] = (np.arange(64)[:, None] == (np.arange(S)[None, :] // 64)).astype(np.float32).astype(bf)
    selm = np.zeros((48, 48, 64), np.float32)
    for k in range(48):
        selm[k, k, :] = 1.0
    c["selm"] = selm.reshape(48, 48 * 64).astype(bf)
    starts = np.arange(255) * 16
    ss = np.arange(64) * 64
    ov = np.clip(np.minimum(starts[:, None] + 32, ss[None, :] + 64) - np.maximum(starts[:, None], ss[None, :]), 0, None) / 32.0
    ov2 = np.zeros((256, 65), np.float32)
    ov2[:255, :64] = ov
    ov2[:255, 64] = 1.0
    c["ov2"] = np.ascontiguousarray(ov2.reshape(2, 128, 65).transpose(1, 0, 2)).astype(bf)
    t = np.arange(S)
    cur = t // 64
    jb = np.arange(64)[None, :]
    forced = (jb == 0) | (jb == cur[:, None]) | (jb == cur[:, None] - 1)
    add = np.where(jb > cur[:, None], -1.0e9, 1.0e4 * forced).astype(np.float32)
    c["addtab"] = np.ascontiguousarray(add.reshape(32, 128, 64).transpose(1, 0, 2))
    c["c_w_out"] = np.ascontiguousarray(inp["c_w_out"][0], np.float32)
    c["w1_1"] = np.ascontiguousarray(inp["mlp_w1"][1], np.float32)
    c["w2_1"] = np.ascontiguousarray(inp["mlp_w2"][1], np.float32)
    return c


def kernel(**inp):
    x = np.asarray(inp["x"], np.float32)
    c = host_consts(inp)
    nc = build_program()
    in_maps = []
    for b in range(8):
        m = dict(c)
        m["xT"] = np.ascontiguousarray(x[b].T)
        in_maps.append(m)
    res = run_bass_kernel_spmd(nc, in_maps, core_ids=list(range(8)))
    out = np.stack([np.ascontiguousarray(r["outT"].T) for r in res.results], 0)
    return out.astype(np.float32)
```
